# Optimizing a Trainium2 kernel written in Bass

```python
import jax, jax.numpy as jnp
from jax import lax
import numpy as np

D_MODEL = 1024
BATCH = 8
SEQ = 2048
DEPTH = 4

N_MIXERS = 3
ROPE_THETA = 500000.0
NORM_EPS = 1e-6
BLOCK = 128
NEG_INF = -1e30
SWA_HEADS = 16
SWA_KV_HEADS = 4
SWA_HEAD_DIM = D_MODEL // SWA_HEADS
SWA_GROUP = SWA_HEADS // SWA_KV_HEADS
SWA_WINDOW = 128
SWA_ROT = SWA_HEAD_DIM // 4
RWKV_HEAD_DIM = 64
RWKV_HEADS = D_MODEL // RWKV_HEAD_DIM
RWKV_DECAY_LORA = 64
RWKV_A_LORA = 64
RWKV_GATE_LORA = 160
RWKV_GN_EPS = 64e-5
MLA_HEADS = 16
MLA_NOPE = 64
MLA_ROPE = 32
MLA_V = 64
MLA_Q_LORA = 384
MLA_KV_LORA = 256
FFN_DIM = 2816
CONV_WIDTH = 3

kernel_name = 'hybrid_swa_rwkv7_mla_convffn'

F32 = jnp.float32


def rms_norm(x, g):
    xf = x.astype(F32)
    y = xf * lax.rsqrt(jnp.mean(xf * xf, axis=-1, keepdims=True) + NORM_EPS)
    return (y * g.astype(F32)).astype(x.dtype)


def rope_tables(seq, rot_dim):
    inv = ROPE_THETA ** (-jnp.arange(0, rot_dim, 2, dtype=F32) / rot_dim)
    ang = jnp.arange(seq, dtype=F32)[:, None] * inv[None, :]
    return jnp.cos(ang), jnp.sin(ang)


def rope_slice(x, cos, sin, start):
    rot = 2 * cos.shape[-1]
    half = rot // 2
    x1 = x[..., start:start + half]
    x2 = x[..., start + half:start + rot]
    c = cos[None, :, None, :].astype(x.dtype)
    s = sin[None, :, None, :].astype(x.dtype)
    return jnp.concatenate([x[..., :start], x1 * c - x2 * s, x2 * c + x1 * s, x[..., start + rot:]], axis=-1)


def swa_mixer(h, w_qkv, q_gain, k_gain, sinks, w_o, cos, sin):
    b, s, _ = h.shape
    nb = s // BLOCK
    qd = SWA_HEADS * SWA_HEAD_DIM
    kd = SWA_KV_HEADS * SWA_HEAD_DIM
    qkv = h @ w_qkv
    q = qkv[..., :qd].reshape(b, s, SWA_HEADS, SWA_HEAD_DIM)
    k = qkv[..., qd:qd + kd].reshape(b, s, SWA_KV_HEADS, SWA_HEAD_DIM)
    v = qkv[..., qd + kd:].reshape(b, s, SWA_KV_HEADS, SWA_HEAD_DIM)
    q = rope_slice(rms_norm(q, q_gain), cos, sin, 0)
    k = rope_slice(rms_norm(k, k_gain), cos, sin, 0)
    q = q.reshape(b, nb, BLOCK, SWA_KV_HEADS, SWA_GROUP, SWA_HEAD_DIM)

    def band(t):
        tp = jnp.pad(t, ((0, 0), (BLOCK, BLOCK), (0, 0), (0, 0)))
        tp = tp.reshape(b, nb + 2, BLOCK, SWA_KV_HEADS, SWA_HEAD_DIM)
        return jnp.concatenate([tp[:, :-2], tp[:, 1:-1], tp[:, 2:]], axis=2)

    kb, vb = band(k), band(v)
    blk = jnp.arange(nb)[:, None]
    qpos = blk * BLOCK + jnp.arange(BLOCK)[None, :]
    kpos = (blk - 1) * BLOCK + jnp.arange(3 * BLOCK)[None, :]
    valid = ((kpos >= 0) & (kpos < s))[:, None, :]
    mask = (jnp.abs(qpos[:, :, None] - kpos[:, None, :]) <= SWA_WINDOW) & valid
    sink = sinks.astype(F32).reshape(SWA_KV_HEADS, SWA_GROUP)[None, :, :, None, None]
    scale = SWA_HEAD_DIM ** -0.5

    def attend(args):
        qb, kbb, vbb, mb = args
        sc = jnp.einsum('bqhgd,bkhd->bhgqk', qb, kbb).astype(F32) * scale
        sc = jnp.where(mb, sc, NEG_INF)
        m = jnp.maximum(jnp.max(sc, axis=-1, keepdims=True), sink)
        p = jnp.exp(sc - m)
        p = p / (jnp.sum(p, axis=-1, keepdims=True) + jnp.exp(sink - m))
        return jnp.einsum('bhgqk,bkhd->bqhgd', p.astype(vbb.dtype), vbb)

    o = lax.map(attend, (jnp.moveaxis(q, 1, 0), jnp.moveaxis(kb, 1, 0), jnp.moveaxis(vb, 1, 0), mask))
    o = jnp.moveaxis(o, 0, 1).reshape(b, s, qd)
    return o @ w_o


def wkv7_scan(r, w, k, v, a, bb, reverse):
    b, s, nh, n = r.shape
    seq = tuple(jnp.moveaxis(t, 1, 0) for t in (r, w, k, v, a, bb))

    def step(state, inp):
        r_t, w_t, k_t, v_t, a_t, b_t = inp
        sa = jnp.einsum('bhvk,bhk->bhv', state, a_t)
        state = state * w_t[:, :, None, :] + sa[..., None] * b_t[:, :, None, :] + v_t[..., None] * k_t[:, :, None, :]
        y = jnp.einsum('bhvk,bhk->bhv', state, r_t)
        return state, y

    s0 = jnp.zeros((b, nh, n, n), F32)
    _, y = lax.scan(step, s0, seq, reverse=reverse)
    return jnp.moveaxis(y, 0, 1)


def rwkv7_mixer(h, mu, w_r, w_k, w_v, w0, w1, w2, a0, a1, a2, g1, g2, k_k, k_a, r_k, lnx_w, lnx_b, w_o):
    b, s, d = h.shape
    hp = jnp.pad(h, ((0, 0), (1, 1), (0, 0)))
    xx = 0.5 * (hp[:, :-2] + hp[:, 2:]) - h
    xr = h + xx * mu[0]
    xw = h + xx * mu[1]
    xk = h + xx * mu[2]
    xv = h + xx * mu[3]
    xa = h + xx * mu[4]
    xg = h + xx * mu[5]
    r = xr @ w_r
    k = xk @ w_k
    v = xv @ w_v
    g = jax.nn.sigmoid(xg @ g1) @ g2

    def heads(t):
        return t.reshape(b, s, RWKV_HEADS, RWKV_HEAD_DIM)

    kk = heads(k * k_k).astype(F32)
    kk = kk / jnp.maximum(jnp.sqrt(jnp.sum(kk * kk, axis=-1, keepdims=True)), 1e-12)
    rf = heads(r).astype(F32)
    vf = heads(v).astype(F32)
    kf = k.astype(F32)
    rkf = r_k.astype(F32)

    def direction(dirn):
        wl = (w0[dirn] + jnp.tanh(xw @ w1[dirn]) @ w2[dirn]).astype(F32)
        decay = jnp.exp(-jnp.exp(-jax.nn.softplus(-wl) - 0.5))
        a = jax.nn.sigmoid((a0[dirn] + (xa @ a1[dirn]) @ a2[dirn]).astype(F32))
        kd = kf * (1.0 + (a - 1.0) * k_a.astype(F32))
        a, kd, decay = heads(a), heads(kd), heads(decay)
        y = wkv7_scan(rf, decay, kd, vf, -kk, kk * a, reverse=(dirn == 1))
        bonus = jnp.sum(rf * kd * rkf, axis=-1, keepdims=True) * vf
        return y, bonus

    y_f, bonus_f = direction(0)
    y_b, bonus_b = direction(1)
    y = y_f + y_b
    mean = jnp.mean(y, axis=-1, keepdims=True)
    var = jnp.mean(jnp.square(y - mean), axis=-1, keepdims=True)
    yn = ((y - mean) * lax.rsqrt(var + RWKV_GN_EPS)).reshape(b, s, d) * lnx_w.astype(F32) + lnx_b.astype(F32)
    out = (yn + (bonus_f + bonus_b).reshape(b, s, d)) * g.astype(F32)
    return out.astype(h.dtype) @ w_o


def mla_mixer(h, w_down, cq_gain, ckv_gain, w_uq, w_ukv, q_gain, k_gain, w_o, cos, sin):
    b, s, _ = h.shape
    nb = s // BLOCK
    down = h @ w_down
    cq = rms_norm(down[..., :MLA_Q_LORA], cq_gain)
    ckv = rms_norm(down[..., MLA_Q_LORA:MLA_Q_LORA + MLA_KV_LORA], ckv_gain)
    k_rope = down[..., MLA_Q_LORA + MLA_KV_LORA:]
    q = (cq @ w_uq).reshape(b, s, MLA_HEADS, MLA_NOPE + MLA_ROPE)
    kv = (ckv @ w_ukv).reshape(b, s, MLA_HEADS, MLA_NOPE + MLA_V)
    k = jnp.concatenate([kv[..., :MLA_NOPE], jnp.broadcast_to(k_rope[:, :, None, :], (b, s, MLA_HEADS, MLA_ROPE))], axis=-1)
    v = kv[..., MLA_NOPE:]
    q = rope_slice(rms_norm(q, q_gain), cos, sin, MLA_NOPE)
    k = rope_slice(rms_norm(k, k_gain), cos, sin, MLA_NOPE)
    scale = (MLA_NOPE + MLA_ROPE) ** -0.5
    qb = jnp.moveaxis(q.reshape(b, nb, BLOCK, MLA_HEADS, MLA_NOPE + MLA_ROPE), 1, 0)

    def attend(qblk):
        sc = jnp.einsum('bqhd,bkhd->bhqk', qblk, k).astype(F32) * scale
        p = jax.nn.softmax(sc, axis=-1)
        return jnp.einsum('bhqk,bkhd->bqhd', p.astype(v.dtype), v)

    o = lax.map(attend, qb)
    o = jnp.moveaxis(o, 0, 1).reshape(b, s, MLA_HEADS * MLA_V)
    return o @ w_o


def conv_ffn(h, w_up, conv_w, conv_b, w_down):
    s = h.shape[1]
    pad = CONV_WIDTH // 2
    u = h @ w_up
    up = jnp.pad(u, ((0, 0), (pad, pad), (0, 0)))
    acc = up[:, :s] * conv_w[0] + conv_b
    for t in range(1, CONV_WIDTH):
        acc = acc + up[:, t:t + s] * conv_w[t]
    gate, val = jnp.split(acc, 2, axis=-1)
    return (jax.nn.silu(gate) * val) @ w_down


def setup_inputs(seed: int = 0) -> dict:
    key = jax.random.key(seed)
    ks = iter(jax.random.split(key, 48))
    n_a = len(range(0, DEPTH, N_MIXERS))
    n_b = len(range(1, DEPTH, N_MIXERS))
    n_c = len(range(2, DEPTH, N_MIXERS))
    d = D_MODEL

    def nrm(shape, scale):
        return scale * jax.random.normal(next(ks), shape, F32)

    def unif(shape, lo, hi):
        return jax.random.uniform(next(ks), shape, F32, lo, hi)

    qkv_w = (SWA_HEADS + 2 * SWA_KV_HEADS) * SWA_HEAD_DIM
    return {
        'x': nrm((BATCH, SEQ, d), 1.0),
        'norm_tok': 1.0 + nrm((DEPTH, d), 0.05),
        'norm_ch': 1.0 + nrm((DEPTH, d), 0.05),
        'ffn_w_up': nrm((DEPTH, d, 2 * FFN_DIM), d ** -0.5),
        'ffn_conv_w': nrm((DEPTH, CONV_WIDTH, 2 * FFN_DIM), CONV_WIDTH ** -0.5),
        'ffn_conv_b': nrm((DEPTH, 2 * FFN_DIM), 0.02),
        'ffn_w_down': nrm((DEPTH, FFN_DIM, d), FFN_DIM ** -0.5),
        'swa_w_qkv': nrm((n_a, d, qkv_w), d ** -0.5),
        'swa_q_gain': 1.0 + nrm((n_a, SWA_HEAD_DIM), 0.05),
        'swa_k_gain': 1.0 + nrm((n_a, SWA_HEAD_DIM), 0.05),
        'swa_sinks': nrm((n_a, SWA_HEADS), 1.0),
        'swa_w_o': nrm((n_a, SWA_HEADS * SWA_HEAD_DIM, d), d ** -0.5),
        'rwkv_mu': unif((n_b, 6, d), 0.0, 1.0),
        'rwkv_w_r': nrm((n_b, d, d), d ** -0.5),
        'rwkv_w_k': nrm((n_b, d, d), d ** -0.5),
        'rwkv_w_v': nrm((n_b, d, d), d ** -0.5),
        'rwkv_w0': unif((n_b, 2, d), -6.0, -1.0),
        'rwkv_w1': nrm((n_b, 2, d, RWKV_DECAY_LORA), d ** -0.5),
        'rwkv_w2': nrm((n_b, 2, RWKV_DECAY_LORA, d), 0.5 * RWKV_DECAY_LORA ** -0.5),
        'rwkv_a0': nrm((n_b, 2, d), 0.5),
        'rwkv_a1': nrm((n_b, 2, d, RWKV_A_LORA), d ** -0.5),
        'rwkv_a2': nrm((n_b, 2, RWKV_A_LORA, d), 0.5 * RWKV_A_LORA ** -0.5),
        'rwkv_g1': nrm((n_b, d, RWKV_GATE_LORA), d ** -0.5),
        'rwkv_g2': nrm((n_b, RWKV_GATE_LORA, d), RWKV_GATE_LORA ** -0.5),
        'rwkv_k_k': 0.85 + nrm((n_b, d), 0.1),
        'rwkv_k_a': 1.0 + nrm((n_b, d), 0.1),
        'rwkv_r_k': nrm((n_b, RWKV_HEADS, RWKV_HEAD_DIM), 0.1),
        'rwkv_lnx_w': 1.0 + nrm((n_b, d), 0.05),
        'rwkv_lnx_b': nrm((n_b, d), 0.01),
        'rwkv_w_o': nrm((n_b, d, d), d ** -0.5),
        'mla_w_down': nrm((n_c, d, MLA_Q_LORA + MLA_KV_LORA + MLA_ROPE), d ** -0.5),
        'mla_cq_gain': 1.0 + nrm((n_c, MLA_Q_LORA), 0.05),
        'mla_ckv_gain': 1.0 + nrm((n_c, MLA_KV_LORA), 0.05),
        'mla_w_uq': nrm((n_c, MLA_Q_LORA, MLA_HEADS * (MLA_NOPE + MLA_ROPE)), MLA_Q_LORA ** -0.5),
        'mla_w_ukv': nrm((n_c, MLA_KV_LORA, MLA_HEADS * (MLA_NOPE + MLA_V)), MLA_KV_LORA ** -0.5),
        'mla_q_gain': 1.0 + nrm((n_c, MLA_NOPE + MLA_ROPE), 0.05),
        'mla_k_gain': 1.0 + nrm((n_c, MLA_NOPE + MLA_ROPE), 0.05),
        'mla_w_o': nrm((n_c, MLA_HEADS * MLA_V, d), d ** -0.5),
    }


def reference(x, norm_tok, norm_ch, ffn_w_up, ffn_conv_w, ffn_conv_b, ffn_w_down,
              swa_w_qkv, swa_q_gain, swa_k_gain, swa_sinks, swa_w_o,
              rwkv_mu, rwkv_w_r, rwkv_w_k, rwkv_w_v, rwkv_w0, rwkv_w1, rwkv_w2,
              rwkv_a0, rwkv_a1, rwkv_a2, rwkv_g1, rwkv_g2, rwkv_k_k, rwkv_k_a, rwkv_r_k,
              rwkv_lnx_w, rwkv_lnx_b, rwkv_w_o,
              mla_w_down, mla_cq_gain, mla_ckv_gain, mla_w_uq, mla_w_ukv,
              mla_q_gain, mla_k_gain, mla_w_o):
    s = x.shape[1]
    cos_a, sin_a = rope_tables(s, SWA_ROT)
    cos_c, sin_c = rope_tables(s, MLA_ROPE)
    for i in range(DEPTH):
        kind = i % N_MIXERS
        j = i // N_MIXERS
        h = rms_norm(x, norm_tok[i])
        if kind == 0:
            y = swa_mixer(h, swa_w_qkv[j], swa_q_gain[j], swa_k_gain[j], swa_sinks[j], swa_w_o[j], cos_a, sin_a)
        elif kind == 1:
            y = rwkv7_mixer(h, rwkv_mu[j], rwkv_w_r[j], rwkv_w_k[j], rwkv_w_v[j], rwkv_w0[j], rwkv_w1[j], rwkv_w2[j],
                            rwkv_a0[j], rwkv_a1[j], rwkv_a2[j], rwkv_g1[j], rwkv_g2[j], rwkv_k_k[j], rwkv_k_a[j],
                            rwkv_r_k[j], rwkv_lnx_w[j], rwkv_lnx_b[j], rwkv_w_o[j])
        else:
            y = mla_mixer(h, mla_w_down[j], mla_cq_gain[j], mla_ckv_gain[j], mla_w_uq[j], mla_w_ukv[j],
                          mla_q_gain[j], mla_k_gain[j], mla_w_o[j], cos_c, sin_c)
        x = x + y
        x = x + conv_ffn(rms_norm(x, norm_ch[i]), ffn_w_up[i], ffn_conv_w[i], ffn_conv_b[i], ffn_w_down[i])
    return x
```

```python
import numpy as np
import ml_dtypes
from contextlib import ExitStack
import concourse.bass as bass
import concourse.mybir as mybir
from concourse.bass_utils import run_bass_kernel_spmd

F32 = mybir.dt.float32
BF16 = mybir.dt.bfloat16
ALU = mybir.AluOpType
AF = mybir.ActivationFunctionType
AX = mybir.AxisListType

S = 2048
D = 1024
NT = 4
FF = 2816
EPS = 1e-6


class Res:
    __slots__ = ("last_w", "rc", "rd", "excl")

    def __init__(self):
        self.excl = False
        self.last_w = None
        self.rc = {}
        self.rd = []


class T:
    def __init__(self, h, nres=1):
        self.h = h
        self.rs = [Res() for _ in range(nres)]

    def __getitem__(self, idx):
        return self.h[idx]

    @property
    def all(self):
        return list(self.rs)


class Op:
    __slots__ = ("eng", "fn", "deps", "needed", "sig", "is_dma", "idx")


class Prog:
    ENGS = ("pe", "dve", "act", "pool", "sp")

    def __init__(self, nc, es, ndma_sems=8):
        self.nc = nc
        self.es = es
        self.ops = []
        self.ndma = ndma_sems
        self.n_alloc = 0

    def sb(self, shape, dt, nres=1, es=None):
        self.n_alloc += 1
        h = (es or self.es).enter_context(self.nc.sbuf_tensor(f"sb{self.n_alloc}", list(shape), dt))
        return T(h, nres)

    def ps(self, shape, dt=F32, nres=1, es=None):
        self.n_alloc += 1
        h = (es or self.es).enter_context(self.nc.psum_tensor(f"ps{self.n_alloc}", list(shape), dt))
        t = T(h, nres)
        for x in t.rs:
            x.excl = True
        return t

    def add(self, eng, fn, r=(), w=(), is_dma=False):
        op = Op()
        op.eng = eng
        op.fn = fn
        op.is_dma = is_dma
        op.needed = False
        op.sig = None
        op.idx = len(self.ops)
        deps = set()
        ex = [x for x in r if x.excl]
        if ex:
            r = [x for x in r if not x.excl]
            w = list(w) + [x for x in ex if x not in w]
        for x in r:
            if x.last_w is not None:
                deps.add(x.last_w)
        for x in w:
            if x.last_w is not None:
                deps.add(x.last_w)
            deps.update(x.rc.values())
            deps.update(x.rd)
        for x in r:
            if is_dma:
                x.rd.append(op.idx)
            else:
                x.rc[eng] = op.idx
        for x in w:
            x.last_w = op.idx
            x.rc = {}
            x.rd = []
        deps.discard(op.idx)
        op.deps = deps
        self.ops.append(op)
        return op

    def dma(self, out, in_, r=(), w=(), q="sp", **kw):
        return self.add(q, lambda e: e.dma_start(out=out, in_=in_, **kw), r, w, is_dma=True)

    def barrier(self):
        last = {}
        dmas = []
        for op in self.ops:
            if op.is_dma:
                dmas.append(op.idx)
            else:
                last[op.eng] = op.idx
        ids = set(last.values()) | set(dmas[-self.ndma:])
        for e in self.ENGS:
            op = self.add(e, lambda en: en.nop())
            op.deps = set(ids)

    def emit(self):
        nc = self.nc
        ops = self.ops
        for op in ops:
            for d in op.deps:
                ops[d].needed = True
        es = self.es
        EPOCH = 8000
        sems = {}
        dsems = [es.enter_context(nc.semaphore(f"s_dma{i}")) for i in range(self.ndma)]
        cnt = {e: 0 for e in self.ENGS}
        dcnt = [0] * self.ndma
        ndma = 0
        dma_prev = {}
        for op in ops:
            if op.is_dma:
                k = ndma % self.ndma
                ndma += 1
                dma_prev[op.idx] = (k, dcnt[k])
                dcnt[k] += 16
                op.sig = (("d", k), dcnt[k])
            elif op.needed:
                ep = cnt[op.eng] // EPOCH
                cnt[op.eng] += 1
                key = ("e", op.eng, ep)
                if key not in sems:
                    sems[key] = es.enter_context(nc.semaphore(f"s_{op.eng}_{ep}"))
                op.sig = (key, cnt[op.eng] - ep * EPOCH)
        self.stats = dict(cnt=dict(cnt), ndma=ndma, nops=len(ops), nsem=len(sems) + self.ndma)

        def semof(key):
            return dsems[key[1]] if key[0] == "d" else sems[key]

        byeng = {e: [op for op in ops if op.eng == e] for e in self.ENGS}

        def run(ename, eobj):
            known = {}
            for op in byeng[ename]:
                waits = {}
                for d in op.deps:
                    dop = ops[d]
                    if dop.eng == ename and not dop.is_dma and ename == "pe":
                        continue
                    key, v = dop.sig
                    if known.get(key, 0) >= v:
                        continue
                    if waits.get(key, 0) < v:
                        waits[key] = v
                if op.is_dma:
                    k, v = dma_prev[op.idx]
                    key = ("d", k)
                    if v > 0 and known.get(key, 0) < v and waits.get(key, 0) < v:
                        waits[key] = v
                for key, v in waits.items():
                    eobj.wait_ge(semof(key), v)
                    known[key] = v
                ins = op.fn(eobj)
                if op.sig is not None:
                    ins.then_inc(semof(op.sig[0]), 16 if op.is_dma else 1)
            if ename == "sp":
                for k in range(self.ndma):
                    if dcnt[k] > 0:
                        eobj.wait_ge(dsems[k], dcnt[k])

        with nc.Block() as block:
            @block.tensor
            def _(e):
                run("pe", e)

            @block.vector
            def _(e):
                run("dve", e)

            @block.scalar
            def _(e):
                run("act", e)

            @block.gpsimd
            def _(e):
                run("pool", e)

            @block.sync
            def _(e):
                run("sp", e)


class KB:
    def __init__(self, nc, es, P, dram):
        self.nc = nc
        self.es = es
        self.P = P
        self.d = dram
        P_ = P
        self.X = P_.sb([128, 8, S], F32, nres=8 * NT)
        self.ident = P_.sb([128, 128], F32)
        self.ones_bf = P_.sb([128, 128], BF16)
        self.ones_f = P_.sb([128, 128], F32)
        self.ps = [P_.ps([128, 512]) for _ in range(8)]
        self.psi = 0
        P_.dma(self.ident[:], dram["c_ident"], w=self.ident.all)
        P_.add("dve", lambda e: e.memset(self.ones_bf[:], 1.0), w=self.ones_bf.all)
        P_.add("dve", lambda e: e.memset(self.ones_f[:], 1.0), w=self.ones_f.all)
        self.cast_rr = 0

    def xr(self, c, tt):
        return self.X.rs[c * NT + tt]

    def xr_all(self):
        return self.X.all

    def nps(self):
        p = self.ps[self.psi % 6]
        self.psi += 1
        return p

    def mm(self, out, lhsT, rhs, start, stop, r, w):
        self.P.add("pe", lambda e: e.matmul(out, lhsT=lhsT, rhs=rhs, start=start, stop=stop), r=r, w=w)

    def cast(self, out, in_, r, w, eng=None):
        if eng is None:
            eng = ("pool", "act")[self.cast_rr % 2]
            self.cast_rr += 1
        if eng == "act":
            self.P.add("act", lambda e: e.copy(out=out, in_=in_), r=r, w=w)
        elif eng == "pool":
            self.P.add("pool", lambda e: e.tensor_copy(out=out, in_=in_), r=r, w=w)
        else:
            self.P.add("dve", lambda e: e.tensor_copy(out=out, in_=in_), r=r, w=w)

    def load_w(self, dst, dst_sl, src2d, kc, m, stg):
        P = self.P
        per = stg[0].h.shape[1]
        kstep = max(1, per // m)
        i = 0
        for k0 in range(0, kc, kstep):
            k1 = min(kc, k0 + kstep)
            st = stg[self.stg_rr % len(stg)]
            self.stg_rr += 1
            n = (k1 - k0) * m
            sv = st[:, 0:n].rearrange("p (k m) -> p k m", m=m)
            P.dma(sv, src2d[k0 * 128:k1 * 128, :].rearrange("(k p) m -> p k m", p=128), w=st.all)
            self.cast(dst_sl(k0, k1), sv, r=st.all, w=dst.all)
            i += 1

    stg_rr = 0

    def load_x(self, es):
        P = self.P
        xin = self.d["x"]
        tok = [P.sb([128, D], F32, es=es) for _ in range(2)]
        for ti in range(16):
            tk = tok[ti % 2]
            P.dma(tk[:], xin[ti * 128:(ti + 1) * 128, :], w=tk.all)
            for half in range(2):
                ps = self.nps()
                for j in range(4):
                    c = half * 4 + j
                    P.add("pe", lambda e, ps=ps, j=j, c=c, tk=tk: e.transpose(ps[:, j * 128:(j + 1) * 128], tk[:, c * 128:(c + 1) * 128], self.ident[:]),
                          r=tk.all + self.ident.all, w=ps.all)
                tt = ti // 4
                o = self.X[:, half * 4:(half + 1) * 4, ti * 128:(ti + 1) * 128]
                i = ps[:, :].rearrange("p (j t) -> p j t", t=128)
                eng = "dve" if half == 0 else "act"
                if eng == "dve":
                    P.add("dve", lambda e, o=o, i=i: e.tensor_copy(out=o, in_=i), r=ps.all, w=[self.xr(c, tt) for c in range(half * 4, half * 4 + 4)])
                else:
                    P.add("act", lambda e, o=o, i=i: e.copy(out=o, in_=i), r=ps.all, w=[self.xr(c, tt) for c in range(half * 4, half * 4 + 4)])

    def store_x(self, es):
        P = self.P
        yout = self.d["y"]
        tok = [P.sb([128, D], F32, es=es) for _ in range(2)]
        for ti in range(16):
            tk = tok[ti % 2]
            tt = ti // 4
            for half in range(2):
                ps = self.nps()
                for j in range(4):
                    c = half * 4 + j
                    P.add("pe", lambda e, ps=ps, j=j, c=c, ti=ti: e.transpose(ps[:, j * 128:(j + 1) * 128], self.X[:, c, ti * 128:(ti + 1) * 128], self.ident[:]),
                          r=[self.xr(c, tt)] + self.ident.all, w=ps.all)
                o = tk[:, half * 512:(half + 1) * 512]
                if half == 0:
                    P.add("dve", lambda e, o=o, ps=ps: e.tensor_copy(out=o, in_=ps[:, :]), r=ps.all, w=tk.all)
                else:
                    P.add("act", lambda e, o=o, ps=ps: e.copy(out=o, in_=ps[:, :]), r=ps.all, w=tk.all)
            P.dma(yout[ti * 128:(ti + 1) * 128, :], tk[:], r=tk.all)

    def rmsnorm(self, H, gain_ap, es, t0=0, t1=S, hoff=0, rstd_out=None):
        P = self.P
        if not hasattr(es, "_rn_tmp"):
            es._rn_tmp = (P.sb([128, 8, 512], BF16, es=es), P.sb([128, 512], F32, es=es), P.sb([128, 512], F32, es=es))
        sq, rs, rs2 = es._rn_tmp
        for a in range(t0, t1, 512):
            b = min(t1, a + 512)
            n = b - a
            tts = sorted(set([a // 512, (b - 1) // 512]))
            xr = [self.xr(c, tt) for c in range(8) for tt in tts]
            P.add("act", lambda e, a=a, b=b, n=n: e.activation(out=sq[:, :, 0:n], in_=self.X[:, :, a:b], func=AF.Square), r=xr, w=sq.all)
            ps = self.nps()
            for c in range(8):
                self.mm(ps[:, 0:n], self.ones_bf[:], sq[:, c, 0:n], c == 0, c == 7, r=sq.all + self.ones_bf.all, w=ps.all)
            P.add("act", lambda e, n=n, ps=ps: e.activation(out=rs[:, 0:n], in_=ps[:, 0:n], func=AF.Sqrt, bias=EPS, scale=1.0 / D), r=ps.all, w=rs.all)
            if rstd_out is not None:
                P.add("dve", lambda e, n=n, a=a, b=b: e.reciprocal(out=rstd_out[:, a:b], in_=rs[:, 0:n]), r=rs.all, w=rstd_out.all)
                continue
            P.add("dve", lambda e, n=n: e.reciprocal(out=rs2[:, 0:n], in_=rs[:, 0:n]), r=rs.all, w=rs2.all)
            for c in range(8):
                P.add("dve", lambda e, c=c, a=a, b=b, n=n: e.scalar_tensor_tensor(
                    out=H[:, c, hoff + a - t0:hoff + b - t0], in0=self.X[:, c, a:b], scalar=gain_ap[:, c:c + 1], in1=rs2[:, 0:n],
                    op0=ALU.mult, op1=ALU.mult), r=[self.xr(c, tt) for tt in tts] + rs2.all, w=H.all)

    def load_vec8(self, dst, src1d, q="sp"):
        self.P.dma(dst, src1d.rearrange("(c p) -> p c", p=128), w=[], q=q, allow_slow_non_contiguous=True)

    def ffn(self, li):
        P = self.P
        d = self.d
        with ExitStack() as es:
            gains = P.sb([128, 8], F32, es=es)
            P.dma(gains[:], d["norm_ch"][li].rearrange("(c p) -> p c", p=128), w=gains.all, allow_slow_non_contiguous=True)
            cw = P.sb([128, 3, 44], F32, es=es)
            cb = P.sb([128, 44], F32, es=es)
            for t in range(3):
                P.dma(cw[:, t, :], d["ffn_conv_w"][li, t].rearrange("(c p) -> p c", p=128), w=cw.all, allow_slow_non_contiguous=True)
            P.dma(cb[:], d["ffn_conv_b"][li].rearrange("(c p) -> p c", p=128), w=cb.all, allow_slow_non_contiguous=True)
            H = P.sb([128, 8, 1026], BF16, es=es)
            Hh = P.sb([128, 8, 2], BF16, es=es)
            G = P.sb([128, 22, 1024], BF16, es=es, nres=22)
            U = [P.sb([128, 1026], F32, es=es) for _ in range(2)]
            A = [P.sb([128, 1024], F32, es=es) for _ in range(2)]
            SG = P.sb([128, 1024], F32, es=es)
            wstg = [P.sb([128, 1024], F32, es=es) for _ in range(2)]
            wup = [[P.sb([128, 8, 128], BF16, es=es) for _ in range(2)] for _ in range(2)]
            wdn = [P.sb([128, 22, 128], BF16, es=es) for _ in range(2)]
            wdstg = P.sb([128, 22, 128], F32, es=es)
            w_up = d["ffn_w_up"][li]
            w_dn = d["ffn_w_down"][li]
            for half in range(2):
                hb = half * 1024
                lo = max(0, hb - 1)
                hi = min(S, hb + 1025)
                c0 = lo - (hb - 1)
                ncol = hi - lo
                if half == 0:
                    self.rmsnorm(H, gains, es, lo, hi, hoff=c0)
                    P.add("pool", lambda e: e.tensor_copy(out=Hh[:, :, 0:1], in_=H[:, :, 1024:1025]), r=H.all, w=Hh.all)
                else:
                    self.rmsnorm(H, gains, es, 1024, 2048, hoff=1)
                    P.add("pool", lambda e: e.tensor_copy(out=H[:, :, 0:1], in_=Hh[:, :, 0:1]), r=Hh.all, w=H.all)
                for j in range(22):
                    wb = wup[j % 2]
                    for part in range(2):
                        col0 = part * FF + j * 128
                        st = wstg[part]
                        sv = st[:, :].rearrange("p (k m) -> p k m", m=128)
                        P.dma(sv, w_up[:, col0:col0 + 128].rearrange("(k p) m -> p k m", p=128), w=st.all)
                        self.cast(wb[part][:], sv, r=st.all, w=wb[part].all, eng="pool")
                    for part in range(2):
                        fc = part * 22 + j
                        u = U[part]
                        if half == 0:
                            P.add("pool", lambda e, u=u: e.memset(u[:, 0:1], 0.0), w=u.all)
                        else:
                            P.add("pool", lambda e, u=u: e.memset(u[:, 1025:1026], 0.0), w=u.all)
                        segs = [(0, 512), (512, 1024), (1024, ncol)]
                        pss = [self.nps() for _ in segs]
                        for k in range(8):
                            for (a, b), ps in zip(segs, pss):
                                self.mm(ps[:, 0:b - a], wb[part][:, k, :], H[:, k, c0 + a:c0 + b], k == 0, k == 7,
                                        r=wb[part].all + H.all, w=ps.all)
                        for (a, b), ps in zip(segs, pss):
                            P.add("act", lambda e, u=u, a=a, b=b, ps=ps, c0=c0: e.copy(out=u[:, c0 + a:c0 + b], in_=ps[:, 0:b - a]), r=ps.all, w=u.all)
                        acc = A[part]
                        P.add("act", lambda e, u=u, acc=acc, fc=fc: e.activation(out=acc[:], in_=u[:, 1:1025], func=AF.Identity,
                                                                                 bias=cb[:, fc:fc + 1], scale=cw[:, 1, fc:fc + 1]),
                              r=u.all + cb.all + cw.all, w=acc.all)
                        P.add("dve", lambda e, u=u, acc=acc, fc=fc: e.scalar_tensor_tensor(out=acc[:], in0=u[:, 0:1024], scalar=cw[:, 0, fc:fc + 1], in1=acc[:],
                                                                                           op0=ALU.mult, op1=ALU.add), r=u.all + cw.all, w=acc.all)
                        P.add("dve", lambda e, u=u, acc=acc, fc=fc: e.scalar_tensor_tensor(out=acc[:], in0=u[:, 2:1026], scalar=cw[:, 2, fc:fc + 1], in1=acc[:],
                                                                                           op0=ALU.mult, op1=ALU.add), r=u.all + cw.all, w=acc.all)
                    P.add("act", lambda e: e.activation(out=SG[:], in_=A[0][:], func=AF.Silu), r=A[0].all, w=SG.all)
                    P.add("pool", lambda e, j=j: e.tensor_tensor(out=G[:, j, :], in0=SG[:], in1=A[1][:], op=ALU.mult), r=SG.all + A[1].all, w=[G.rs[j]])
                for dc in range(8):
                    wd = wdn[dc % 2]
                    P.dma(wdstg[:], w_dn[:, dc * 128:(dc + 1) * 128].rearrange("(k p) m -> p k m", p=128), w=wdstg.all)
                    self.cast(wd[:], wdstg[:], r=wdstg.all, w=wd.all, eng="pool")
                    for t2 in range(2):
                        ps = self.nps()
                        for j in range(22):
                            self.mm(ps[:, :], wd[:, j, :], G[:, j, t2 * 512:(t2 + 1) * 512], j == 0, j == 21, r=wd.all + [G.rs[j]], w=ps.all)
                        tt = half * 2 + t2
                        xs = self.X[:, dc, tt * 512:(tt + 1) * 512]
                        P.add("dve", lambda e, xs=xs, ps=ps: e.tensor_tensor(out=xs, in0=ps[:, :], in1=xs, op=ALU.add), r=ps.all + [self.xr(dc, tt)], w=[self.xr(dc, tt)])
        P.barrier()


    def head_qk(self, outT, terms, dh, gain_ap, PrT, Ct, St, wk, kd=None, ones_t=None):
        P = self.P
        raw, sq, rs, rs2, xn, t1, t2 = wk
        kd = kd or dh
        ones_t = ones_t or self.ones_f
        for tt in range(NT):
            sl = slice(tt * 512, (tt + 1) * 512)
            ps = self.nps()
            tl = terms(tt)
            for i, (lt, rh, rd) in enumerate(tl):
                self.mm(ps[0:dh, :], lt, rh, i == 0, i == len(tl) - 1, r=rd, w=ps.all)
            P.add("act", lambda e, ps=ps: e.copy(out=raw[0:dh, :], in_=ps[0:dh, :]), r=ps.all, w=raw.all)
            P.add("act", lambda e, ps=ps: e.activation(out=sq[0:dh, :], in_=ps[0:dh, :], func=AF.Square), r=ps.all, w=sq.all)
            ps2 = self.nps()
            self.mm(ps2[0:kd, :], ones_t[0:kd, 0:kd], sq[0:kd, :], True, True, r=sq.all + ones_t.all, w=ps2.all)
            P.add("act", lambda e, ps2=ps2: e.activation(out=rs[0:dh, :], in_=ps2[0:dh, :], func=AF.Sqrt, bias=EPS, scale=1.0 / dh), r=ps2.all, w=rs.all)
            P.add("dve", lambda e: e.reciprocal(out=rs2[0:dh, :], in_=rs[0:dh, :]), r=rs.all, w=rs2.all)
            P.add("dve", lambda e: e.scalar_tensor_tensor(out=xn[0:dh, :], in0=raw[0:dh, :], scalar=gain_ap, in1=rs2[0:dh, :], op0=ALU.mult, op1=ALU.mult),
                  r=raw.all + rs2.all, w=xn.all)
            ps3 = self.nps()
            self.mm(ps3[0:kd, :], PrT[0:kd, 0:kd], xn[0:kd, :], True, True, r=xn.all + PrT.all, w=ps3.all)
            P.add("pool", lambda e, sl=sl: e.tensor_tensor(out=t1[0:dh, :], in0=xn[0:dh, :], in1=Ct[0:dh, sl], op=ALU.mult), r=xn.all + Ct.all, w=t1.all)
            P.add("dve", lambda e, sl=sl, ps3=ps3: e.tensor_tensor(out=t2[0:dh, :], in0=ps3[0:dh, :], in1=St[0:dh, sl], op=ALU.mult), r=ps3.all + St.all, w=t2.all)
            P.add("pool", lambda e, sl=sl: e.tensor_tensor(out=outT[0:dh, sl], in0=t1[0:dh, :], in1=t2[0:dh, :], op=ALU.add), r=t1.all + t2.all, w=outT.all)

    def attn_norm(self, ps_o, ps_sum, out_ap, out_res, sink_ap, wk2):
        P = self.P
        den, bc = wk2
        if sink_ap is not None:
            P.add("dve", lambda e: e.tensor_scalar(out=den[0:64, :], in0=ps_sum[0:64, :], scalar1=sink_ap, scalar2=None, op0=ALU.add), r=ps_sum.all, w=den.all)
            P.add("dve", lambda e: e.reciprocal(out=bc[0:64, :], in_=den[0:64, :]), r=den.all, w=bc.all)
        else:
            P.add("dve", lambda e: e.reciprocal(out=bc[0:64, :], in_=ps_sum[0:64, :]), r=ps_sum.all, w=bc.all)
        P.add("dve", lambda e: e.tensor_tensor(out=out_ap, in0=ps_o[0:64, :], in1=bc[0:64, :], op=ALU.mult), r=ps_o.all + bc.all, w=out_res)

    def oproj_accum(self, wo_bf, OTc, nk):
        P = self.P
        for dc in range(8):
            for tt in range(NT):
                ps = self.nps()
                for k in range(nk):
                    self.mm(ps[:, :], wo_bf[:, k, dc * 128:(dc + 1) * 128], OTc[:, k, tt * 512:(tt + 1) * 512], k == 0, k == nk - 1,
                            r=wo_bf.all + OTc.all, w=ps.all)
                xs = self.X[:, dc, tt * 512:(tt + 1) * 512]
                P.add("dve", lambda e, xs=xs, ps=ps: e.tensor_tensor(out=xs, in0=ps[:, :], in1=xs, op=ALU.add), r=ps.all + [self.xr(dc, tt)], w=[self.xr(dc, tt)])

    def dbg(self, ap, slot, npart, ncols):
        if not getattr(self, "debug", False):
            return
        self.P.barrier()
        self.P.add("dve", lambda e: e.tensor_copy(out=self.X[0:npart, slot, 0:ncols], in_=ap), r=[], w=self.X.all)
        self.P.barrier()

    def qk_work(self, es):
        P = self.P
        return [P.sb([128, 512], F32, es=es) for _ in range(7)]

    def swa(self, j):
        P = self.P
        d = self.d
        li = 3 * j
        with ExitStack() as es:
            gains = P.sb([128, 8], F32, es=es)
            P.dma(gains[:], d["norm_tok"][li].rearrange("(c p) -> p c", p=128), w=gains.all, allow_slow_non_contiguous=True)
            qg_t = P.sb([64, 1], F32, es=es)
            kg_t = P.sb([64, 1], F32, es=es)
            P.dma(qg_t[:], d["swa_q_gain"][j].rearrange("(p o) -> p o", o=1), w=qg_t.all, allow_slow_non_contiguous=True)
            P.dma(kg_t[:], d["swa_k_gain"][j].rearrange("(p o) -> p o", o=1), w=kg_t.all, allow_slow_non_contiguous=True)
            sk = P.sb([128, 16], F32, es=es)
            P.dma(sk[:], d["swa_sinks"][j:j + 1, :].to_broadcast([128, 16]), w=sk.all, allow_slow_non_contiguous=True)
            sk0 = sk
            sk = P.sb([128, 16], F32, es=es)
            P.add("act", lambda e: e.activation(out=sk[:], in_=sk0[:], func=AF.Exp), r=sk0.all, w=sk.all)
            Ct = P.sb([64, S], F32, es=es)
            St = P.sb([64, S], F32, es=es)
            PrT = P.sb([64, 64], F32, es=es)
            MLO = P.sb([128, 128], BF16, es=es)
            MHI = P.sb([128, 128], BF16, es=es)
            mstg = P.sb([128, 256], F32, es=es)
            P.dma(Ct[:], d["c_swa_cos"], w=Ct.all)
            P.dma(St[:], d["c_swa_sin"], w=St.all)
            P.dma(PrT[:], d["c_swa_rot"], w=PrT.all)
            P.dma(mstg[:, 0:128], d["c_mlo"], w=mstg.all)
            P.dma(mstg[:, 128:256], d["c_mhi"], w=mstg.all)
            P.add("dve", lambda e: e.tensor_copy(out=MLO[:], in_=mstg[:, 0:128]), r=mstg.all, w=MLO.all)
            P.add("dve", lambda e: e.tensor_copy(out=MHI[:], in_=mstg[:, 128:256]), r=mstg.all, w=MHI.all)
            H = P.sb([128, 8, S], BF16, es=es)
            self.rmsnorm(H, gains, es)
            stg = [P.sb([128, 2048], F32, es=es) for _ in range(2)]
            wqkv = d["swa_w_qkv"][j]
            wo = d["swa_w_o"][j]
            Wkv = P.sb([128, 8, 512], BF16, es=es)
            self.load_w(Wkv, lambda k0, k1: Wkv[:, k0:k1, :], wqkv[:, 1024:1536], 8, 512, stg)
            V = P.sb([128, 16, 4, 65], BF16, es=es)
            P.add("pool", lambda e: e.memset(V[:], 1.0), w=V.all)
            for ti in range(16):
                ps = self.nps()
                for k in range(8):
                    self.mm(ps[:, 0:256], H[:, k, ti * 128:(ti + 1) * 128], Wkv[:, k, 256:512], k == 0, k == 7, r=H.all + Wkv.all, w=ps.all)
                P.add("act", lambda e, ti=ti, ps=ps: e.copy(out=V[:, ti, :, 0:64], in_=ps[:, 0:256].rearrange("p (g v) -> p g v", v=64)), r=ps.all, w=V.all)
            wk = self.qk_work(es)
            den = P.sb([128, 512], F32, es=es)
            bc = P.sb([128, 512], F32, es=es)
            KT = P.sb([64, S], BF16, es=es)
            QT = P.sb([64, S], BF16, es=es)
            PT = [P.sb([128, 384], BF16, es=es) for _ in range(2)]
            OTc = P.sb([128, 2, S], BF16, es=es)
            Wq = P.sb([128, 8, 256], BF16, es=es)
            Wo = P.sb([128, 2, 1024], BF16, es=es)
            scale = 64 ** -0.5
            for g in range(4):
                self.load_w(Wq, lambda k0, k1: Wq[:, k0:k1, :], wqkv[:, g * 256:(g + 1) * 256], 8, 256, stg)
                self.load_w(Wo, lambda k0, k1: Wo[:, k0:k1, :], wo[g * 256:(g + 1) * 256, :], 2, 1024, stg)
                self.head_qk(KT, lambda tt: [(Wkv[:, k, g * 64:(g + 1) * 64], H[:, k, tt * 512:(tt + 1) * 512], H.all + Wkv.all) for k in range(8)],
                             64, kg_t[:, 0:1], PrT, Ct, St, wk)
                for hh in range(4):
                    h = g * 4 + hh
                    self.head_qk(QT, lambda tt: [(Wq[:, k, hh * 64:(hh + 1) * 64], H[:, k, tt * 512:(tt + 1) * 512], H.all + Wq.all) for k in range(8)],
                                 64, qg_t[:, 0:1], PrT, Ct, St, wk)
                    for qg in range(4):
                        ps_o = self.ps[6]
                        ps_sum = self.ps[7]
                        for qi in range(4):
                            i = qg * 4 + qi
                            js = [jj for jj in (i - 1, i, i + 1) if 0 <= jj < 16]
                            ps_s = self.nps()
                            for n, jj in enumerate(js):
                                self.mm(ps_s[:, n * 128:(n + 1) * 128], KT[:, jj * 128:(jj + 1) * 128], QT[:, i * 128:(i + 1) * 128], True, True,
                                        r=KT.all + QT.all, w=ps_s.all)
                            pt = PT[i % 2]
                            nn = len(js) * 128
                            P.add("act", lambda e, pt=pt, ps_s=ps_s, nn=nn: e.activation(out=pt[:, 0:nn], in_=ps_s[:, 0:nn], func=AF.Exp, scale=scale), r=ps_s.all, w=pt.all)
                            for n, jj in enumerate(js):
                                if jj == i - 1:
                                    P.add("pool", lambda e, pt=pt, n=n: e.tensor_tensor(out=pt[:, n * 128:(n + 1) * 128], in0=pt[:, n * 128:(n + 1) * 128], in1=MHI[:], op=ALU.mult),
                                          r=pt.all + MHI.all, w=pt.all)
                                elif jj == i + 1:
                                    P.add("pool", lambda e, pt=pt, n=n: e.tensor_tensor(out=pt[:, n * 128:(n + 1) * 128], in0=pt[:, n * 128:(n + 1) * 128], in1=MLO[:], op=ALU.mult),
                                          r=pt.all + MLO.all, w=pt.all)
                            for n, jj in enumerate(js):
                                self.mm(ps_o[0:64, qi * 128:(qi + 1) * 128], V[:, jj, g, 0:64], pt[:, n * 128:(n + 1) * 128], n == 0, n == len(js) - 1,
                                        r=V.all + pt.all, w=ps_o.all)
                            for n, jj in enumerate(js):
                                self.mm(ps_sum[0:64, qi * 128:(qi + 1) * 128], self.ones_bf[:, 0:64], pt[:, n * 128:(n + 1) * 128], n == 0, n == len(js) - 1,
                                        r=self.ones_bf.all + pt.all, w=ps_sum.all)
                        pb = (hh % 2) * 64
                        self.attn_norm(ps_o, ps_sum, OTc[pb:pb + 64, hh // 2, qg * 512:(qg + 1) * 512], OTc.all, sk[0:64, h:h + 1], (den, bc))
                if g == 0:
                    self.dbg(H[:, 0, :], 0, 128, S)
                    self.dbg(KT[:, :], 1, 64, S)
                    self.dbg(QT[:, :], 2, 64, S)
                    self.dbg(OTc[:, 0, :], 3, 128, S)
                    self.dbg(sk[:, :], 4, 128, 16)
                    self.dbg(V[:, 0, 0, 0:64], 5, 128, 64)
                    self.dbg(OTc[:, 1, :], 6, 128, S)
                if not getattr(self, "debug", False):
                    self.oproj_accum(Wo, OTc, 2)
        P.barrier()

    def mla(self, j):
        P = self.P
        d = self.d
        li = 2
        with ExitStack() as es:
            CQ = P.sb([128, 3, S], BF16, es=es)
            CKV = P.sb([128, 2, S], BF16, es=es)
            KR = P.sb([32, S], BF16, es=es)
            stg = [P.sb([128, 2048], F32, es=es) for _ in range(2)]
            with ExitStack() as esA:
                gains = P.sb([128, 8], F32, es=esA)
                P.dma(gains[:], d["norm_tok"][li].rearrange("(c p) -> p c", p=128), w=gains.all, allow_slow_non_contiguous=True)
                cg = P.sb([128, 5], F32, es=esA)
                P.dma(cg[:, 0:3], d["mla_cq_gain"][j].rearrange("(c p) -> p c", p=128), w=cg.all, allow_slow_non_contiguous=True)
                P.dma(cg[:, 3:5], d["mla_ckv_gain"][j].rearrange("(c p) -> p c", p=128), w=cg.all, allow_slow_non_contiguous=True)
                H = P.sb([128, 8, S], BF16, es=esA)
                self.rmsnorm(H, gains, esA)
                Wd = P.sb([128, 8, 672], BF16, es=esA)
                self.load_w(Wd, lambda k0, k1: Wd[:, k0:k1, :], d["mla_w_down"][j], 8, 672, stg)
                raw = [P.sb([128, 512], F32, es=esA) for _ in range(5)]
                sq = [P.sb([128, 512], F32, es=esA) for _ in range(5)]
                rs = P.sb([128, 512], F32, es=esA)
                rs2 = P.sb([128, 512], F32, es=esA)
                for tt in range(NT):
                    sl = slice(tt * 512, (tt + 1) * 512)
                    for c in range(6):
                        m = 128 if c < 5 else 32
                        ps = self.nps()
                        for k in range(8):
                            self.mm(ps[0:m, :], Wd[:, k, c * 128:c * 128 + m], H[:, k, sl], k == 0, k == 7, r=H.all + Wd.all, w=ps.all)
                        if c < 5:
                            P.add("act", lambda e, ps=ps, c=c: e.copy(out=raw[c][:], in_=ps[:, :]), r=ps.all, w=raw[c].all)
                            P.add("act", lambda e, ps=ps, c=c: e.activation(out=sq[c][:], in_=ps[:, :], func=AF.Square), r=ps.all, w=sq[c].all)
                        else:
                            P.add("act", lambda e, ps=ps, sl=sl: e.copy(out=KR[0:32, sl], in_=ps[0:32, :]), r=ps.all, w=KR.all)
                    for (c0, c1, dst, nf) in ((0, 3, CQ, 384), (3, 5, CKV, 256)):
                        ps = self.nps()
                        for c in range(c0, c1):
                            self.mm(ps[:, :], self.ones_f[:, :], sq[c][:], c == c0, c == c1 - 1, r=sq[c].all + self.ones_f.all, w=ps.all)
                        P.add("act", lambda e, ps=ps, nf=nf: e.activation(out=rs[:], in_=ps[:, :], func=AF.Sqrt, bias=EPS, scale=1.0 / nf), r=ps.all, w=rs.all)
                        P.add("dve", lambda e: e.reciprocal(out=rs2[:], in_=rs[:]), r=rs.all, w=rs2.all)
                        for c in range(c0, c1):
                            P.add("dve", lambda e, c=c, c0=c0, dst=dst, sl=sl: e.scalar_tensor_tensor(out=dst[:, c - c0, sl], in0=raw[c][:], scalar=cg[:, c:c + 1], in1=rs2[:],
                                                                                                 op0=ALU.mult, op1=ALU.mult), r=raw[c].all + rs2.all + cg.all, w=dst.all)
            P.barrier()
            qg_t = P.sb([96, 1], F32, es=es)
            kg_t = P.sb([96, 1], F32, es=es)
            P.dma(qg_t[:], d["mla_q_gain"][j].rearrange("(p o) -> p o", o=1), w=qg_t.all, allow_slow_non_contiguous=True)
            P.dma(kg_t[:], d["mla_k_gain"][j].rearrange("(p o) -> p o", o=1), w=kg_t.all, allow_slow_non_contiguous=True)
            Ct = P.sb([96, S], F32, es=es)
            St = P.sb([96, S], F32, es=es)
            PrT = P.sb([128, 128], F32, es=es)
            ones96 = P.sb([128, 128], F32, es=es)
            P.dma(ones96[:], d["c_ones96"], w=ones96.all)
            IdS = P.sb([32, 96], BF16, es=es)
            P.dma(Ct[:], d["c_mla_cos"], w=Ct.all)
            P.dma(St[:], d["c_mla_sin"], w=St.all)
            P.dma(PrT[:], d["c_mla_rot"], w=PrT.all)
            P.dma(stg[0][0:32, 0:96], d["c_mla_ids"], w=stg[0].all)
            P.add("dve", lambda e: e.tensor_copy(out=IdS[:], in_=stg[0][0:32, 0:96]), r=stg[0].all, w=IdS.all)
            Wuq = P.sb([128, 3, 1536], BF16, es=es)
            self.load_w(Wuq, lambda k0, k1: Wuq[:, k0:k1, :], d["mla_w_uq"][j], 3, 1536, stg)
            Wkn = P.sb([128, 2, 16, 96], BF16, es=es)
            Wv = P.sb([128, 2, 16, 64], BF16, es=es)
            P.add("pool", lambda e: e.memset(Wkn[:], 0.0), w=Wkn.all)
            wukv = d["mla_w_ukv"][j]
            for k in range(2):
                st = stg[k % 2]
                P.dma(st[:, 0:2048], wukv[k * 128:(k + 1) * 128, :], w=st.all)
                sv = st[:, 0:2048].rearrange("p (h t) -> p h t", t=128)
                P.add("act", lambda e, k=k, sv=sv: e.copy(out=Wkn[:, k, :, 0:64], in_=sv[:, :, 0:64]), r=st.all, w=Wkn.all)
                P.add("pool", lambda e, k=k, sv=sv: e.tensor_copy(out=Wv[:, k, :, :], in_=sv[:, :, 64:128]), r=st.all, w=Wv.all)
            wk = self.qk_work(es)
            for t_ in wk:
                P.add("pool", lambda e, t_=t_: e.memset(t_[:], 0.0), w=t_.all)
            den = P.sb([128, 512], F32, es=es)
            bc = P.sb([128, 512], F32, es=es)
            KT = P.sb([128, S], BF16, es=es)
            QT = P.sb([128, S], BF16, es=es)
            P.add("pool", lambda e: e.memset(KT[:], 0.0), w=KT.all)
            P.add("pool", lambda e: e.memset(QT[:], 0.0), w=QT.all)
            Vh = P.sb([128, 16, 65], BF16, es=es)
            P.add("pool", lambda e: e.memset(Vh[:], 1.0), w=Vh.all)
            PT = [P.sb([128, 512], BF16, es=es) for _ in range(3)]
            OTc = P.sb([128, 1, S], BF16, es=es)
            Wo = P.sb([128, 1, 1024], BF16, es=es)
            wo = d["mla_w_o"][j]
            scale = 96 ** -0.5
            pti = 0
            for h in range(16):
                if h % 2 == 0:
                    self.load_w(Wo, lambda k0, k1: Wo[:, k0:k1, :], wo[(h // 2) * 128:(h // 2 + 1) * 128, :], 1, 1024, stg)
                for ti in range(16):
                    ps = self.nps()
                    for k in range(2):
                        self.mm(ps[:, 0:64], CKV[:, k, ti * 128:(ti + 1) * 128], Wv[:, k, h, :], k == 0, k == 1, r=CKV.all + Wv.all, w=ps.all)
                    P.add("act", lambda e, ti=ti, ps=ps: e.copy(out=Vh[:, ti, 0:64], in_=ps[:, 0:64]), r=ps.all, w=Vh.all)
                self.head_qk(KT, lambda tt: [(Wkn[:, k, h, :], CKV[:, k, tt * 512:(tt + 1) * 512], CKV.all + Wkn.all) for k in range(2)]
                             + [(IdS[0:32, :], KR[0:32, tt * 512:(tt + 1) * 512], IdS.all + KR.all)],
                             96, kg_t[:, 0:1], PrT, Ct, St, wk, kd=128, ones_t=ones96)
                self.head_qk(QT, lambda tt: [(Wuq[:, k, h * 96:(h + 1) * 96], CQ[:, k, tt * 512:(tt + 1) * 512], CQ.all + Wuq.all) for k in range(3)],
                             96, qg_t[:, 0:1], PrT, Ct, St, wk, kd=128, ones_t=ones96)
                for qg in range(4):
                    ps_o = self.ps[6]
                    ps_sum = self.ps[7]
                    for jj in range(16):
                        ps_s = self.nps()
                        self.mm(ps_s[:, :], KT[:, jj * 128:(jj + 1) * 128], QT[:, qg * 512:(qg + 1) * 512], True, True, r=KT.all + QT.all, w=ps_s.all)
                        pt = PT[pti % 3]
                        pti += 1
                        P.add("act", lambda e, pt=pt, ps_s=ps_s: e.activation(out=pt[:], in_=ps_s[:, :], func=AF.Exp, scale=scale), r=ps_s.all, w=pt.all)
                        self.mm(ps_o[0:64, :], Vh[:, jj, 0:64], pt[:], jj == 0, jj == 15, r=Vh.all + pt.all, w=ps_o.all)
                        self.mm(ps_sum[0:64, :], self.ones_bf[:, 0:64], pt[:], jj == 0, jj == 15, r=self.ones_bf.all + pt.all, w=ps_sum.all)
                    pb = (h % 2) * 64
                    self.attn_norm(ps_o, ps_sum, OTc[pb:pb + 64, 0, qg * 512:(qg + 1) * 512], OTc.all, None, (den, bc))
                if h % 2 == 1:
                    self.oproj_accum(Wo, OTc, 1)
        P.barrier()

    def o_tt(self, eng, out, in0, in1, op, r, w):
        self.P.add(eng, lambda e: e.tensor_tensor(out=out, in0=in0, in1=in1, op=op), r=r, w=w)

    def o_ts(self, eng, out, in0, s1, s2, op0, op1, r, w):
        if s2 is None:
            self.P.add(eng, lambda e: e.tensor_scalar(out=out, in0=in0, scalar1=s1, scalar2=None, op0=op0), r=r, w=w)
        else:
            self.P.add(eng, lambda e: e.tensor_scalar(out=out, in0=in0, scalar1=s1, scalar2=s2, op0=op0, op1=op1), r=r, w=w)

    def o_stt(self, out, in0, scalar, in1, op0, op1, r, w):
        self.P.add("dve", lambda e: e.scalar_tensor_tensor(out=out, in0=in0, scalar=scalar, in1=in1, op0=op0, op1=op1), r=r, w=w)

    def o_act(self, out, in_, func, r, w, bias=None, scale=None):
        kw = {}
        if bias is not None:
            kw["bias"] = bias
        if scale is not None:
            kw["scale"] = scale
        self.P.add("act", lambda e: e.activation(out=out, in_=in_, func=func, **kw), r=r, w=w)

    def o_cp(self, eng, out, in_, r, w):
        if eng == "act":
            self.P.add("act", lambda e: e.copy(out=out, in_=in_), r=r, w=w)
        else:
            self.P.add(eng, lambda e: e.tensor_copy(out=out, in_=in_), r=r, w=w)

    def rwkv(self, j):
        P = self.P
        d = self.d
        nc = self.nc
        li = 1
        NCH = S // 64
        def scr(name, shape, dt):
            t = nc.dram_tensor(name, list(shape), dt)
            return t.ap(), Res()
        S_ar = [scr(f"rw_ar{dd}", [128, 8, 2, S], BF16) for dd in range(2)]
        S_b = [scr(f"rw_b{dd}", [128, 8, S], BF16) for dd in range(2)]
        S_k = [scr(f"rw_k{dd}", [128, 8, S], BF16) for dd in range(2)]
        S_v = scr("rw_v", [128, 8, S], BF16)
        S_pc = [scr(f"rw_pc{dd}", [NCH, 128, 8], F32) for dd in range(2)]
        S_g = scr("rw_g", [128, 8, S], F32)
        S_bn = scr("rw_bn", [128, 8, S], F32)
        S_y = scr("rw_y", [128, 8, S], F32)

        with ExitStack() as es:
            gains = P.sb([128, 8], F32, es=es)
            P.dma(gains[:], d["norm_tok"][li].rearrange("(c p) -> p c", p=128), w=gains.all, allow_slow_non_contiguous=True)
            RSTD = P.sb([128, S], F32, es=es)
            Wr = P.sb([128, 8, 1024], BF16, es=es)
            Wk = P.sb([128, 8, 1024], BF16, es=es)
            Wv = P.sb([128, 8, 1024], BF16, es=es)
            W1 = P.sb([128, 8, 2, 64], BF16, es=es)
            A1 = P.sb([128, 8, 2, 64], BF16, es=es)
            G1 = P.sb([128, 8, 160], BF16, es=es)
            W2 = P.sb([64, 2, 1024], BF16, es=es)
            A2 = P.sb([64, 2, 1024], BF16, es=es)
            G2a = P.sb([128, 1024], BF16, es=es)
            G2b = P.sb([32, 1024], BF16, es=es)
            W0bc = P.sb([128, 2, 1024], F32, es=es)
            MU = P.sb([128, 6, 8], F32, es=es)
            A0 = P.sb([128, 2, 8], F32, es=es)
            KK_ = P.sb([128, 8], F32, es=es)
            KA_ = P.sb([128, 8], F32, es=es)
            RK_ = P.sb([128, 8], F32, es=es)
            BD64 = P.sb([128, 128], F32, es=es)
            TRI = [P.sb([128, 256], F32, es=es) for _ in range(2)]
            P.dma(MU[:], d["rwkv_mu"][j].rearrange("i (c p) -> p i c", p=128), w=MU.all, allow_slow_non_contiguous=True)
            P.dma(A0[:], d["rwkv_a0"][j].rearrange("i (c p) -> p i c", p=128), w=A0.all, allow_slow_non_contiguous=True)
            P.dma(KK_[:], d["rwkv_k_k"][j].rearrange("(c p) -> p c", p=128), w=KK_.all, allow_slow_non_contiguous=True)
            P.dma(KA_[:], d["rwkv_k_a"][j].rearrange("(c p) -> p c", p=128), w=KA_.all, allow_slow_non_contiguous=True)
            P.dma(RK_[:], d["rwkv_r_k"][j].rearrange("h k -> (h k)").rearrange("(c p) -> p c", p=128), w=RK_.all, allow_slow_non_contiguous=True)
            P.dma(BD64[:], d["c_bd64"], w=BD64.all)
            P.dma(TRI[0][:], d["c_tri_f"], w=TRI[0].all)
            P.dma(TRI[1][:], d["c_tri_b"], w=TRI[1].all)
            for dd in range(2):
                P.dma(W0bc[:, dd, :], d["rwkv_w0"][j, dd:dd + 1, :].to_broadcast([128, 1024]), w=W0bc.all, allow_slow_non_contiguous=True)
            with ExitStack() as esw:
                stg = [P.sb([128, 2048], F32, es=esw) for _ in range(2)]
                self.load_w(Wr, lambda k0, k1: Wr[:, k0:k1, :], d["rwkv_w_r"][j], 8, 1024, stg)
                self.load_w(Wk, lambda k0, k1: Wk[:, k0:k1, :], d["rwkv_w_k"][j], 8, 1024, stg)
                self.load_w(Wv, lambda k0, k1: Wv[:, k0:k1, :], d["rwkv_w_v"][j], 8, 1024, stg)
                for dd in range(2):
                    self.load_w(W1, lambda k0, k1, dd=dd: W1[:, k0:k1, dd, :], d["rwkv_w1"][j, dd], 8, 64, stg)
                    self.load_w(A1, lambda k0, k1, dd=dd: A1[:, k0:k1, dd, :], d["rwkv_a1"][j, dd], 8, 64, stg)
                self.load_w(G1, lambda k0, k1: G1[:, k0:k1, :], d["rwkv_g1"][j], 8, 160, stg)
                for dd in range(2):
                    for (dst, src) in ((W2, d["rwkv_w2"][j, dd]), (A2, d["rwkv_a2"][j, dd])):
                        st = stg[self.stg_rr % 2]
                        self.stg_rr += 1
                        P.dma(st[0:64, 0:1024], src, w=st.all)
                        self.cast(dst[:, dd, :], st[0:64, 0:1024], r=st.all, w=dst.all)
                st = stg[self.stg_rr % 2]
                self.stg_rr += 1
                P.dma(st[:, 0:1024], d["rwkv_g2"][j][0:128, :], w=st.all)
                self.cast(G2a[:], st[:, 0:1024], r=st.all, w=G2a.all)
                st = stg[self.stg_rr % 2]
                self.stg_rr += 1
                P.dma(st[0:32, 0:1024], d["rwkv_g2"][j][128:160, :], w=st.all)
                self.cast(G2b[:], st[0:32, 0:1024], r=st.all, w=G2b.all)
                self.rmsnorm(None, gains, esw, rstd_out=RSTD)
                P.barrier()
            HXf = P.sb([128, 2064], F32, es=es)
            HX = HXf
            Hh_ = HXf[:, 0:1040].rearrange("p (c t) -> p c t", t=130)
            XX_ = HXf[:, 1040:2064].rearrange("p (c t) -> p c t", t=128)
            TMPM = P.sb([128, 8, 128], F32, es=es)
            MIX = [P.sb([128, 8, 128], BF16, es=es) for _ in range(6)]
            O_ar = [P.sb([128, 8, 2, 128], BF16, es=es) for _ in range(2)]
            O_b = [P.sb([128, 8, 128], BF16, es=es) for _ in range(2)]
            O_k = [P.sb([128, 8, 128], BF16, es=es) for _ in range(2)]
            O_v = P.sb([128, 8, 128], BF16, es=es)
            O_pc = [P.sb([128, 2, 8], F32, es=es) for _ in range(2)]
            L1w = [P.sb([64, 128], BF16, es=es) for _ in range(2)]
            L1a = [P.sb([64, 128], BF16, es=es) for _ in range(2)]
            L1g = P.sb([128, 128], BF16, es=es)
            L1g2 = P.sb([32, 128], BF16, es=es)
            sm = [P.sb([128, 128], F32, es=es) for _ in range(22)]
            gq = 0
            (t_r, t_k, t_v, t_kq, t_sq, t_nr, t_kk, t_rr, t_a, t_t, t_kd, t_b, t_ep, t_em, t_epv, t_sb, t_sb2, t_x1, t_x2, t_x3, t_x4, t_x5) = sm
            for ti in range(16):
                t0 = ti * 128
                tt = ti // 4
                lo = max(0, t0 - 1)
                hi = min(S, t0 + 129)
                c0 = lo - (t0 - 1)
                n = hi - lo
                xrs = [self.xr(c, q) for c in range(8) for q in sorted(set([lo // 512, (hi - 1) // 512]))]
                if t0 == 0:
                    P.add("pool", lambda e: e.memset(Hh_[:, :, 0:1], 0.0), w=HX.all)
                if t0 + 129 > S:
                    P.add("pool", lambda e: e.memset(Hh_[:, :, 129:130], 0.0), w=HX.all)
                for c in range(8):
                    self.o_stt(Hh_[:, c, c0:c0 + n], self.X[:, c, lo:hi], gains[:, c:c + 1], RSTD[:, lo:hi], ALU.mult, ALU.mult, r=xrs + RSTD.all, w=HX.all)
                hc = Hh_[:, :, 1:129]
                xx = XX_
                self.o_tt("pool", TMPM[:], Hh_[:, :, 0:128], Hh_[:, :, 2:130], ALU.add, r=HX.all, w=TMPM.all)
                self.o_stt(xx, TMPM[:], 0.5, hc, ALU.mult, ALU.subtract, r=TMPM.all + HX.all, w=HX.all)
                for i in range(6):
                    self.o_tt("dve", TMPM[:], xx, MU[:, i, :].unsqueeze(2).to_broadcast([128, 8, 128]), ALU.mult, r=HX.all + MU.all, w=TMPM.all)
                    self.o_tt("pool", MIX[i][:], TMPM[:], hc, ALU.add, r=TMPM.all + HX.all, w=MIX[i].all)
                m_r, m_w, m_k, m_v, m_a, m_g = MIX
                for dd in range(2):
                    ps = self.nps()
                    for k in range(8):
                        self.mm(ps[0:64, 0:128], W1[:, k, dd, :], m_w[:, k, :], k == 0, k == 7, r=W1.all + m_w.all, w=ps.all)
                    self.o_act(L1w[dd][:], ps[0:64, 0:128], AF.Tanh, r=ps.all, w=L1w[dd].all)
                    ps = self.nps()
                    for k in range(8):
                        self.mm(ps[0:64, 0:128], A1[:, k, dd, :], m_a[:, k, :], k == 0, k == 7, r=A1.all + m_a.all, w=ps.all)
                    self.o_cp("act", L1a[dd][:], ps[0:64, 0:128], r=ps.all, w=L1a[dd].all)
                ps = self.nps()
                for k in range(8):
                    self.mm(ps[:, 0:128], G1[:, k, 0:128], m_g[:, k, :], k == 0, k == 7, r=G1.all + m_g.all, w=ps.all)
                self.o_act(L1g[:], ps[:, 0:128], AF.Sigmoid, r=ps.all, w=L1g.all)
                ps = self.nps()
                for k in range(8):
                    self.mm(ps[0:32, 0:128], G1[:, k, 128:160], m_g[:, k, :], k == 0, k == 7, r=G1.all + m_g.all, w=ps.all)
                self.o_act(L1g2[:], ps[0:32, 0:128], AF.Sigmoid, r=ps.all, w=L1g2.all)
                LW = HXf[:, 0:2048]
                for dd in range(2):
                    for hf in range(2):
                        ps = self.nps()
                        self.mm(ps[:, :], L1w[dd][:], W2[:, dd, hf * 512:(hf + 1) * 512], True, True, r=L1w[dd].all + W2.all, w=ps.all)
                        sl = slice(dd * 1024 + hf * 512, dd * 1024 + (hf + 1) * 512)
                        self.o_tt("dve", LW[:, sl], ps[:, :], W0bc[:, dd, hf * 512:(hf + 1) * 512], ALU.add, r=ps.all + W0bc.all, w=HX.all)
                    sl = slice(dd * 1024, (dd + 1) * 1024)
                    self.o_act(LW[:, sl], LW[:, sl], AF.Sigmoid, r=HX.all, w=HX.all)
                    self.o_ts("pool", LW[:, sl], LW[:, sl], -0.6065306597126334, None, ALU.mult, None, r=HX.all, w=HX.all)
                for oc in range(8):
                    fs = slice(oc * 128, (oc + 1) * 128)
                    for (wt, mx, dst) in ((Wr, m_r, t_r), (Wk, m_k, t_k), (Wv, m_v, t_v)):
                        ps = self.nps()
                        for k in range(8):
                            self.mm(ps[:, 0:128], wt[:, k, fs], mx[:, k, :], k == 0, k == 7, r=wt.all + mx.all, w=ps.all)
                        self.o_cp("act", dst[:], ps[:, 0:128], r=ps.all, w=dst.all)
                    self.o_cp("pool", O_v[:, oc, :], t_v[:], r=t_v.all, w=O_v.all)
                    self.o_ts("dve", t_kq[:], t_k[:], KK_[:, oc:oc + 1], None, ALU.mult, None, r=t_k.all + KK_.all, w=t_kq.all)
                    self.o_tt("pool", t_sq[:], t_kq[:], t_kq[:], ALU.mult, r=t_kq.all, w=t_sq.all)
                    ps = self.nps()
                    self.mm(ps[:, 0:128], BD64[:], t_sq[:], True, True, r=BD64.all + t_sq.all, w=ps.all)
                    self.o_act(t_nr[:], ps[:, 0:128], AF.Sqrt, r=ps.all, w=t_nr.all)
                    self.o_ts("dve", t_nr[:], t_nr[:], 1e-12, None, ALU.max, None, r=t_nr.all, w=t_nr.all)
                    P.add("dve", lambda e: e.reciprocal(out=t_sq[:], in_=t_nr[:]), r=t_nr.all, w=t_sq.all)
                    self.o_tt("dve", t_kk[:], t_kq[:], t_sq[:], ALU.mult, r=t_kq.all + t_sq.all, w=t_kk.all)
                    self.o_ts("pool", t_rr[:], t_r[:], RK_[:, oc:oc + 1], None, ALU.mult, None, r=t_r.all + RK_.all, w=t_rr.all)
                    ps = self.nps()
                    self.mm(ps[:, 0:128], G2a[:, fs], L1g[:], True, False, r=G2a.all + L1g.all, w=ps.all)
                    self.mm(ps[:, 0:128], G2b[:, fs], L1g2[:], False, True, r=G2b.all + L1g2.all, w=ps.all)
                    tg = (t_x1, t_x2)[oc % 2]
                    self.o_cp("act", tg[:], ps[:, 0:128], r=ps.all, w=tg.all)
                    P.dma(S_g[0][:, oc, t0:t0 + 128], tg[:], r=tg.all, w=[S_g[1]])
                    for dd in range(2):
                        ps = self.nps()
                        self.mm(ps[:, 0:128], A2[:, dd, fs], L1a[dd][:], True, True, r=A2.all + L1a[dd].all, w=ps.all)
                        self.o_act(t_a[:], ps[:, 0:128], AF.Sigmoid, r=ps.all + A0.all, w=t_a.all, bias=A0[:, dd, oc:oc + 1])
                        self.o_ts("dve", t_t[:], t_a[:], 1.0, KA_[:, oc:oc + 1], ALU.subtract, ALU.mult, r=t_a.all + KA_.all, w=t_t.all)
                        self.o_stt(t_kd[:], t_t[:], 1.0, t_k[:], ALU.add, ALU.mult, r=t_t.all + t_k.all, w=t_kd.all)
                        self.o_tt("pool", t_b[:], t_kk[:], t_a[:], ALU.mult, r=t_kk.all + t_a.all, w=t_b.all)
                        ps = self.nps()
                        self.mm(ps[:, 0:256], LW[:, dd * 1024 + oc * 128:dd * 1024 + (oc + 1) * 128], TRI[dd][:], True, True, r=HX.all + TRI[dd].all, w=ps.all)
                        self.o_act(t_ep[:], ps[:, 0:128], AF.Exp, r=ps.all, w=t_ep.all)
                        self.o_act(t_em[:], ps[:, 0:128], AF.Exp, r=ps.all, w=t_em.all, scale=-1.0)
                        self.o_act(t_epv[:], ps[:, 128:256], AF.Exp, r=ps.all, w=t_epv.all)
                        for cc in range(2):
                            col = cc * 64 + (63 if dd == 0 else 0)
                            self.o_cp("pool", O_pc[dd][:, cc, oc:oc + 1], t_ep[:, col:col + 1], r=t_ep.all, w=O_pc[dd].all)
                        self.o_stt(O_ar[dd][:, oc, 0, :], t_kk[:], -1.0, t_epv[:], ALU.mult, ALU.mult, r=t_kk.all + t_epv.all, w=O_ar[dd].all)
                        self.o_tt("pool", O_ar[dd][:, oc, 1, :], t_r[:], t_ep[:], ALU.mult, r=t_r.all + t_ep.all, w=O_ar[dd].all)
                        self.o_tt("dve", O_b[dd][:, oc, :], t_b[:], t_em[:], ALU.mult, r=t_b.all + t_em.all, w=O_b[dd].all)
                        self.o_tt("pool", O_k[dd][:, oc, :], t_kd[:], t_em[:], ALU.mult, r=t_kd.all + t_em.all, w=O_k[dd].all)
                        if dd == 0:
                            self.o_tt("dve", t_sb[:], t_rr[:], t_kd[:], ALU.mult, r=t_rr.all + t_kd.all, w=t_sb.all)
                        else:
                            self.o_tt("dve", t_sb2[:], t_rr[:], t_kd[:], ALU.mult, r=t_rr.all + t_kd.all, w=t_sb2.all)
                            self.o_tt("pool", t_sb[:], t_sb[:], t_sb2[:], ALU.add, r=t_sb.all + t_sb2.all, w=t_sb.all)
                    ps = self.nps()
                    self.mm(ps[:, 0:128], BD64[:], t_sb[:], True, True, r=BD64.all + t_sb.all, w=ps.all)
                    tb = (t_x3, t_x4)[oc % 2]
                    self.o_tt("dve", tb[:], ps[:, 0:128], t_v[:], ALU.mult, r=ps.all + t_v.all, w=tb.all)
                    P.dma(S_bn[0][:, oc, t0:t0 + 128], tb[:], r=tb.all, w=[S_bn[1]])
                ts_ = slice(t0, t0 + 128)
                for dd in range(2):
                    P.dma(S_ar[dd][0][:, :, :, ts_], O_ar[dd][:], r=O_ar[dd].all, w=[S_ar[dd][1]])
                    P.dma(S_b[dd][0][:, :, ts_], O_b[dd][:], r=O_b[dd].all, w=[S_b[dd][1]])
                    P.dma(S_k[dd][0][:, :, ts_], O_k[dd][:], r=O_k[dd].all, w=[S_k[dd][1]])
                    for cc in range(2):
                        P.dma(S_pc[dd][0][2 * ti + cc], O_pc[dd][:, cc, :], r=O_pc[dd].all, w=[S_pc[dd][1]])
                P.dma(S_v[0][:, :, ts_], O_v[:], r=O_v.all, w=[S_v[1]])
        P.barrier()
        import os
        if os.environ.get("RWKV_STOP") == "A":
            return

        with ExitStack() as es:
            IST = P.sb([128, 64], BF16, es=es)
            MSK = [P.sb([128, 512], BF16, es=es) for _ in range(2)]
            LMSK = [P.sb([128, 512], BF16, es=es) for _ in range(2)]
            BD64 = P.sb([128, 128], F32, es=es)
            LNW = P.sb([128, 8], F32, es=es)
            LNB = P.sb([128, 8], F32, es=es)
            Wo = P.sb([128, 8, 1024], BF16, es=es)
            P.dma(BD64[:], d["c_bd64"], w=BD64.all)
            P.dma(LNW[:], d["rwkv_lnx_w"][j].rearrange("(c p) -> p c", p=128), w=LNW.all, allow_slow_non_contiguous=True)
            P.dma(LNB[:], d["rwkv_lnx_b"][j].rearrange("(c p) -> p c", p=128), w=LNB.all, allow_slow_non_contiguous=True)
            with ExitStack() as esw:
                stg = [P.sb([128, 2048], F32, es=esw) for _ in range(2)]
                self.load_w(Wo, lambda k0, k1: Wo[:, k0:k1, :], d["rwkv_w_o"][j], 8, 1024, stg)
                P.dma(stg[0][:, 0:64], d["c_ist"], w=stg[0].all)
                self.o_cp("dve", IST[:], stg[0][:, 0:64], r=stg[0].all, w=IST.all)
                for dd, (mk, lk) in enumerate((("c_mask_f", "c_lmask_f"), ("c_mask_b", "c_lmask_b"))):
                    P.dma(stg[1][:, 0:512], d[mk], w=stg[1].all)
                    self.o_cp("dve", MSK[dd][:], stg[1][:, 0:512], r=stg[1].all, w=MSK[dd].all)
                    P.dma(stg[1][:, 512:1024], d[lk], w=stg[1].all)
                    self.o_cp("dve", LMSK[dd][:], stg[1][:, 512:1024], r=stg[1].all, w=LMSK[dd].all)
                P.barrier()
            def bdtile():
                t = P.sb([128, 8, 128], BF16, es=es)
                P.add("pool", lambda e: e.memset(t[:], 0.0), w=t.all)
                return t
            I_ar = [P.sb([128, 8, 128], BF16, es=es) for _ in range(2)]
            I_abd = [bdtile() for _ in range(2)]
            I_bbd = [bdtile() for _ in range(2)]
            I_kbd = [bdtile() for _ in range(2)]
            I_vbd = [bdtile() for _ in range(2)]
            I_b = [P.sb([128, 8, 64], BF16, es=es) for _ in range(2)]
            I_pc = [P.sb([128, 8], F32, es=es) for _ in range(2)]
            I_y = [P.sb([128, 8, 64], F32, es=es) for _ in range(2)]
            I_g = [P.sb([128, 8, 64], F32, es=es) for _ in range(2)]
            I_bn = [P.sb([128, 8, 64], F32, es=es) for _ in range(2)]
            V_st = P.sb([128, 8, 64], BF16, es=es)
            V_bd = bdtile()
            Bt_bd = bdtile()
            Kt_bd = bdtile()
            ATB = P.sb([128, 8, 128], BF16, es=es)
            ATK = P.sb([128, 8, 128], BF16, es=es)
            M_bd = bdtile()
            Aak_bd = bdtile()
            L_bd = bdtile()
            LX = P.sb([128, 8, 128], BF16, es=es)
            M_st = P.sb([128, 8, 64], BF16, es=es)
            Xf = P.sb([128, 8, 64], F32, es=es)
            U_bd = bdtile()
            H_f = P.sb([128, 8, 64], F32, es=es)
            H_bf = P.sb([128, 8, 64], BF16, es=es)
            H_bd = bdtile()
            TMPH = P.sb([128, 8, 64], F32, es=es)
            Yt = P.sb([128, 8, 64], F32, es=es)
            Y2 = P.sb([128, 8, 64], F32, es=es)
            Y3 = P.sb([128, 8, 64], F32, es=es)
            Y4 = P.sb([128, 8, 64], F32, es=es)
            Zt = P.sb([128, 8, 64], BF16, es=es)

            def bd_write(dst, src_lo, src_hi, r, eng0="dve", eng1="pool"):
                self.o_cp(eng0, dst[0:64, :, 0:64], src_lo, r=r, w=dst.all)
                self.o_cp(eng1, dst[64:128, :, 64:128], src_hi, r=r, w=dst.all)

            def load_chunk(dd, ch, buf):
                ts_ = slice(ch * 64, (ch + 1) * 64)
                P.dma(I_ar[buf][:].rearrange("p c (e t) -> p c e t", e=2), S_ar[dd][0][:, :, :, ts_], r=[S_ar[dd][1]], w=I_ar[buf].all)
                for e_ in range(2):
                    ps_ = slice(e_ * 64, (e_ + 1) * 64)
                    fs_ = slice(e_ * 64, (e_ + 1) * 64)
                    P.dma(I_abd[buf][ps_, :, fs_], S_ar[dd][0][ps_, :, 0, ts_], r=[S_ar[dd][1]], w=I_abd[buf].all)
                    P.dma(I_bbd[buf][ps_, :, fs_], S_b[dd][0][ps_, :, ts_], r=[S_b[dd][1]], w=I_bbd[buf].all)
                    P.dma(I_kbd[buf][ps_, :, fs_], S_k[dd][0][ps_, :, ts_], r=[S_k[dd][1]], w=I_kbd[buf].all)
                    P.dma(I_vbd[buf][ps_, :, fs_], S_v[0][ps_, :, ts_], r=[S_v[1]], w=I_vbd[buf].all)
                P.dma(I_b[buf][:], S_b[dd][0][:, :, ts_], r=[S_b[dd][1]], w=I_b[buf].all)
                P.dma(I_pc[buf][:], S_pc[dd][0][ch], r=[S_pc[dd][1]], w=I_pc[buf].all)
                if dd == 1:
                    P.dma(I_y[buf][:], S_y[0][:, :, ts_], r=[S_y[1]], w=I_y[buf].all)
                    P.dma(I_g[buf][:], S_g[0][:, :, ts_], r=[S_g[1]], w=I_g[buf].all)
                    P.dma(I_bn[buf][:], S_bn[0][:, :, ts_], r=[S_bn[1]], w=I_bn[buf].all)

            def bank(i):
                return i

            STEP = int(os.environ.get("RWKV_STEP", "99"))
            ndirs = int(os.environ.get("RWKV_DIRS", "2"))
            nlim = int(os.environ.get("RWKV_NCH", str(NCH)))
            for dd in range(ndirs):
                order = (list(range(NCH)) if dd == 0 else list(range(NCH - 1, -1, -1)))[:nlim]
                P.add("pool", lambda e: e.memset(H_f[:], 0.0), w=H_f.all)
                P.add("pool", lambda e: e.memset(H_bf[:], 0.0), w=H_bf.all)
                P.add("pool", lambda e: e.memset(H_bd[:], 0.0), w=H_bd.all)
                if os.environ.get('RWKV_NOLOAD') != '1':
                    load_chunk(dd, (order + [0])[0], 0)
                for oi, ch in enumerate(order):
                    buf = oi % 2
                    if oi + 1 < len(order):
                        load_chunk(dd, order[oi + 1], 1 - buf)
                    ar, abd, bbd, kbd, vbd, bst, pc = I_ar[buf], I_abd[buf], I_bbd[buf], I_kbd[buf], I_vbd[buf], I_b[buf], I_pc[buf]
                    tsl = slice(ch * 64, (ch + 1) * 64)
                    psv, psb, psk = self.nps(), self.nps(), self.nps()
                    for c in range(8):
                        self.mm(psv[:, c * 64:(c + 1) * 64], vbd[:, c, :], IST[:], True, True, r=vbd.all + IST.all, w=psv.all)
                        self.mm(psb[:, c * 64:(c + 1) * 64], bbd[:, c, :], IST[:], True, True, r=bbd.all + IST.all, w=psb.all)
                        self.mm(psk[:, c * 64:(c + 1) * 64], kbd[:, c, :], IST[:], True, True, r=kbd.all + IST.all, w=psk.all)
                    v3 = psv[:, :].rearrange("p (c v) -> p c v", v=64)
                    self.o_cp("act", V_st[:], v3, r=psv.all, w=V_st.all)
                    TF = int(os.environ.get("RWKV_T", "9"))
                    if TF <= 1:
                        continue
                    self.o_cp("dve", V_bd[0:64, :, 0:64], v3[0:64], r=psv.all, w=V_bd.all)
                    if TF <= 2:
                        continue
                    self.o_cp("act", V_bd[64:128, :, 64:128], v3[64:128], r=psv.all, w=V_bd.all)
                    if TF <= 3:
                        continue
                    b3 = psb[:, :].rearrange("p (c v) -> p c v", v=64)
                    bd_write(Bt_bd, b3[0:64], b3[64:128], psb.all, "dve", "act")
                    k3 = psk[:, :].rearrange("p (c v) -> p c v", v=64)
                    bd_write(Kt_bd, k3[0:64], k3[64:128], psk.all, "dve", "act")
                    if STEP <= 1:
                        continue
                    pb = [self.nps(), self.nps()]
                    pk = [self.nps(), self.nps()]
                    pl = self.nps()
                    for c in range(8):
                        cs = slice((c % 4) * 128, (c % 4 + 1) * 128)
                        self.mm(pb[c // 4][:, cs], bbd[:, c, :], ar[:, c, :], True, True, r=bbd.all + ar.all, w=pb[c // 4].all)
                        self.mm(pk[c // 4][:, cs], kbd[:, c, :], ar[:, c, :], True, True, r=kbd.all + ar.all, w=pk[c // 4].all)
                        self.mm(pl[:, c * 64:(c + 1) * 64], abd[:, c, :], bst[:, c, :], True, True, r=abd.all + bst.all, w=pl.all)
                    for hb in range(2):
                        self.o_tt("dve", ATB[:, hb * 4:(hb + 1) * 4, :], pb[hb][:, :].rearrange("p (c t) -> p c t", t=128), MSK[dd][:, :].rearrange("p (c t) -> p c t", t=128),
                                  ALU.mult, r=pb[hb].all + MSK[dd].all, w=ATB.all)
                        self.o_tt("dve", ATK[:, hb * 4:(hb + 1) * 4, :], pk[hb][:, :].rearrange("p (c t) -> p c t", t=128), MSK[dd][:, :].rearrange("p (c t) -> p c t", t=128),
                                  ALU.mult, r=pk[hb].all + MSK[dd].all, w=ATK.all)
                    bd_write(M_bd, ATB[0:64, :, 0:64], ATB[64:128, :, 0:64], ATB.all, "pool", "pool")
                    bd_write(Aak_bd, ATK[0:64, :, 0:64], ATK[64:128, :, 0:64], ATK.all, "pool", "pool")
                    self.o_cp("act", M_st[:], ATB[:, :, 0:64], r=ATB.all, w=M_st.all)
                    self.o_tt("dve", LX[:, :, 0:64], pl[:, :].rearrange("p (c t) -> p c t", t=64), LMSK[dd][:, :].rearrange("p (c t) -> p c t", t=64), ALU.mult,
                              r=pl.all + LMSK[dd].all, w=LX.all)
                    bd_write(L_bd, LX[0:64, :, 0:64], LX[64:128, :, 0:64], LX.all, "pool", "act")
                    if STEP <= 2:
                        continue
                    pw = self.nps()
                    for c in range(8):
                        self.mm(pw[:, c * 64:(c + 1) * 64], abd[:, c, :], H_bf[:, c, :], True, False, r=abd.all + H_bf.all, w=pw.all)
                        self.mm(pw[:, c * 64:(c + 1) * 64], Aak_bd[:, c, :], V_st[:, c, :], False, True, r=Aak_bd.all + V_st.all, w=pw.all)
                    w3 = pw[:, :].rearrange("p (c v) -> p c v", v=64)
                    self.o_cp("act", Xf[:], w3, r=pw.all, w=Xf.all)
                    self.o_cp("dve", LX[:, :, 64:128], w3, r=pw.all, w=LX.all)
                    if STEP <= 3:
                        continue
                    for lev in range(6):
                        last = lev == 5
                        pa = [self.nps(), self.nps()]
                        pbm = self.nps()
                        for c in range(8):
                            cs = slice((c % 4) * 128, (c % 4 + 1) * 128)
                            if last:
                                self.mm(pa[c // 4][:, (c % 4) * 128 + 64:(c % 4 + 1) * 128], M_bd[:, c, :], LX[:, c, 64:128], True, True, r=M_bd.all + LX.all, w=pa[c // 4].all)
                            else:
                                self.mm(pa[c // 4][:, cs], M_bd[:, c, :], LX[:, c, :], True, True, r=M_bd.all + LX.all, w=pa[c // 4].all)
                                self.mm(pbm[:, c * 64:(c + 1) * 64], L_bd[:, c, :], M_st[:, c, :], True, True, r=L_bd.all + M_st.all, w=pbm.all)
                        for hb in range(2):
                            a3 = pa[hb][:, :].rearrange("p (c t) -> p c t", t=128)
                            self.o_tt("dve", Xf[:, hb * 4:(hb + 1) * 4, :], Xf[:, hb * 4:(hb + 1) * 4, :], a3[:, :, 64:128], ALU.add, r=pa[hb].all + Xf.all, w=Xf.all)
                        if not last:
                            for hb in range(2):
                                a3 = pa[hb][:, :].rearrange("p (c t) -> p c t", t=128)
                                if lev < 4:
                                    self.o_cp("act", LX[:, hb * 4:(hb + 1) * 4, 0:64], a3[:, :, 0:64], r=pa[hb].all, w=LX.all)
                                    self.o_cp("dve", L_bd[0:64, hb * 4:(hb + 1) * 4, 0:64], a3[0:64, :, 0:64], r=pa[hb].all, w=L_bd.all)
                                    self.o_cp("act", L_bd[64:128, hb * 4:(hb + 1) * 4, 64:128], a3[64:128, :, 0:64], r=pa[hb].all, w=L_bd.all)
                            m3 = pbm[:, :].rearrange("p (c t) -> p c t", t=64)
                            self.o_cp("act", M_st[:], m3, r=pbm.all, w=M_st.all)
                            bd_write(M_bd, m3[0:64], m3[64:128], pbm.all, "dve", "act")
                        self.o_cp("pool", LX[:, :, 64:128], Xf[:], r=Xf.all, w=LX.all)
                    if STEP <= 4:
                        continue
                    bd_write(U_bd, LX[0:64, :, 64:128], LX[64:128, :, 64:128], LX.all, "pool", "pool")
                    py = self.nps()
                    for c in range(8):
                        cs = slice(c * 64, (c + 1) * 64)
                        self.mm(py[:, cs], H_bd[:, c, :], ar[:, c, 64:128], True, False, r=H_bd.all + ar.all, w=py.all)
                        self.mm(py[:, cs], U_bd[:, c, :], ATB[:, c, 64:128], False, False, r=U_bd.all + ATB.all, w=py.all)
                        self.mm(py[:, cs], V_bd[:, c, :], ATK[:, c, 64:128], False, True, r=V_bd.all + ATK.all, w=py.all)
                    y3 = py[:, :].rearrange("p (c t) -> p c t", t=64)
                    if STEP <= 5:
                        continue
                    ph = self.nps()
                    for c in range(8):
                        cs = slice(c * 64, (c + 1) * 64)
                        self.mm(ph[:, cs], Bt_bd[:, c, :], LX[:, c, 64:128], True, False, r=Bt_bd.all + LX.all, w=ph.all)
                        self.mm(ph[:, cs], Kt_bd[:, c, :], V_st[:, c, :], False, True, r=Kt_bd.all + V_st.all, w=ph.all)
                    h3 = ph[:, :].rearrange("p (c v) -> p c v", v=64)
                    self.o_tt("dve", TMPH[:], h3, H_f[:], ALU.add, r=ph.all + H_f.all, w=TMPH.all)
                    self.o_tt("dve", H_f[:], TMPH[:], pc[:, :].unsqueeze(2).to_broadcast([128, 8, 64]), ALU.mult, r=TMPH.all + pc.all, w=H_f.all)
                    if dd == 0:
                        self.o_cp("act", Yt[:], y3, r=py.all, w=Yt.all)
                        P.dma(S_y[0][:, :, tsl], Yt[:], r=Yt.all, w=[S_y[1]])
                    else:
                        self.o_tt("dve", Yt[:], y3, I_y[buf][:], ALU.add, r=py.all + I_y[buf].all, w=Yt.all)
                        pm = self.nps()
                        self.mm(pm[:, :], BD64[:], Yt[:].rearrange("p c t -> p (c t)"), True, True, r=BD64.all + Yt.all, w=pm.all)
                        self.o_stt(Y2[:], pm[:, :].rearrange("p (c t) -> p c t", t=64), -1.0 / 64, Yt[:], ALU.mult, ALU.add, r=pm.all + Yt.all, w=Y2.all)
                        self.o_tt("pool", Y3[:], Y2[:], Y2[:], ALU.mult, r=Y2.all, w=Y3.all)
                        pv = self.nps()
                        self.mm(pv[:, :], BD64[:], Y3[:].rearrange("p c t -> p (c t)"), True, True, r=BD64.all + Y3.all, w=pv.all)
                        self.o_act(Y3[:].rearrange("p c t -> p (c t)"), pv[:, :], AF.Sqrt, r=pv.all, w=Y3.all, bias=64e-5, scale=1.0 / 64)
                        P.add("dve", lambda e: e.reciprocal(out=Y4[:], in_=Y3[:]), r=Y3.all, w=Y4.all)
                        self.o_tt("dve", Y2[:], Y2[:], Y4[:], ALU.mult, r=Y2.all + Y4.all, w=Y2.all)
                        self.o_tt("pool", Y2[:], Y2[:], LNW[:, :].unsqueeze(2).to_broadcast([128, 8, 64]), ALU.mult, r=Y2.all + LNW.all, w=Y2.all)
                        self.o_tt("dve", Y2[:], Y2[:], LNB[:, :].unsqueeze(2).to_broadcast([128, 8, 64]), ALU.add, r=Y2.all + LNB.all, w=Y2.all)
                        self.o_tt("pool", Y2[:], Y2[:], I_bn[buf][:], ALU.add, r=Y2.all + I_bn[buf].all, w=Y2.all)
                        self.o_tt("dve", Zt[:], Y2[:], I_g[buf][:], ALU.mult, r=Y2.all + I_g[buf].all, w=Zt.all)
                        po = self.nps()
                        for dc in range(8):
                            for c in range(8):
                                self.mm(po[:, dc * 64:(dc + 1) * 64], Wo[:, c, dc * 128:(dc + 1) * 128], Zt[:, c, :], c == 0, c == 7, r=Wo.all + Zt.all, w=po.all)
                        xs = self.X[:, :, tsl]
                        xres = [self.xr(c, ch // 8) for c in range(8)]
                        self.o_tt("dve", xs, po[:, :].rearrange("p (c t) -> p c t", t=64), xs, ALU.add, r=po.all + xres, w=xres)
                    self.o_cp("act", H_bf[:], H_f[:], r=H_f.all, w=H_bf.all)
                    bd_write(H_bd, H_f[0:64, :, :], H_f[64:128, :, :], H_f.all, "pool", "pool")
        P.barrier()


def build(stages, debug=False):
    nc = bass.Bass("TRN2", target_bir_lowering=False)
    dram = {}

    def din(name, shape):
        dram[name] = nc.dram_tensor(name, list(shape), F32, kind="ExternalInput").ap()

    din("x", [S, D])
    for name, shape in PARAM_SHAPES.items():
        din(name, shape)
    for name, shape in CONST_SHAPES.items():
        din(name, shape)
    dram["y"] = nc.dram_tensor("y", [S, D], F32, kind="ExternalOutput").ap()
    with ExitStack() as es:
        P = Prog(nc, es)
        kb = KB(nc, es, P, dram)
        kb.debug = debug
        with ExitStack() as es2:
            kb.load_x(es2)
        P.barrier()
        for st in stages:
            kind, li = st
            if kind == "ffn":
                kb.ffn(li)
            elif kind == "swa":
                kb.swa(li)
            elif kind == "mla":
                kb.mla(li)
            elif kind == "rwkv":
                kb.rwkv(li)
        with ExitStack() as es2:
            kb.store_x(es2)
        P.emit()
        print("prog stats", P.stats)
    return nc


PARAM_SHAPES = {
    "norm_tok": (4, 1024), "norm_ch": (4, 1024), "ffn_w_up": (4, 1024, 5632), "ffn_conv_w": (4, 3, 5632),
    "ffn_conv_b": (4, 5632), "ffn_w_down": (4, 2816, 1024),
    "swa_w_qkv": (2, 1024, 1536), "swa_q_gain": (2, 64), "swa_k_gain": (2, 64), "swa_sinks": (2, 16), "swa_w_o": (2, 1024, 1024),
    "rwkv_mu": (1, 6, 1024), "rwkv_w_r": (1, 1024, 1024), "rwkv_w_k": (1, 1024, 1024), "rwkv_w_v": (1, 1024, 1024),
    "rwkv_w0": (1, 2, 1024), "rwkv_w1": (1, 2, 1024, 64), "rwkv_w2": (1, 2, 64, 1024), "rwkv_a0": (1, 2, 1024),
    "rwkv_a1": (1, 2, 1024, 64), "rwkv_a2": (1, 2, 64, 1024), "rwkv_g1": (1, 1024, 160), "rwkv_g2": (1, 160, 1024),
    "rwkv_k_k": (1, 1024), "rwkv_k_a": (1, 1024), "rwkv_r_k": (1, 16, 64), "rwkv_lnx_w": (1, 1024), "rwkv_lnx_b": (1, 1024),
    "rwkv_w_o": (1, 1024, 1024),
    "mla_w_down": (1, 1024, 672), "mla_cq_gain": (1, 384), "mla_ckv_gain": (1, 256), "mla_w_uq": (1, 384, 1536),
    "mla_w_ukv": (1, 256, 2048), "mla_q_gain": (1, 96), "mla_k_gain": (1, 96), "mla_w_o": (1, 1024, 1024),
}


def make_consts():
    c = {}
    c["c_ident"] = np.eye(128, dtype=np.float32)
    theta = np.float32(500000.0)

    def tables(rot):
        inv = (theta ** (-np.arange(0, rot, 2, dtype=np.float32) / np.float32(rot))).astype(np.float32)
        ang = (np.arange(S, dtype=np.float32)[:, None] * inv[None, :]).astype(np.float32)
        return np.cos(ang).astype(np.float32), np.sin(ang).astype(np.float32)

    def rope_consts(dh, start, rot):
        cs, sn = tables(rot)
        half = rot // 2
        C = np.ones((dh, S), np.float32)
        Sn = np.zeros((dh, S), np.float32)
        Pm = np.zeros((dh, dh), np.float32)
        for i in range(half):
            C[start + i] = cs[:, i]
            C[start + half + i] = cs[:, i]
            Sn[start + i] = sn[:, i]
            Sn[start + half + i] = sn[:, i]
            Pm[start + i, start + half + i] = -1.0
            Pm[start + half + i, start + i] = 1.0
        return C, Sn, np.ascontiguousarray(Pm.T)

    c["c_swa_cos"], c["c_swa_sin"], c["c_swa_rot"] = rope_consts(64, 0, 16)
    c["c_mla_cos"], c["c_mla_sin"], r96 = rope_consts(96, 64, 32)
    rp = np.zeros((128, 128), np.float32)
    rp[:96, :96] = r96
    c["c_mla_rot"] = rp
    o96 = np.zeros((128, 128), np.float32)
    o96[:96, :96] = 1.0
    c["c_ones96"] = o96
    b = np.arange(128)[:, None]
    a = np.arange(128)[None, :]
    c["c_mlo"] = (b <= a).astype(np.float32)
    c["c_mhi"] = (a <= b).astype(np.float32)
    p = np.arange(128)
    c["c_bd64"] = (p[:, None] // 64 == p[None, :] // 64).astype(np.float32)
    same = (p[:, None] // 64 == p[None, :] // 64)
    c["c_tri_f"] = np.concatenate([(same & (p[:, None] <= p[None, :])), (same & (p[:, None] < p[None, :]))], axis=1).astype(np.float32)
    c["c_tri_b"] = np.concatenate([(same & (p[:, None] >= p[None, :])), (same & (p[:, None] > p[None, :]))], axis=1).astype(np.float32)
    s_ = (p % 64)[:, None]
    t_ = np.arange(64)[None, :]
    c["c_mask_f"] = np.tile(np.concatenate([s_ < t_, s_ <= t_], axis=1), (1, 4)).astype(np.float32)
    c["c_mask_b"] = np.tile(np.concatenate([s_ > t_, s_ >= t_], axis=1), (1, 4)).astype(np.float32)
    c["c_lmask_f"] = np.tile(t_ < s_, (1, 8)).astype(np.float32)
    c["c_lmask_b"] = np.tile(t_ > s_, (1, 8)).astype(np.float32)
    c["c_ist"] = (s_ == t_).astype(np.float32)
    ids = np.zeros((32, 96), np.float32)
    ids[np.arange(32), 64 + np.arange(32)] = 1.0
    c["c_mla_ids"] = ids
    return c


CONST_SHAPES = {k: v.shape for k, v in make_consts().items()}

FULL_STAGES = [("swa", 0), ("ffn", 0), ("rwkv", 0), ("ffn", 1), ("mla", 0), ("ffn", 2), ("swa", 1), ("ffn", 3)]


def run(inputs, stages, ncores=8, debug=False):
    nc = build(stages, debug)
    consts = make_consts()
    x = np.ascontiguousarray(inputs["x"], dtype=np.float32)
    in_maps = []
    for b in range(ncores):
        m = {"x": x[b]}
        for name in PARAM_SHAPES:
            m[name] = np.ascontiguousarray(inputs[name], dtype=np.float32)
        m.update(consts)
        in_maps.append(m)
    res = run_bass_kernel_spmd(nc, in_maps, core_ids=list(range(ncores)))
    return np.stack([r["y"] for r in res.results], axis=0)


def kernel(**inputs):
    return run(inputs, FULL_STAGES, 8).astype(np.float32)
```

```python
import numpy as np
import ml_dtypes
from contextlib import ExitStack
import concourse.bass as bass
import concourse.mybir as mybir
from concourse.bass_utils import run_bass_kernel_spmd

F32 = mybir.dt.float32
BF16 = mybir.dt.bfloat16
ALU = mybir.AluOpType
AF = mybir.ActivationFunctionType
AX = mybir.AxisListType

import os
SAME_ENGINE_SYNC = os.environ.get('SES', '0') == '1'
S = 2048
D = 1024
NT = 4
FF = 2816
EPS = 1e-6


class Res:
    __slots__ = ("last_w", "rc", "rd", "excl")

    def __init__(self):
        self.excl = False
        self.last_w = None
        self.rc = {}
        self.rd = []


class T:
    def __init__(self, h, nres=1):
        self.h = h
        self.rs = [Res() for _ in range(nres)]

    def __getitem__(self, idx):
        return self.h[idx]

    @property
    def all(self):
        return list(self.rs)


class Op:
    __slots__ = ("eng", "fn", "deps", "needed", "sig", "is_dma", "idx")


class Prog:
    ENGS = ("pe", "dve", "act", "pool", "sp")

    def __init__(self, nc, es, ndma_sems=8):
        self.nc = nc
        self.es = es
        self.ops = []
        self.ndma = ndma_sems
        self.n_alloc = 0

    def sb(self, shape, dt, nres=1, es=None):
        self.n_alloc += 1
        h = (es or self.es).enter_context(self.nc.sbuf_tensor(f"sb{self.n_alloc}", list(shape), dt))
        return T(h, nres)

    def ps(self, shape, dt=F32, nres=1, es=None):
        self.n_alloc += 1
        h = (es or self.es).enter_context(self.nc.psum_tensor(f"ps{self.n_alloc}", list(shape), dt))
        t = T(h, nres)
        for x in t.rs:
            x.excl = True
        return t

    def add(self, eng, fn, r=(), w=(), is_dma=False):
        op = Op()
        op.eng = eng
        op.fn = fn
        op.is_dma = is_dma
        op.needed = False
        op.sig = None
        op.idx = len(self.ops)
        deps = set()
        ex = [x for x in r if x.excl]
        if ex:
            r = [x for x in r if not x.excl]
            w = list(w) + [x for x in ex if x not in w]
        for x in r:
            if x.last_w is not None:
                deps.add(x.last_w)
        for x in w:
            if x.last_w is not None:
                deps.add(x.last_w)
            deps.update(x.rc.values())
            deps.update(x.rd)
        for x in r:
            if is_dma:
                x.rd.append(op.idx)
            else:
                x.rc[eng] = op.idx
        for x in w:
            x.last_w = op.idx
            x.rc = {}
            x.rd = []
        deps.discard(op.idx)
        op.deps = deps
        self.ops.append(op)
        return op

    def dma(self, out, in_, r=(), w=(), q="sp", **kw):
        return self.add(q, lambda e: e.dma_start(out=out, in_=in_, **kw), r, w, is_dma=True)

    def barrier(self):
        last = {}
        dmas = []
        for op in self.ops:
            if op.is_dma:
                dmas.append(op.idx)
            else:
                last[op.eng] = op.idx
        ids = set(last.values()) | set(dmas[-self.ndma:])
        for e in self.ENGS:
            op = self.add(e, lambda en: en.nop())
            op.deps = set(ids)

    def emit(self):
        nc = self.nc
        ops = self.ops
        for op in ops:
            for d in op.deps:
                ops[d].needed = True
        es = self.es
        EPOCH = 8000
        sems = {}
        dsems = [es.enter_context(nc.semaphore(f"s_dma{i}")) for i in range(self.ndma)]
        cnt = {e: 0 for e in self.ENGS}
        dcnt = [0] * self.ndma
        ndma = 0
        dma_prev = {}
        for op in ops:
            if op.is_dma:
                k = ndma % self.ndma
                ndma += 1
                dma_prev[op.idx] = (k, dcnt[k])
                dcnt[k] += 16
                op.sig = (("d", k), dcnt[k])
            elif op.needed:
                ep = cnt[op.eng] // EPOCH
                cnt[op.eng] += 1
                key = ("e", op.eng, ep)
                if key not in sems:
                    sems[key] = es.enter_context(nc.semaphore(f"s_{op.eng}_{ep}"))
                op.sig = (key, cnt[op.eng] - ep * EPOCH)
        self.stats = dict(cnt=dict(cnt), ndma=ndma, nops=len(ops), nsem=len(sems) + self.ndma)

        def semof(key):
            return dsems[key[1]] if key[0] == "d" else sems[key]

        byeng = {e: [op for op in ops if op.eng == e] for e in self.ENGS}

        def run(ename, eobj):
            known = {}
            for op in byeng[ename]:
                waits = {}
                for d in op.deps:
                    dop = ops[d]
                    if dop.eng == ename and not dop.is_dma and (ename == "pe" or not SAME_ENGINE_SYNC):
                        continue
                    key, v = dop.sig
                    if known.get(key, 0) >= v:
                        continue
                    if waits.get(key, 0) < v:
                        waits[key] = v
                if op.is_dma:
                    k, v = dma_prev[op.idx]
                    key = ("d", k)
                    if v > 0 and known.get(key, 0) < v and waits.get(key, 0) < v:
                        waits[key] = v
                for key, v in waits.items():
                    eobj.wait_ge(semof(key), v)
                    known[key] = v
                ins = op.fn(eobj)
                if op.sig is not None:
                    ins.then_inc(semof(op.sig[0]), 16 if op.is_dma else 1)
            if ename == "sp":
                for k in range(self.ndma):
                    if dcnt[k] > 0:
                        eobj.wait_ge(dsems[k], dcnt[k])

        with nc.Block() as block:
            @block.tensor
            def _(e):
                run("pe", e)

            @block.vector
            def _(e):
                run("dve", e)

            @block.scalar
            def _(e):
                run("act", e)

            @block.gpsimd
            def _(e):
                run("pool", e)

            @block.sync
            def _(e):
                run("sp", e)


class KB:
    def __init__(self, nc, es, P, dram):
        self.nc = nc
        self.es = es
        self.P = P
        self.d = dram
        P_ = P
        self.X = P_.sb([128, 8, S], F32, nres=8 * NT)
        self.ident = P_.sb([128, 128], F32)
        self.ones_bf = P_.sb([128, 128], BF16)
        self.ones_f = P_.sb([128, 128], F32)
        self.ps = [P_.ps([128, 512]) for _ in range(8)]
        self.psi = 0
        P_.dma(self.ident[:], dram["c_ident"], w=self.ident.all)
        P_.add("dve", lambda e: e.memset(self.ones_bf[:], 1.0), w=self.ones_bf.all)
        P_.add("dve", lambda e: e.memset(self.ones_f[:], 1.0), w=self.ones_f.all)
        self.cast_rr = 0

    def xr(self, c, tt):
        return self.X.rs[c * NT + tt]

    def xr_all(self):
        return self.X.all

    def nps(self):
        p = self.ps[self.psi % 6]
        self.psi += 1
        return p

    def mm(self, out, lhsT, rhs, start, stop, r, w):
        self.P.add("pe", lambda e: e.matmul(out, lhsT=lhsT, rhs=rhs, start=start, stop=stop), r=r, w=w)

    def cast(self, out, in_, r, w, eng=None):
        if eng is None:
            eng = ("pool", "act")[self.cast_rr % 2]
            self.cast_rr += 1
        if eng == "act":
            self.P.add("act", lambda e: e.copy(out=out, in_=in_), r=r, w=w)
        elif eng == "pool":
            self.P.add("pool", lambda e: e.tensor_copy(out=out, in_=in_), r=r, w=w)
        else:
            self.P.add("dve", lambda e: e.tensor_copy(out=out, in_=in_), r=r, w=w)

    def load_w(self, dst, dst_sl, src2d, kc, m, stg):
        P = self.P
        per = stg[0].h.shape[1]
        kstep = max(1, per // m)
        i = 0
        for k0 in range(0, kc, kstep):
            k1 = min(kc, k0 + kstep)
            st = stg[self.stg_rr % len(stg)]
            self.stg_rr += 1
            n = (k1 - k0) * m
            sv = st[:, 0:n].rearrange("p (k m) -> p k m", m=m)
            P.dma(sv, src2d[k0 * 128:k1 * 128, :].rearrange("(k p) m -> p k m", p=128), w=st.all)
            self.cast(dst_sl(k0, k1), sv, r=st.all, w=dst.all)
            i += 1

    stg_rr = 0

    def load_x(self, es):
        P = self.P
        xin = self.d["x"]
        tok = [P.sb([128, D], F32, es=es) for _ in range(2)]
        for ti in range(16):
            tk = tok[ti % 2]
            P.dma(tk[:], xin[ti * 128:(ti + 1) * 128, :], w=tk.all)
            for half in range(2):
                ps = self.nps()
                for j in range(4):
                    c = half * 4 + j
                    P.add("pe", lambda e, ps=ps, j=j, c=c, tk=tk: e.transpose(ps[:, j * 128:(j + 1) * 128], tk[:, c * 128:(c + 1) * 128], self.ident[:]),
                          r=tk.all + self.ident.all, w=ps.all)
                tt = ti // 4
                o = self.X[:, half * 4:(half + 1) * 4, ti * 128:(ti + 1) * 128]
                i = ps[:, :].rearrange("p (j t) -> p j t", t=128)
                eng = "dve" if half == 0 else "act"
                if eng == "dve":
                    P.add("dve", lambda e, o=o, i=i: e.tensor_copy(out=o, in_=i), r=ps.all, w=[self.xr(c, tt) for c in range(half * 4, half * 4 + 4)])
                else:
                    P.add("act", lambda e, o=o, i=i: e.copy(out=o, in_=i), r=ps.all, w=[self.xr(c, tt) for c in range(half * 4, half * 4 + 4)])

    def store_x(self, es):
        P = self.P
        yout = self.d["y"]
        tok = [P.sb([128, D], F32, es=es) for _ in range(2)]
        for ti in range(16):
            tk = tok[ti % 2]
            tt = ti // 4
            for half in range(2):
                ps = self.nps()
                for j in range(4):
                    c = half * 4 + j
                    P.add("pe", lambda e, ps=ps, j=j, c=c, ti=ti: e.transpose(ps[:, j * 128:(j + 1) * 128], self.X[:, c, ti * 128:(ti + 1) * 128], self.ident[:]),
                          r=[self.xr(c, tt)] + self.ident.all, w=ps.all)
                o = tk[:, half * 512:(half + 1) * 512]
                if half == 0:
                    P.add("dve", lambda e, o=o, ps=ps: e.tensor_copy(out=o, in_=ps[:, :]), r=ps.all, w=tk.all)
                else:
                    P.add("act", lambda e, o=o, ps=ps: e.copy(out=o, in_=ps[:, :]), r=ps.all, w=tk.all)
            P.dma(yout[ti * 128:(ti + 1) * 128, :], tk[:], r=tk.all)

    def rmsnorm(self, H, gain_ap, es, t0=0, t1=S, hoff=0, rstd_out=None):
        P = self.P
        if not hasattr(es, "_rn_tmp"):
            es._rn_tmp = (P.sb([128, 8, 512], BF16, es=es), P.sb([128, 512], F32, es=es), P.sb([128, 512], F32, es=es))
        sq, rs, rs2 = es._rn_tmp
        for a in range(t0, t1, 512):
            b = min(t1, a + 512)
            n = b - a
            tts = sorted(set([a // 512, (b - 1) // 512]))
            xr = [self.xr(c, tt) for c in range(8) for tt in tts]
            P.add("act", lambda e, a=a, b=b, n=n: e.activation(out=sq[:, :, 0:n], in_=self.X[:, :, a:b], func=AF.Square), r=xr, w=sq.all)
            ps = self.nps()
            for c in range(8):
                self.mm(ps[:, 0:n], self.ones_bf[:], sq[:, c, 0:n], c == 0, c == 7, r=sq.all + self.ones_bf.all, w=ps.all)
            P.add("act", lambda e, n=n, ps=ps: e.activation(out=rs[:, 0:n], in_=ps[:, 0:n], func=AF.Ln, bias=EPS, scale=1.0 / D), r=ps.all, w=rs.all)
            if rstd_out is not None:
                P.add("act", lambda e, n=n, a=a, b=b: e.activation(out=rstd_out[:, a:b], in_=rs[:, 0:n], func=AF.Exp, scale=-0.5), r=rs.all, w=rstd_out.all)
                continue
            P.add("act", lambda e, n=n: e.activation(out=rs2[:, 0:n], in_=rs[:, 0:n], func=AF.Exp, scale=-0.5), r=rs.all, w=rs2.all)
            for c in range(8):
                P.add("dve", lambda e, c=c, a=a, b=b, n=n: e.scalar_tensor_tensor(
                    out=H[:, c, hoff + a - t0:hoff + b - t0], in0=self.X[:, c, a:b], scalar=gain_ap[:, c:c + 1], in1=rs2[:, 0:n],
                    op0=ALU.mult, op1=ALU.mult), r=[self.xr(c, tt) for tt in tts] + rs2.all, w=H.all)

    def load_vec8(self, dst, src1d, q="sp"):
        self.P.dma(dst, src1d.rearrange("(c p) -> p c", p=128), w=[], q=q, allow_slow_non_contiguous=True)

    def ffn(self, li):
        P = self.P
        d = self.d
        with ExitStack() as es:
            gains = P.sb([128, 8], F32, es=es)
            P.dma(gains[:], d["norm_ch"][li].rearrange("(c p) -> p c", p=128), w=gains.all, allow_slow_non_contiguous=True)
            cw = P.sb([128, 3, 44], F32, es=es)
            cb = P.sb([128, 44], F32, es=es)
            for t in range(3):
                P.dma(cw[:, t, :], d["ffn_conv_w"][li, t].rearrange("(c p) -> p c", p=128), w=cw.all, allow_slow_non_contiguous=True)
            P.dma(cb[:], d["ffn_conv_b"][li].rearrange("(c p) -> p c", p=128), w=cb.all, allow_slow_non_contiguous=True)
            H = P.sb([128, 8, 1026], BF16, es=es)
            Hh = P.sb([128, 8, 2], BF16, es=es)
            G = P.sb([128, 22, 1024], BF16, es=es, nres=22)
            U = [P.sb([128, 1026], F32, es=es) for _ in range(2)]
            A = [P.sb([128, 1024], F32, es=es) for _ in range(2)]
            SG = P.sb([128, 1024], F32, es=es)
            wstg = [P.sb([128, 1024], F32, es=es) for _ in range(2)]
            wup = [[P.sb([128, 8, 128], BF16, es=es) for _ in range(2)] for _ in range(2)]
            wdn = [P.sb([128, 22, 128], BF16, es=es) for _ in range(2)]
            wdstg = P.sb([128, 22, 128], F32, es=es)
            w_up = d["ffn_w_up"][li]
            w_dn = d["ffn_w_down"][li]
            for half in range(2):
                hb = half * 1024
                lo = max(0, hb - 1)
                hi = min(S, hb + 1025)
                c0 = lo - (hb - 1)
                ncol = hi - lo
                if half == 0:
                    self.rmsnorm(H, gains, es, lo, hi, hoff=c0)
                    P.add("pool", lambda e: e.tensor_copy(out=Hh[:, :, 0:1], in_=H[:, :, 1024:1025]), r=H.all, w=Hh.all)
                else:
                    self.rmsnorm(H, gains, es, 1024, 2048, hoff=1)
                    P.add("pool", lambda e: e.tensor_copy(out=H[:, :, 0:1], in_=Hh[:, :, 0:1]), r=Hh.all, w=H.all)
                for j in range(22):
                    wb = wup[j % 2]
                    for part in range(2):
                        col0 = part * FF + j * 128
                        st = wstg[part]
                        sv = st[:, :].rearrange("p (k m) -> p k m", m=128)
                        P.dma(sv, w_up[:, col0:col0 + 128].rearrange("(k p) m -> p k m", p=128), w=st.all)
                        self.cast(wb[part][:], sv, r=st.all, w=wb[part].all, eng=("dve", "act")[part])
                    for part in range(2):
                        fc = part * 22 + j
                        u = U[part]
                        if half == 0:
                            P.add("pool", lambda e, u=u: e.memset(u[:, 0:1], 0.0), w=u.all)
                        else:
                            P.add("pool", lambda e, u=u: e.memset(u[:, 1025:1026], 0.0), w=u.all)
                        segs = [(0, 512), (512, 1024), (1024, ncol)]
                        pss = [self.nps() for _ in segs]
                        for k in range(8):
                            for (a, b), ps in zip(segs, pss):
                                self.mm(ps[:, 0:b - a], wb[part][:, k, :], H[:, k, c0 + a:c0 + b], k == 0, k == 7,
                                        r=wb[part].all + H.all, w=ps.all)
                        for (a, b), ps in zip(segs, pss):
                            P.add("act", lambda e, u=u, a=a, b=b, ps=ps, c0=c0: e.copy(out=u[:, c0 + a:c0 + b], in_=ps[:, 0:b - a]), r=ps.all, w=u.all)
                        acc = A[part]
                        P.add("act", lambda e, u=u, acc=acc, fc=fc: e.activation(out=acc[:], in_=u[:, 1:1025], func=AF.Identity,
                                                                                 bias=cb[:, fc:fc + 1], scale=cw[:, 1, fc:fc + 1]),
                              r=u.all + cb.all + cw.all, w=acc.all)
                        P.add("dve", lambda e, u=u, acc=acc, fc=fc: e.scalar_tensor_tensor(out=acc[:], in0=u[:, 0:1024], scalar=cw[:, 0, fc:fc + 1], in1=acc[:],
                                                                                           op0=ALU.mult, op1=ALU.add), r=u.all + cw.all, w=acc.all)
                        P.add("dve", lambda e, u=u, acc=acc, fc=fc: e.scalar_tensor_tensor(out=acc[:], in0=u[:, 2:1026], scalar=cw[:, 2, fc:fc + 1], in1=acc[:],
                                                                                           op0=ALU.mult, op1=ALU.add), r=u.all + cw.all, w=acc.all)
                    P.add("act", lambda e: e.activation(out=SG[:], in_=A[0][:], func=AF.Silu), r=A[0].all, w=SG.all)
                    P.add("pool", lambda e, j=j: e.tensor_tensor(out=G[:, j, :], in0=SG[:], in1=A[1][:], op=ALU.mult), r=SG.all + A[1].all, w=[G.rs[j]])
                for dc in range(8):
                    wd = wdn[dc % 2]
                    P.dma(wdstg[:], w_dn[:, dc * 128:(dc + 1) * 128].rearrange("(k p) m -> p k m", p=128), w=wdstg.all)
                    self.cast(wd[:, 0:11, :], wdstg[:, 0:11, :], r=wdstg.all, w=wd.all, eng="act")
                    self.cast(wd[:, 11:22, :], wdstg[:, 11:22, :], r=wdstg.all, w=wd.all, eng="dve")
                    for t2 in range(2):
                        ps = self.nps()
                        for j in range(22):
                            self.mm(ps[:, :], wd[:, j, :], G[:, j, t2 * 512:(t2 + 1) * 512], j == 0, j == 21, r=wd.all + [G.rs[j]], w=ps.all)
                        tt = half * 2 + t2
                        xs = self.X[:, dc, tt * 512:(tt + 1) * 512]
                        P.add("dve", lambda e, xs=xs, ps=ps: e.tensor_tensor(out=xs, in0=ps[:, :], in1=xs, op=ALU.add), r=ps.all + [self.xr(dc, tt)], w=[self.xr(dc, tt)])
        P.barrier()


    def head_qk(self, *a, **kw):
        for _ in self.head_qk_gen(*a, **kw):
            pass

    def head_qk_gen(self, outT, terms, dh, gain_ap, PrT, Ct, St, wk, kd=None, ones_t=None):
        P = self.P
        raw, sq, rs, rs2, xn, t1, t2 = wk
        kd = kd or dh
        ones_t = ones_t or self.ones_f
        for tt in range(NT):
            sl = slice(tt * 512, (tt + 1) * 512)
            ps = self.nps()
            tl = terms(tt)
            for i, (lt, rh, rd) in enumerate(tl):
                self.mm(ps[0:dh, :], lt, rh, i == 0, i == len(tl) - 1, r=rd, w=ps.all)
            yield
            P.add("act", lambda e, ps=ps: e.copy(out=raw[0:dh, :], in_=ps[0:dh, :]), r=ps.all, w=raw.all)
            P.add("act", lambda e, ps=ps: e.activation(out=sq[0:dh, :], in_=ps[0:dh, :], func=AF.Square), r=ps.all, w=sq.all)
            yield
            ps2 = self.nps()
            self.mm(ps2[0:kd, :], ones_t[0:kd, 0:kd], sq[0:kd, :], True, True, r=sq.all + ones_t.all, w=ps2.all)
            yield
            P.add("act", lambda e, ps2=ps2: e.activation(out=rs[0:dh, :], in_=ps2[0:dh, :], func=AF.Ln, bias=EPS, scale=1.0 / dh), r=ps2.all, w=rs.all)
            P.add("act", lambda e: e.activation(out=rs2[0:dh, :], in_=rs[0:dh, :], func=AF.Exp, scale=-0.5), r=rs.all, w=rs2.all)
            yield
            P.add("dve", lambda e: e.scalar_tensor_tensor(out=xn[0:dh, :], in0=raw[0:dh, :], scalar=gain_ap, in1=rs2[0:dh, :], op0=ALU.mult, op1=ALU.mult),
                  r=raw.all + rs2.all, w=xn.all)
            yield
            ps3 = self.nps()
            self.mm(ps3[0:kd, :], PrT[0:kd, 0:kd], xn[0:kd, :], True, True, r=xn.all + PrT.all, w=ps3.all)
            P.add("pool", lambda e, sl=sl: e.tensor_tensor(out=t1[0:dh, :], in0=xn[0:dh, :], in1=Ct[0:dh, sl], op=ALU.mult), r=xn.all + Ct.all, w=t1.all)
            yield
            P.add("dve", lambda e, sl=sl, ps3=ps3: e.tensor_tensor(out=t2[0:dh, :], in0=ps3[0:dh, :], in1=St[0:dh, sl], op=ALU.mult), r=ps3.all + St.all, w=t2.all)
            yield
            P.add("pool", lambda e, sl=sl: e.tensor_tensor(out=outT[0:dh, sl], in0=t1[0:dh, :], in1=t2[0:dh, :], op=ALU.add), r=t1.all + t2.all, w=outT.all)
            yield

    @staticmethod
    def interleave(main, side, k=1):
        for _ in main:
            for _ in range(k):
                if side is not None and next(side, "END") == "END":
                    side = None
        if side is not None:
            for _ in side:
                pass

    def attn_norm(self, ps_o, ps_sum, out_ap, out_res, sink_ap, wk2):
        P = self.P
        den, bc = wk2
        if sink_ap is not None:
            P.add("act", lambda e: e.activation(out=den[0:64, :], in_=ps_sum[0:64, :], func=AF.Ln, bias=sink_ap), r=ps_sum.all, w=den.all)
        else:
            P.add("act", lambda e: e.activation(out=den[0:64, :], in_=ps_sum[0:64, :], func=AF.Ln), r=ps_sum.all, w=den.all)
        P.add("act", lambda e: e.activation(out=bc[0:64, :], in_=den[0:64, :], func=AF.Exp, scale=-1.0), r=den.all, w=bc.all)
        P.add("dve", lambda e: e.tensor_tensor(out=out_ap, in0=ps_o[0:64, :], in1=bc[0:64, :], op=ALU.mult), r=ps_o.all + bc.all, w=out_res)

    def oproj_accum(self, wo_bf, OTc, nk):
        P = self.P
        for dc in range(8):
            for tt in range(NT):
                ps = self.nps()
                for k in range(nk):
                    self.mm(ps[:, :], wo_bf[:, k, dc * 128:(dc + 1) * 128], OTc[:, k, tt * 512:(tt + 1) * 512], k == 0, k == nk - 1,
                            r=wo_bf.all + OTc.all, w=ps.all)
                xs = self.X[:, dc, tt * 512:(tt + 1) * 512]
                P.add("dve", lambda e, xs=xs, ps=ps: e.tensor_tensor(out=xs, in0=ps[:, :], in1=xs, op=ALU.add), r=ps.all + [self.xr(dc, tt)], w=[self.xr(dc, tt)])

    def dbg(self, ap, slot, npart, ncols):
        if not getattr(self, "debug", False):
            return
        self.P.barrier()
        self.P.add("dve", lambda e: e.tensor_copy(out=self.X[0:npart, slot, 0:ncols], in_=ap), r=[], w=self.X.all)
        self.P.barrier()

    def qk_work(self, es):
        P = self.P
        return [P.sb([128, 512], F32, es=es) for _ in range(7)]

    def swa(self, j):
        P = self.P
        d = self.d
        li = 3 * j
        with ExitStack() as es:
            gains = P.sb([128, 8], F32, es=es)
            P.dma(gains[:], d["norm_tok"][li].rearrange("(c p) -> p c", p=128), w=gains.all, allow_slow_non_contiguous=True)
            qg_t = P.sb([64, 1], F32, es=es)
            kg_t = P.sb([64, 1], F32, es=es)
            P.dma(qg_t[:], d["swa_q_gain"][j].rearrange("(p o) -> p o", o=1), w=qg_t.all, allow_slow_non_contiguous=True)
            P.dma(kg_t[:], d["swa_k_gain"][j].rearrange("(p o) -> p o", o=1), w=kg_t.all, allow_slow_non_contiguous=True)
            sk = P.sb([128, 16], F32, es=es)
            P.dma(sk[:], d["swa_sinks"][j:j + 1, :].to_broadcast([128, 16]), w=sk.all, allow_slow_non_contiguous=True)
            sk0 = sk
            sk = P.sb([128, 16], F32, es=es)
            P.add("act", lambda e: e.activation(out=sk[:], in_=sk0[:], func=AF.Exp), r=sk0.all, w=sk.all)
            Ct = P.sb([64, S], F32, es=es)
            St = P.sb([64, S], F32, es=es)
            PrT = P.sb([64, 64], F32, es=es)
            MLO = P.sb([128, 128], BF16, es=es)
            MHI = P.sb([128, 128], BF16, es=es)
            mstg = P.sb([128, 256], F32, es=es)
            P.dma(Ct[:], d["c_swa_cos"], w=Ct.all)
            P.dma(St[:], d["c_swa_sin"], w=St.all)
            P.dma(PrT[:], d["c_swa_rot"], w=PrT.all)
            P.dma(mstg[:, 0:128], d["c_mlo"], w=mstg.all)
            P.dma(mstg[:, 128:256], d["c_mhi"], w=mstg.all)
            P.add("dve", lambda e: e.tensor_copy(out=MLO[:], in_=mstg[:, 0:128]), r=mstg.all, w=MLO.all)
            P.add("dve", lambda e: e.tensor_copy(out=MHI[:], in_=mstg[:, 128:256]), r=mstg.all, w=MHI.all)
            H = P.sb([128, 8, S], BF16, es=es)
            self.rmsnorm(H, gains, es)
            stg = [P.sb([128, 1024], F32, es=es) for _ in range(2)]
            wqkv = d["swa_w_qkv"][j]
            wo = d["swa_w_o"][j]
            Wkv = P.sb([128, 8, 512], BF16, es=es)
            self.load_w(Wkv, lambda k0, k1: Wkv[:, k0:k1, :], wqkv[:, 1024:1536], 8, 512, stg)
            V = P.sb([128, 16, 4, 65], BF16, es=es)
            P.add("pool", lambda e: e.memset(V[:], 1.0), w=V.all)
            for ti in range(16):
                ps = self.nps()
                for k in range(8):
                    self.mm(ps[:, 0:256], H[:, k, ti * 128:(ti + 1) * 128], Wkv[:, k, 256:512], k == 0, k == 7, r=H.all + Wkv.all, w=ps.all)
                P.add("act", lambda e, ti=ti, ps=ps: e.copy(out=V[:, ti, :, 0:64], in_=ps[:, 0:256].rearrange("p (g v) -> p g v", v=64)), r=ps.all, w=V.all)
            wk = self.qk_work(es)
            den = P.sb([128, 512], F32, es=es)
            bc = P.sb([128, 512], F32, es=es)
            KTs = [P.sb([64, S], BF16, es=es) for _ in range(2)]
            QTs = [P.sb([64, S], BF16, es=es) for _ in range(2)]
            PT = [P.sb([128, 384], BF16, es=es) for _ in range(2)]
            OTc = P.sb([128, 2, S], BF16, es=es)
            Wq = P.sb([128, 8, 256], BF16, es=es)
            Wo = P.sb([128, 2, 1024], BF16, es=es)
            scale = 64 ** -0.5

            def prepK(g):
                return self.head_qk_gen(KTs[g % 2], lambda tt, g=g: [(Wkv[:, k, g * 64:(g + 1) * 64], H[:, k, tt * 512:(tt + 1) * 512], H.all + Wkv.all) for k in range(8)],
                                        64, kg_t[:, 0:1], PrT, Ct, St, wk)

            def prepQ(h):
                hh = h % 4
                return self.head_qk_gen(QTs[h % 2], lambda tt, hh=hh: [(Wq[:, k, hh * 64:(hh + 1) * 64], H[:, k, tt * 512:(tt + 1) * 512], H.all + Wq.all) for k in range(8)],
                                        64, qg_t[:, 0:1], PrT, Ct, St, wk)

            def chain(*gens):
                for g_ in gens:
                    if g_ is not None:
                        yield from g_

            def attn(h):
                g = h // 4
                hh = h % 4
                KT = KTs[g % 2]
                QT = QTs[h % 2]

                def smm(i):
                    js = [jj for jj in (i - 1, i, i + 1) if 0 <= jj < 16]
                    ps_s = self.nps()
                    for n, jj in enumerate(js):
                        self.mm(ps_s[:, n * 128:(n + 1) * 128], KT[:, jj * 128:(jj + 1) * 128], QT[:, i * 128:(i + 1) * 128], True, True,
                                r=KT.all + QT.all, w=ps_s.all)
                    return js, ps_s
                nxt = smm(0)
                for qg in range(4):
                    ps_o = self.ps[6]
                    ps_sum = self.ps[7]
                    for qi in range(4):
                        i = qg * 4 + qi
                        js, ps_s = nxt
                        if i + 1 < 16:
                            nxt = smm(i + 1)
                        pt = PT[i % 2]
                        nn = len(js) * 128
                        P.add("act", lambda e, pt=pt, ps_s=ps_s, nn=nn: e.activation(out=pt[:, 0:nn], in_=ps_s[:, 0:nn], func=AF.Exp, scale=scale), r=ps_s.all, w=pt.all)
                        for n, jj in enumerate(js):
                            if jj == i - 1:
                                P.add("dve", lambda e, pt=pt, n=n: e.tensor_tensor(out=pt[:, n * 128:(n + 1) * 128], in0=pt[:, n * 128:(n + 1) * 128], in1=MHI[:], op=ALU.mult),
                                      r=pt.all + MHI.all, w=pt.all)
                            elif jj == i + 1:
                                P.add("pool", lambda e, pt=pt, n=n: e.tensor_tensor(out=pt[:, n * 128:(n + 1) * 128], in0=pt[:, n * 128:(n + 1) * 128], in1=MLO[:], op=ALU.mult),
                                      r=pt.all + MLO.all, w=pt.all)
                        for n, jj in enumerate(js):
                            self.mm(ps_o[0:64, qi * 128:(qi + 1) * 128], V[:, jj, g, 0:64], pt[:, n * 128:(n + 1) * 128], n == 0, n == len(js) - 1,
                                    r=V.all + pt.all, w=ps_o.all)
                        for n, jj in enumerate(js):
                            self.mm(ps_sum[0:64, qi * 128:(qi + 1) * 128], self.ones_bf[:, 0:64], pt[:, n * 128:(n + 1) * 128], n == 0, n == len(js) - 1,
                                    r=self.ones_bf.all + pt.all, w=ps_sum.all)
                        yield
                    pb = (hh % 2) * 64
                    self.attn_norm(ps_o, ps_sum, OTc[pb:pb + 64, hh // 2, qg * 512:(qg + 1) * 512], OTc.all, sk[0:64, h:h + 1], (den, bc))
                    yield

            self.load_w(Wq, lambda k0, k1: Wq[:, k0:k1, :], wqkv[:, 0:256], 8, 256, stg)
            for _ in chain(prepK(0), prepQ(0)):
                pass
            for h in range(16):
                g = h // 4
                if h % 4 == 0:
                    self.load_w(Wo, lambda k0, k1: Wo[:, k0:k1, :], wo[g * 256:(g + 1) * 256, :], 2, 1024, stg)
                side = None
                if h + 1 < 16:
                    if (h + 1) % 4 == 0:
                        self.load_w(Wq, lambda k0, k1: Wq[:, k0:k1, :], wqkv[:, (g + 1) * 256:(g + 2) * 256], 8, 256, stg)
                        side = chain(prepK(g + 1), prepQ(h + 1))
                    else:
                        side = prepQ(h + 1)
                self.interleave(attn(h), side, k=3)
                if h % 4 == 3:
                    self.oproj_accum(Wo, OTc, 2)
        P.barrier()

    def mla(self, j):
        P = self.P
        d = self.d
        li = 2
        with ExitStack() as es:
            CQ = P.sb([128, 3, S], BF16, es=es)
            CKV = P.sb([128, 2, S], BF16, es=es)
            KR = P.sb([32, S], BF16, es=es)
            stg = [P.sb([128, 2048], F32, es=es) for _ in range(2)]
            with ExitStack() as esA:
                gains = P.sb([128, 8], F32, es=esA)
                P.dma(gains[:], d["norm_tok"][li].rearrange("(c p) -> p c", p=128), w=gains.all, allow_slow_non_contiguous=True)
                cg = P.sb([128, 5], F32, es=esA)
                P.dma(cg[:, 0:3], d["mla_cq_gain"][j].rearrange("(c p) -> p c", p=128), w=cg.all, allow_slow_non_contiguous=True)
                P.dma(cg[:, 3:5], d["mla_ckv_gain"][j].rearrange("(c p) -> p c", p=128), w=cg.all, allow_slow_non_contiguous=True)
                H = P.sb([128, 8, S], BF16, es=esA)
                self.rmsnorm(H, gains, esA)
                Wd = P.sb([128, 8, 672], BF16, es=esA)
                self.load_w(Wd, lambda k0, k1: Wd[:, k0:k1, :], d["mla_w_down"][j], 8, 672, stg)
                raw = [P.sb([128, 512], F32, es=esA) for _ in range(5)]
                sq = [P.sb([128, 512], F32, es=esA) for _ in range(5)]
                rs = P.sb([128, 512], F32, es=esA)
                rs2 = P.sb([128, 512], F32, es=esA)
                for tt in range(NT):
                    sl = slice(tt * 512, (tt + 1) * 512)
                    for c in range(6):
                        m = 128 if c < 5 else 32
                        ps = self.nps()
                        for k in range(8):
                            self.mm(ps[0:m, :], Wd[:, k, c * 128:c * 128 + m], H[:, k, sl], k == 0, k == 7, r=H.all + Wd.all, w=ps.all)
                        if c < 5:
                            P.add("act", lambda e, ps=ps, c=c: e.copy(out=raw[c][:], in_=ps[:, :]), r=ps.all, w=raw[c].all)
                            P.add("act", lambda e, ps=ps, c=c: e.activation(out=sq[c][:], in_=ps[:, :], func=AF.Square), r=ps.all, w=sq[c].all)
                        else:
                            P.add("act", lambda e, ps=ps, sl=sl: e.copy(out=KR[0:32, sl], in_=ps[0:32, :]), r=ps.all, w=KR.all)
                    for (c0, c1, dst, nf) in ((0, 3, CQ, 384), (3, 5, CKV, 256)):
                        ps = self.nps()
                        for c in range(c0, c1):
                            self.mm(ps[:, :], self.ones_f[:, :], sq[c][:], c == c0, c == c1 - 1, r=sq[c].all + self.ones_f.all, w=ps.all)
                        P.add("act", lambda e, ps=ps, nf=nf: e.activation(out=rs[:], in_=ps[:, :], func=AF.Sqrt, bias=EPS, scale=1.0 / nf), r=ps.all, w=rs.all)
                        P.add("dve", lambda e: e.reciprocal(out=rs2[:], in_=rs[:]), r=rs.all, w=rs2.all)
                        for c in range(c0, c1):
                            P.add("dve", lambda e, c=c, c0=c0, dst=dst, sl=sl: e.scalar_tensor_tensor(out=dst[:, c - c0, sl], in0=raw[c][:], scalar=cg[:, c:c + 1], in1=rs2[:],
                                                                                                 op0=ALU.mult, op1=ALU.mult), r=raw[c].all + rs2.all + cg.all, w=dst.all)
            P.barrier()
            qg_t = P.sb([96, 1], F32, es=es)
            kg_t = P.sb([96, 1], F32, es=es)
            P.dma(qg_t[:], d["mla_q_gain"][j].rearrange("(p o) -> p o", o=1), w=qg_t.all, allow_slow_non_contiguous=True)
            P.dma(kg_t[:], d["mla_k_gain"][j].rearrange("(p o) -> p o", o=1), w=kg_t.all, allow_slow_non_contiguous=True)
            Ct = P.sb([96, S], F32, es=es)
            St = P.sb([96, S], F32, es=es)
            PrT = P.sb([128, 128], F32, es=es)
            ones96 = P.sb([128, 128], F32, es=es)
            P.dma(ones96[:], d["c_ones96"], w=ones96.all)
            IdS = P.sb([32, 96], BF16, es=es)
            P.dma(Ct[:], d["c_mla_cos"], w=Ct.all)
            P.dma(St[:], d["c_mla_sin"], w=St.all)
            P.dma(PrT[:], d["c_mla_rot"], w=PrT.all)
            P.dma(stg[0][0:32, 0:96], d["c_mla_ids"], w=stg[0].all)
            P.add("dve", lambda e: e.tensor_copy(out=IdS[:], in_=stg[0][0:32, 0:96]), r=stg[0].all, w=IdS.all)
            Wuq = P.sb([128, 3, 1536], BF16, es=es)
            self.load_w(Wuq, lambda k0, k1: Wuq[:, k0:k1, :], d["mla_w_uq"][j], 3, 1536, stg)
            Wkn = P.sb([128, 2, 16, 96], BF16, es=es)
            Wv = P.sb([128, 2, 16, 64], BF16, es=es)
            P.add("pool", lambda e: e.memset(Wkn[:], 0.0), w=Wkn.all)
            wukv = d["mla_w_ukv"][j]
            for k in range(2):
                st = stg[k % 2]
                P.dma(st[:, 0:2048], wukv[k * 128:(k + 1) * 128, :], w=st.all)
                sv = st[:, 0:2048].rearrange("p (h t) -> p h t", t=128)
                P.add("act", lambda e, k=k, sv=sv: e.copy(out=Wkn[:, k, :, 0:64], in_=sv[:, :, 0:64]), r=st.all, w=Wkn.all)
                P.add("pool", lambda e, k=k, sv=sv: e.tensor_copy(out=Wv[:, k, :, :], in_=sv[:, :, 64:128]), r=st.all, w=Wv.all)
            wk = self.qk_work(es)
            for t_ in wk:
                P.add("pool", lambda e, t_=t_: e.memset(t_[:], 0.0), w=t_.all)
            den = P.sb([128, 512], F32, es=es)
            bc = P.sb([128, 512], F32, es=es)
            KTs = [P.sb([128, S], BF16, es=es) for _ in range(2)]
            QTs = [P.sb([128, S], BF16, es=es) for _ in range(2)]
            Vhs = [P.sb([128, 16, 64], BF16, es=es) for _ in range(2)]
            for t_ in KTs + QTs:
                P.add("pool", lambda e, t_=t_: e.memset(t_[:], 0.0), w=t_.all)
            PT = [P.sb([128, 512], BF16, es=es) for _ in range(3)]
            OTc = P.sb([128, 1, S], BF16, es=es)
            Wo = P.sb([128, 1, 1024], BF16, es=es)
            wo = d["mla_w_o"][j]
            scale = 96 ** -0.5

            def prepV(h):
                Vh = Vhs[h % 2]
                for ti in range(16):
                    ps = self.nps()
                    for k in range(2):
                        self.mm(ps[:, 0:64], CKV[:, k, ti * 128:(ti + 1) * 128], Wv[:, k, h, :], k == 0, k == 1, r=CKV.all + Wv.all, w=ps.all)
                    yield
                    P.add("act", lambda e, ti=ti, ps=ps, Vh=Vh: e.copy(out=Vh[:, ti, :], in_=ps[:, 0:64]), r=ps.all, w=Vh.all)
                    yield

            def prepK(h):
                return self.head_qk_gen(KTs[h % 2], lambda tt, h=h: [(Wkn[:, k, h, :], CKV[:, k, tt * 512:(tt + 1) * 512], CKV.all + Wkn.all) for k in range(2)]
                                        + [(IdS[0:32, :], KR[0:32, tt * 512:(tt + 1) * 512], IdS.all + KR.all)],
                                        96, kg_t[:, 0:1], PrT, Ct, St, wk, kd=128, ones_t=ones96)

            def prepQ(h):
                return self.head_qk_gen(QTs[h % 2], lambda tt, h=h: [(Wuq[:, k, h * 96:(h + 1) * 96], CQ[:, k, tt * 512:(tt + 1) * 512], CQ.all + Wuq.all) for k in range(3)],
                                        96, qg_t[:, 0:1], PrT, Ct, St, wk, kd=128, ones_t=ones96)

            def chain(*gens):
                for g_ in gens:
                    yield from g_

            def attn(h):
                KT, QT, Vh = KTs[h % 2], QTs[h % 2], Vhs[h % 2]
                pti = 0

                def smm(qg, jj):
                    ps_s = self.nps()
                    self.mm(ps_s[:, :], KT[:, jj * 128:(jj + 1) * 128], QT[:, qg * 512:(qg + 1) * 512], True, True, r=KT.all + QT.all, w=ps_s.all)
                    return ps_s
                seq = [(qg, jj) for qg in range(4) for jj in range(16)]
                nxt = smm(*seq[0])
                for n, (qg, jj) in enumerate(seq):
                    ps_o = self.ps[6]
                    ps_sum = self.ps[7]
                    ps_s = nxt
                    if n + 1 < len(seq):
                        nxt = smm(*seq[n + 1])
                    pt = PT[pti % 3]
                    pti += 1
                    P.add("act", lambda e, pt=pt, ps_s=ps_s: e.activation(out=pt[:], in_=ps_s[:, :], func=AF.Exp, scale=scale), r=ps_s.all, w=pt.all)
                    self.mm(ps_o[0:64, :], Vh[:, jj, :], pt[:], jj == 0, jj == 15, r=Vh.all + pt.all, w=ps_o.all)
                    self.mm(ps_sum[0:64, :], self.ones_bf[:, 0:64], pt[:], jj == 0, jj == 15, r=self.ones_bf.all + pt.all, w=ps_sum.all)
                    yield
                    if jj == 15:
                        pb = (h % 2) * 64
                        self.attn_norm(ps_o, ps_sum, OTc[pb:pb + 64, 0, qg * 512:(qg + 1) * 512], OTc.all, None, (den, bc))
                        yield

            for _ in chain(prepV(0), prepK(0), prepQ(0)):
                pass
            for h in range(16):
                if h % 2 == 0:
                    self.load_w(Wo, lambda k0, k1: Wo[:, k0:k1, :], wo[(h // 2) * 128:(h // 2 + 1) * 128, :], 1, 1024, stg)
                side = chain(prepV(h + 1), prepK(h + 1), prepQ(h + 1)) if h + 1 < 16 else None
                self.interleave(attn(h), side, k=2)
                if h % 2 == 1:
                    self.oproj_accum(Wo, OTc, 1)
        P.barrier()

    def o_tt(self, eng, out, in0, in1, op, r, w):
        self.P.add(eng, lambda e: e.tensor_tensor(out=out, in0=in0, in1=in1, op=op), r=r, w=w)

    def o_ts(self, eng, out, in0, s1, s2, op0, op1, r, w):
        if s2 is None:
            self.P.add(eng, lambda e: e.tensor_scalar(out=out, in0=in0, scalar1=s1, scalar2=None, op0=op0), r=r, w=w)
        else:
            self.P.add(eng, lambda e: e.tensor_scalar(out=out, in0=in0, scalar1=s1, scalar2=s2, op0=op0, op1=op1), r=r, w=w)

    def o_stt(self, out, in0, scalar, in1, op0, op1, r, w):
        self.P.add("dve", lambda e: e.scalar_tensor_tensor(out=out, in0=in0, scalar=scalar, in1=in1, op0=op0, op1=op1), r=r, w=w)

    def o_act(self, out, in_, func, r, w, bias=None, scale=None):
        kw = {}
        if bias is not None:
            kw["bias"] = bias
        if scale is not None:
            kw["scale"] = scale
        self.P.add("act", lambda e: e.activation(out=out, in_=in_, func=func, **kw), r=r, w=w)

    def o_cp(self, eng, out, in_, r, w):
        if eng == "act":
            self.P.add("act", lambda e: e.copy(out=out, in_=in_), r=r, w=w)
        else:
            self.P.add(eng, lambda e: e.tensor_copy(out=out, in_=in_), r=r, w=w)

    def rwkv(self, j):
        P = self.P
        d = self.d
        nc = self.nc
        li = 1
        NCH = S // 64
        def scr(name, shape, dt):
            t = nc.dram_tensor(name, list(shape), dt)
            return t.ap(), Res()
        S_ar = [scr(f"rw_ar{dd}", [128, 8, 2, S], BF16) for dd in range(2)]
        S_b = [scr(f"rw_b{dd}", [128, 8, S], BF16) for dd in range(2)]
        S_k = [scr(f"rw_k{dd}", [128, 8, S], BF16) for dd in range(2)]
        S_v = scr("rw_v", [128, 8, S], BF16)
        S_pc = [scr(f"rw_pc{dd}", [NCH, 128, 8], F32) for dd in range(2)]
        S_g = scr("rw_g", [128, 8, S], F32)
        S_bn = scr("rw_bn", [128, 8, S], F32)
        S_y = scr("rw_y", [128, 8, S], F32)

        with ExitStack() as es:
            gains = P.sb([128, 8], F32, es=es)
            P.dma(gains[:], d["norm_tok"][li].rearrange("(c p) -> p c", p=128), w=gains.all, allow_slow_non_contiguous=True)
            RSTD = P.sb([128, S], F32, es=es)
            Wr = P.sb([128, 8, 1024], BF16, es=es)
            Wk = P.sb([128, 8, 1024], BF16, es=es)
            Wv = P.sb([128, 8, 1024], BF16, es=es)
            W1 = P.sb([128, 8, 2, 64], BF16, es=es)
            A1 = P.sb([128, 8, 2, 64], BF16, es=es)
            G1 = P.sb([128, 8, 160], BF16, es=es)
            W2 = P.sb([64, 2, 1024], BF16, es=es)
            A2 = P.sb([64, 2, 1024], BF16, es=es)
            G2a = P.sb([128, 1024], BF16, es=es)
            G2b = P.sb([32, 1024], BF16, es=es)
            W0bc = P.sb([128, 2, 1024], F32, es=es)
            MU = P.sb([128, 6, 8], F32, es=es)
            A0 = P.sb([128, 2, 8], F32, es=es)
            KK_ = P.sb([128, 8], F32, es=es)
            KA_ = P.sb([128, 8], F32, es=es)
            RK_ = P.sb([128, 8], F32, es=es)
            BD64 = P.sb([128, 128], F32, es=es)
            TRI = [P.sb([128, 256], F32, es=es) for _ in range(2)]
            P.dma(MU[:], d["rwkv_mu"][j].rearrange("i (c p) -> p i c", p=128), w=MU.all, allow_slow_non_contiguous=True)
            P.dma(A0[:], d["rwkv_a0"][j].rearrange("i (c p) -> p i c", p=128), w=A0.all, allow_slow_non_contiguous=True)
            P.dma(KK_[:], d["rwkv_k_k"][j].rearrange("(c p) -> p c", p=128), w=KK_.all, allow_slow_non_contiguous=True)
            P.dma(KA_[:], d["rwkv_k_a"][j].rearrange("(c p) -> p c", p=128), w=KA_.all, allow_slow_non_contiguous=True)
            P.dma(RK_[:], d["rwkv_r_k"][j].rearrange("h k -> (h k)").rearrange("(c p) -> p c", p=128), w=RK_.all, allow_slow_non_contiguous=True)
            P.dma(BD64[:], d["c_bd64"], w=BD64.all)
            P.dma(TRI[0][:], d["c_tri_f"], w=TRI[0].all)
            P.dma(TRI[1][:], d["c_tri_b"], w=TRI[1].all)
            for dd in range(2):
                P.dma(W0bc[:, dd, :], d["rwkv_w0"][j, dd:dd + 1, :].to_broadcast([128, 1024]), w=W0bc.all, allow_slow_non_contiguous=True)
            with ExitStack() as esw:
                stg = [P.sb([128, 2048], F32, es=esw) for _ in range(2)]
                self.load_w(Wr, lambda k0, k1: Wr[:, k0:k1, :], d["rwkv_w_r"][j], 8, 1024, stg)
                self.load_w(Wk, lambda k0, k1: Wk[:, k0:k1, :], d["rwkv_w_k"][j], 8, 1024, stg)
                self.load_w(Wv, lambda k0, k1: Wv[:, k0:k1, :], d["rwkv_w_v"][j], 8, 1024, stg)
                for dd in range(2):
                    self.load_w(W1, lambda k0, k1, dd=dd: W1[:, k0:k1, dd, :], d["rwkv_w1"][j, dd], 8, 64, stg)
                    self.load_w(A1, lambda k0, k1, dd=dd: A1[:, k0:k1, dd, :], d["rwkv_a1"][j, dd], 8, 64, stg)
                self.load_w(G1, lambda k0, k1: G1[:, k0:k1, :], d["rwkv_g1"][j], 8, 160, stg)
                for dd in range(2):
                    for (dst, src) in ((W2, d["rwkv_w2"][j, dd]), (A2, d["rwkv_a2"][j, dd])):
                        st = stg[self.stg_rr % 2]
                        self.stg_rr += 1
                        P.dma(st[0:64, 0:1024], src, w=st.all)
                        self.cast(dst[:, dd, :], st[0:64, 0:1024], r=st.all, w=dst.all)
                st = stg[self.stg_rr % 2]
                self.stg_rr += 1
                P.dma(st[:, 0:1024], d["rwkv_g2"][j][0:128, :], w=st.all)
                self.cast(G2a[:], st[:, 0:1024], r=st.all, w=G2a.all)
                st = stg[self.stg_rr % 2]
                self.stg_rr += 1
                P.dma(st[0:32, 0:1024], d["rwkv_g2"][j][128:160, :], w=st.all)
                self.cast(G2b[:], st[0:32, 0:1024], r=st.all, w=G2b.all)
                self.rmsnorm(None, gains, esw, rstd_out=RSTD)
                P.barrier()
            HXf = P.sb([128, 2064], F32, es=es)
            HX = HXf
            Hh_ = HXf[:, 0:1040].rearrange("p (c t) -> p c t", t=130)
            XX_ = HXf[:, 1040:2064].rearrange("p (c t) -> p c t", t=128)
            TMPM = P.sb([128, 8, 128], F32, es=es)
            MIX = [P.sb([128, 8, 128], BF16, es=es) for _ in range(6)]
            O_ar = [P.sb([128, 8, 2, 128], BF16, es=es) for _ in range(2)]
            O_b = [P.sb([128, 8, 128], BF16, es=es) for _ in range(2)]
            O_k = [P.sb([128, 8, 128], BF16, es=es) for _ in range(2)]
            O_v = P.sb([128, 8, 128], BF16, es=es)
            O_pc = [P.sb([128, 2, 8], F32, es=es) for _ in range(2)]
            L1w = [P.sb([64, 128], BF16, es=es) for _ in range(2)]
            L1a = [P.sb([64, 128], BF16, es=es) for _ in range(2)]
            L1g = P.sb([128, 128], BF16, es=es)
            L1g2 = P.sb([32, 128], BF16, es=es)
            sm = [P.sb([128, 128], F32, es=es) for _ in range(22)]
            gq = 0
            (t_r, t_k, t_v, t_kq, t_sq, t_nr, t_kk, t_rr, t_a, t_t, t_kd, t_b, t_ep, t_em, t_epv, t_sb, t_sb2, t_x1, t_x2, t_x3, t_x4, t_x5) = sm
            for ti in range(16):
                t0 = ti * 128
                tt = ti // 4
                lo = max(0, t0 - 1)
                hi = min(S, t0 + 129)
                c0 = lo - (t0 - 1)
                n = hi - lo
                xrs = [self.xr(c, q) for c in range(8) for q in sorted(set([lo // 512, (hi - 1) // 512]))]
                if t0 == 0:
                    P.add("pool", lambda e: e.memset(Hh_[:, :, 0:1], 0.0), w=HX.all)
                if t0 + 129 > S:
                    P.add("pool", lambda e: e.memset(Hh_[:, :, 129:130], 0.0), w=HX.all)
                for c in range(8):
                    self.o_stt(Hh_[:, c, c0:c0 + n], self.X[:, c, lo:hi], gains[:, c:c + 1], RSTD[:, lo:hi], ALU.mult, ALU.mult, r=xrs + RSTD.all, w=HX.all)
                hc = Hh_[:, :, 1:129]
                xx = XX_
                self.o_tt("pool", TMPM[:], Hh_[:, :, 0:128], Hh_[:, :, 2:130], ALU.add, r=HX.all, w=TMPM.all)
                self.o_stt(xx, TMPM[:], 0.5, hc, ALU.mult, ALU.subtract, r=TMPM.all + HX.all, w=HX.all)
                for i in range(6):
                    self.o_tt("dve", TMPM[:], xx, MU[:, i, :].unsqueeze(2).to_broadcast([128, 8, 128]), ALU.mult, r=HX.all + MU.all, w=TMPM.all)
                    self.o_tt("pool", MIX[i][:], TMPM[:], hc, ALU.add, r=TMPM.all + HX.all, w=MIX[i].all)
                m_r, m_w, m_k, m_v, m_a, m_g = MIX
                for dd in range(2):
                    ps = self.nps()
                    for k in range(8):
                        self.mm(ps[0:64, 0:128], W1[:, k, dd, :], m_w[:, k, :], k == 0, k == 7, r=W1.all + m_w.all, w=ps.all)
                    self.o_act(L1w[dd][:], ps[0:64, 0:128], AF.Tanh, r=ps.all, w=L1w[dd].all)
                    ps = self.nps()
                    for k in range(8):
                        self.mm(ps[0:64, 0:128], A1[:, k, dd, :], m_a[:, k, :], k == 0, k == 7, r=A1.all + m_a.all, w=ps.all)
                    self.o_cp("act", L1a[dd][:], ps[0:64, 0:128], r=ps.all, w=L1a[dd].all)
                ps = self.nps()
                for k in range(8):
                    self.mm(ps[:, 0:128], G1[:, k, 0:128], m_g[:, k, :], k == 0, k == 7, r=G1.all + m_g.all, w=ps.all)
                self.o_act(L1g[:], ps[:, 0:128], AF.Sigmoid, r=ps.all, w=L1g.all)
                ps = self.nps()
                for k in range(8):
                    self.mm(ps[0:32, 0:128], G1[:, k, 128:160], m_g[:, k, :], k == 0, k == 7, r=G1.all + m_g.all, w=ps.all)
                self.o_act(L1g2[:], ps[0:32, 0:128], AF.Sigmoid, r=ps.all, w=L1g2.all)
                LW = HXf[:, 0:2048]
                for dd in range(2):
                    for hf in range(2):
                        ps = self.nps()
                        self.mm(ps[:, :], L1w[dd][:], W2[:, dd, hf * 512:(hf + 1) * 512], True, True, r=L1w[dd].all + W2.all, w=ps.all)
                        sl = slice(dd * 1024 + hf * 512, dd * 1024 + (hf + 1) * 512)
                        self.o_tt("dve", LW[:, sl], ps[:, :], W0bc[:, dd, hf * 512:(hf + 1) * 512], ALU.add, r=ps.all + W0bc.all, w=HX.all)
                    sl = slice(dd * 1024, (dd + 1) * 1024)
                    self.o_act(LW[:, sl], LW[:, sl], AF.Sigmoid, r=HX.all, w=HX.all)
                    self.o_ts("pool", LW[:, sl], LW[:, sl], -0.6065306597126334, None, ALU.mult, None, r=HX.all, w=HX.all)
                for oc in range(8):
                    fs = slice(oc * 128, (oc + 1) * 128)
                    for (wt, mx, dst) in ((Wr, m_r, t_r), (Wk, m_k, t_k), (Wv, m_v, t_v)):
                        ps = self.nps()
                        for k in range(8):
                            self.mm(ps[:, 0:128], wt[:, k, fs], mx[:, k, :], k == 0, k == 7, r=wt.all + mx.all, w=ps.all)
                        self.o_cp("act", dst[:], ps[:, 0:128], r=ps.all, w=dst.all)
                    self.o_cp("pool", O_v[:, oc, :], t_v[:], r=t_v.all, w=O_v.all)
                    self.o_ts("dve", t_kq[:], t_k[:], KK_[:, oc:oc + 1], None, ALU.mult, None, r=t_k.all + KK_.all, w=t_kq.all)
                    self.o_tt("pool", t_sq[:], t_kq[:], t_kq[:], ALU.mult, r=t_kq.all, w=t_sq.all)
                    ps = self.nps()
                    self.mm(ps[:, 0:128], BD64[:], t_sq[:], True, True, r=BD64.all + t_sq.all, w=ps.all)
                    self.o_act(t_nr[:], ps[:, 0:128], AF.Sqrt, r=ps.all, w=t_nr.all)
                    self.o_ts("dve", t_nr[:], t_nr[:], 1e-12, None, ALU.max, None, r=t_nr.all, w=t_nr.all)
                    P.add("dve", lambda e: e.reciprocal(out=t_sq[:], in_=t_nr[:]), r=t_nr.all, w=t_sq.all)
                    self.o_tt("dve", t_kk[:], t_kq[:], t_sq[:], ALU.mult, r=t_kq.all + t_sq.all, w=t_kk.all)
                    self.o_ts("pool", t_rr[:], t_r[:], RK_[:, oc:oc + 1], None, ALU.mult, None, r=t_r.all + RK_.all, w=t_rr.all)
                    ps = self.nps()
                    self.mm(ps[:, 0:128], G2a[:, fs], L1g[:], True, False, r=G2a.all + L1g.all, w=ps.all)
                    self.mm(ps[:, 0:128], G2b[:, fs], L1g2[:], False, True, r=G2b.all + L1g2.all, w=ps.all)
                    tg = (t_x1, t_x2)[oc % 2]
                    self.o_cp("act", tg[:], ps[:, 0:128], r=ps.all, w=tg.all)
                    P.dma(S_g[0][:, oc, t0:t0 + 128], tg[:], r=tg.all, w=[S_g[1]])
                    for dd in range(2):
                        ps = self.nps()
                        self.mm(ps[:, 0:128], A2[:, dd, fs], L1a[dd][:], True, True, r=A2.all + L1a[dd].all, w=ps.all)
                        self.o_act(t_a[:], ps[:, 0:128], AF.Sigmoid, r=ps.all + A0.all, w=t_a.all, bias=A0[:, dd, oc:oc + 1])
                        self.o_ts("dve", t_t[:], t_a[:], 1.0, KA_[:, oc:oc + 1], ALU.subtract, ALU.mult, r=t_a.all + KA_.all, w=t_t.all)
                        self.o_stt(t_kd[:], t_t[:], 1.0, t_k[:], ALU.add, ALU.mult, r=t_t.all + t_k.all, w=t_kd.all)
                        self.o_tt("pool", t_b[:], t_kk[:], t_a[:], ALU.mult, r=t_kk.all + t_a.all, w=t_b.all)
                        ps = self.nps()
                        self.mm(ps[:, 0:256], LW[:, dd * 1024 + oc * 128:dd * 1024 + (oc + 1) * 128], TRI[dd][:], True, True, r=HX.all + TRI[dd].all, w=ps.all)
                        self.o_act(t_ep[:], ps[:, 0:128], AF.Exp, r=ps.all, w=t_ep.all)
                        self.o_act(t_em[:], ps[:, 0:128], AF.Exp, r=ps.all, w=t_em.all, scale=-1.0)
                        self.o_act(t_epv[:], ps[:, 128:256], AF.Exp, r=ps.all, w=t_epv.all)
                        for cc in range(2):
                            col = cc * 64 + (63 if dd == 0 else 0)
                            self.o_cp("pool", O_pc[dd][:, cc, oc:oc + 1], t_ep[:, col:col + 1], r=t_ep.all, w=O_pc[dd].all)
                        self.o_stt(O_ar[dd][:, oc, 0, :], t_kk[:], -1.0, t_epv[:], ALU.mult, ALU.mult, r=t_kk.all + t_epv.all, w=O_ar[dd].all)
                        self.o_tt("pool", O_ar[dd][:, oc, 1, :], t_r[:], t_ep[:], ALU.mult, r=t_r.all + t_ep.all, w=O_ar[dd].all)
                        self.o_tt("dve", O_b[dd][:, oc, :], t_b[:], t_em[:], ALU.mult, r=t_b.all + t_em.all, w=O_b[dd].all)
                        self.o_tt("pool", O_k[dd][:, oc, :], t_kd[:], t_em[:], ALU.mult, r=t_kd.all + t_em.all, w=O_k[dd].all)
                        if dd == 0:
                            self.o_tt("dve", t_sb[:], t_rr[:], t_kd[:], ALU.mult, r=t_rr.all + t_kd.all, w=t_sb.all)
                        else:
                            self.o_tt("dve", t_sb2[:], t_rr[:], t_kd[:], ALU.mult, r=t_rr.all + t_kd.all, w=t_sb2.all)
                            self.o_tt("pool", t_sb[:], t_sb[:], t_sb2[:], ALU.add, r=t_sb.all + t_sb2.all, w=t_sb.all)
                    ps = self.nps()
                    self.mm(ps[:, 0:128], BD64[:], t_sb[:], True, True, r=BD64.all + t_sb.all, w=ps.all)
                    tb = (t_x3, t_x4)[oc % 2]
                    self.o_tt("dve", tb[:], ps[:, 0:128], t_v[:], ALU.mult, r=ps.all + t_v.all, w=tb.all)
                    P.dma(S_bn[0][:, oc, t0:t0 + 128], tb[:], r=tb.all, w=[S_bn[1]])
                ts_ = slice(t0, t0 + 128)
                for dd in range(2):
                    P.dma(S_ar[dd][0][:, :, :, ts_], O_ar[dd][:], r=O_ar[dd].all, w=[S_ar[dd][1]])
                    P.dma(S_b[dd][0][:, :, ts_], O_b[dd][:], r=O_b[dd].all, w=[S_b[dd][1]])
                    P.dma(S_k[dd][0][:, :, ts_], O_k[dd][:], r=O_k[dd].all, w=[S_k[dd][1]])
                    for cc in range(2):
                        P.dma(S_pc[dd][0][2 * ti + cc], O_pc[dd][:, cc, :], r=O_pc[dd].all, w=[S_pc[dd][1]])
                P.dma(S_v[0][:, :, ts_], O_v[:], r=O_v.all, w=[S_v[1]])
        P.barrier()
        import os
        if os.environ.get("RWKV_STOP") == "A":
            return

        with ExitStack() as es:
            IST = P.sb([128, 64], BF16, es=es)
            MSK = [P.sb([128, 512], BF16, es=es) for _ in range(2)]
            LMSK = [P.sb([128, 512], BF16, es=es) for _ in range(2)]
            BD64 = P.sb([128, 128], F32, es=es)
            LNW = P.sb([128, 8], F32, es=es)
            LNB = P.sb([128, 8], F32, es=es)
            Wo = P.sb([128, 8, 1024], BF16, es=es)
            P.dma(BD64[:], d["c_bd64"], w=BD64.all)
            P.dma(LNW[:], d["rwkv_lnx_w"][j].rearrange("(c p) -> p c", p=128), w=LNW.all, allow_slow_non_contiguous=True)
            P.dma(LNB[:], d["rwkv_lnx_b"][j].rearrange("(c p) -> p c", p=128), w=LNB.all, allow_slow_non_contiguous=True)
            with ExitStack() as esw:
                stg = [P.sb([128, 2048], F32, es=esw) for _ in range(2)]
                self.load_w(Wo, lambda k0, k1: Wo[:, k0:k1, :], d["rwkv_w_o"][j], 8, 1024, stg)
                P.dma(stg[0][:, 0:64], d["c_ist"], w=stg[0].all)
                self.o_cp("dve", IST[:], stg[0][:, 0:64], r=stg[0].all, w=IST.all)
                for dd, (mk, lk) in enumerate((("c_mask_f", "c_lmask_f"), ("c_mask_b", "c_lmask_b"))):
                    P.dma(stg[1][:, 0:512], d[mk], w=stg[1].all)
                    self.o_cp("dve", MSK[dd][:], stg[1][:, 0:512], r=stg[1].all, w=MSK[dd].all)
                    P.dma(stg[1][:, 512:1024], d[lk], w=stg[1].all)
                    self.o_cp("dve", LMSK[dd][:], stg[1][:, 512:1024], r=stg[1].all, w=LMSK[dd].all)
                P.barrier()
            def bdtile():
                t = P.sb([128, 8, 128], BF16, es=es)
                P.add("pool", lambda e: e.memset(t[:], 0.0), w=t.all)
                return t
            I_ar = [P.sb([128, 8, 128], BF16, es=es) for _ in range(2)]
            I_abd = [bdtile() for _ in range(2)]
            I_bbd = [bdtile() for _ in range(2)]
            I_kbd = [bdtile() for _ in range(2)]
            I_vbd = [bdtile() for _ in range(2)]
            I_b = [P.sb([128, 8, 64], BF16, es=es) for _ in range(2)]
            I_pc = [P.sb([128, 8], F32, es=es) for _ in range(2)]
            I_y = [P.sb([128, 8, 64], F32, es=es) for _ in range(2)]
            I_g = [P.sb([128, 8, 64], F32, es=es) for _ in range(2)]
            I_bn = [P.sb([128, 8, 64], F32, es=es) for _ in range(2)]
            V_st = P.sb([128, 8, 64], BF16, es=es)
            V_bd = bdtile()
            Bt_bd = bdtile()
            Kt_bd = bdtile()
            ATB = P.sb([128, 8, 128], BF16, es=es)
            ATK = P.sb([128, 8, 128], BF16, es=es)
            M_bd = bdtile()
            Aak_bd = bdtile()
            L_bd = bdtile()
            LX = P.sb([128, 8, 128], BF16, es=es)
            M_st = P.sb([128, 8, 64], BF16, es=es)
            Xf = P.sb([128, 8, 64], F32, es=es)
            U_bd = bdtile()
            H_f = P.sb([128, 8, 64], F32, es=es)
            H_bf = P.sb([128, 8, 64], BF16, es=es)
            H_bd = bdtile()
            TMPH = P.sb([128, 8, 64], F32, es=es)
            Yt = P.sb([128, 8, 64], F32, es=es)
            Y2 = P.sb([128, 8, 64], F32, es=es)
            Y3 = P.sb([128, 8, 64], F32, es=es)
            Y4 = P.sb([128, 8, 64], F32, es=es)
            Zt = P.sb([128, 8, 64], BF16, es=es)

            def bd_write(dst, src_lo, src_hi, r, eng0="dve", eng1="pool"):
                self.o_cp(eng0, dst[0:64, :, 0:64], src_lo, r=r, w=dst.all)
                self.o_cp(eng1, dst[64:128, :, 64:128], src_hi, r=r, w=dst.all)

            def load_chunk(dd, ch, buf):
                ts_ = slice(ch * 64, (ch + 1) * 64)
                P.dma(I_ar[buf][:].rearrange("p c (e t) -> p c e t", e=2), S_ar[dd][0][:, :, :, ts_], r=[S_ar[dd][1]], w=I_ar[buf].all)
                for e_ in range(2):
                    ps_ = slice(e_ * 64, (e_ + 1) * 64)
                    fs_ = slice(e_ * 64, (e_ + 1) * 64)
                    P.dma(I_abd[buf][ps_, :, fs_], S_ar[dd][0][ps_, :, 0, ts_], r=[S_ar[dd][1]], w=I_abd[buf].all)
                    P.dma(I_bbd[buf][ps_, :, fs_], S_b[dd][0][ps_, :, ts_], r=[S_b[dd][1]], w=I_bbd[buf].all)
                    P.dma(I_kbd[buf][ps_, :, fs_], S_k[dd][0][ps_, :, ts_], r=[S_k[dd][1]], w=I_kbd[buf].all)
                    P.dma(I_vbd[buf][ps_, :, fs_], S_v[0][ps_, :, ts_], r=[S_v[1]], w=I_vbd[buf].all)
                P.dma(I_b[buf][:], S_b[dd][0][:, :, ts_], r=[S_b[dd][1]], w=I_b[buf].all)
                P.dma(I_pc[buf][:], S_pc[dd][0][ch], r=[S_pc[dd][1]], w=I_pc[buf].all)
                if dd == 1:
                    P.dma(I_y[buf][:], S_y[0][:, :, ts_], r=[S_y[1]], w=I_y[buf].all)
                    P.dma(I_g[buf][:], S_g[0][:, :, ts_], r=[S_g[1]], w=I_g[buf].all)
                    P.dma(I_bn[buf][:], S_bn[0][:, :, ts_], r=[S_bn[1]], w=I_bn[buf].all)

            def bank(i):
                return i

            STEP = int(os.environ.get("RWKV_STEP", "99"))
            ndirs = int(os.environ.get("RWKV_DIRS", "2"))
            nlim = int(os.environ.get("RWKV_NCH", str(NCH)))
            for dd in range(ndirs):
                order = (list(range(NCH)) if dd == 0 else list(range(NCH - 1, -1, -1)))[:nlim]
                P.add("pool", lambda e: e.memset(H_f[:], 0.0), w=H_f.all)
                P.add("pool", lambda e: e.memset(H_bf[:], 0.0), w=H_bf.all)
                P.add("pool", lambda e: e.memset(H_bd[:], 0.0), w=H_bd.all)
                if os.environ.get('RWKV_NOLOAD') != '1':
                    load_chunk(dd, (order + [0])[0], 0)
                for oi, ch in enumerate(order):
                    buf = oi % 2
                    if oi + 1 < len(order):
                        load_chunk(dd, order[oi + 1], 1 - buf)
                    ar, abd, bbd, kbd, vbd, bst, pc = I_ar[buf], I_abd[buf], I_bbd[buf], I_kbd[buf], I_vbd[buf], I_b[buf], I_pc[buf]
                    tsl = slice(ch * 64, (ch + 1) * 64)
                    psv, psb, psk = self.nps(), self.nps(), self.nps()
                    for c in range(8):
                        self.mm(psv[:, c * 64:(c + 1) * 64], vbd[:, c, :], IST[:], True, True, r=vbd.all + IST.all, w=psv.all)
                        self.mm(psb[:, c * 64:(c + 1) * 64], bbd[:, c, :], IST[:], True, True, r=bbd.all + IST.all, w=psb.all)
                        self.mm(psk[:, c * 64:(c + 1) * 64], kbd[:, c, :], IST[:], True, True, r=kbd.all + IST.all, w=psk.all)
                    v3 = psv[:, :].rearrange("p (c v) -> p c v", v=64)
                    self.o_cp("act", V_st[:], v3, r=psv.all, w=V_st.all)
                    TF = int(os.environ.get("RWKV_T", "9"))
                    if TF <= 1:
                        continue
                    self.o_cp("dve", V_bd[0:64, :, 0:64], v3[0:64], r=psv.all, w=V_bd.all)
                    if TF <= 2:
                        continue
                    self.o_cp("act", V_bd[64:128, :, 64:128], v3[64:128], r=psv.all, w=V_bd.all)
                    if TF <= 3:
                        continue
                    b3 = psb[:, :].rearrange("p (c v) -> p c v", v=64)
                    bd_write(Bt_bd, b3[0:64], b3[64:128], psb.all, "dve", "act")
                    k3 = psk[:, :].rearrange("p (c v) -> p c v", v=64)
                    bd_write(Kt_bd, k3[0:64], k3[64:128], psk.all, "dve", "act")
                    if STEP <= 1:
                        continue
                    pb = [self.nps(), self.nps()]
                    pk = [self.nps(), self.nps()]
                    pl = self.nps()
                    for c in range(8):
                        cs = slice((c % 4) * 128, (c % 4 + 1) * 128)
                        self.mm(pb[c // 4][:, cs], bbd[:, c, :], ar[:, c, :], True, True, r=bbd.all + ar.all, w=pb[c // 4].all)
                        self.mm(pk[c // 4][:, cs], kbd[:, c, :], ar[:, c, :], True, True, r=kbd.all + ar.all, w=pk[c // 4].all)
                        self.mm(pl[:, c * 64:(c + 1) * 64], abd[:, c, :], bst[:, c, :], True, True, r=abd.all + bst.all, w=pl.all)
                    for hb in range(2):
                        self.o_tt("dve", ATB[:, hb * 4:(hb + 1) * 4, :], pb[hb][:, :].rearrange("p (c t) -> p c t", t=128), MSK[dd][:, :].rearrange("p (c t) -> p c t", t=128),
                                  ALU.mult, r=pb[hb].all + MSK[dd].all, w=ATB.all)
                        self.o_tt("dve", ATK[:, hb * 4:(hb + 1) * 4, :], pk[hb][:, :].rearrange("p (c t) -> p c t", t=128), MSK[dd][:, :].rearrange("p (c t) -> p c t", t=128),
                                  ALU.mult, r=pk[hb].all + MSK[dd].all, w=ATK.all)
                    bd_write(M_bd, ATB[0:64, :, 0:64], ATB[64:128, :, 0:64], ATB.all, "pool", "pool")
                    bd_write(Aak_bd, ATK[0:64, :, 0:64], ATK[64:128, :, 0:64], ATK.all, "pool", "pool")
                    self.o_cp("act", M_st[:], ATB[:, :, 0:64], r=ATB.all, w=M_st.all)
                    self.o_tt("dve", LX[:, :, 0:64], pl[:, :].rearrange("p (c t) -> p c t", t=64), LMSK[dd][:, :].rearrange("p (c t) -> p c t", t=64), ALU.mult,
                              r=pl.all + LMSK[dd].all, w=LX.all)
                    bd_write(L_bd, LX[0:64, :, 0:64], LX[64:128, :, 0:64], LX.all, "pool", "act")
                    if STEP <= 2:
                        continue
                    pw = self.nps()
                    for c in range(8):
                        self.mm(pw[:, c * 64:(c + 1) * 64], abd[:, c, :], H_bf[:, c, :], True, False, r=abd.all + H_bf.all, w=pw.all)
                        self.mm(pw[:, c * 64:(c + 1) * 64], Aak_bd[:, c, :], V_st[:, c, :], False, True, r=Aak_bd.all + V_st.all, w=pw.all)
                    w3 = pw[:, :].rearrange("p (c v) -> p c v", v=64)
                    self.o_cp("act", Xf[:], w3, r=pw.all, w=Xf.all)
                    self.o_cp("dve", LX[:, :, 64:128], w3, r=pw.all, w=LX.all)
                    if STEP <= 3:
                        continue
                    for lev in range(6):
                        last = lev == 5
                        pa = [self.nps(), self.nps()]
                        pbm = self.nps()
                        for c in range(8):
                            cs = slice((c % 4) * 128, (c % 4 + 1) * 128)
                            if last:
                                self.mm(pa[c // 4][:, (c % 4) * 128 + 64:(c % 4 + 1) * 128], M_bd[:, c, :], LX[:, c, 64:128], True, True, r=M_bd.all + LX.all, w=pa[c // 4].all)
                            else:
                                self.mm(pa[c // 4][:, cs], M_bd[:, c, :], LX[:, c, :], True, True, r=M_bd.all + LX.all, w=pa[c // 4].all)
                                self.mm(pbm[:, c * 64:(c + 1) * 64], L_bd[:, c, :], M_st[:, c, :], True, True, r=L_bd.all + M_st.all, w=pbm.all)
                        for hb in range(2):
                            a3 = pa[hb][:, :].rearrange("p (c t) -> p c t", t=128)
                            self.o_tt("dve", Xf[:, hb * 4:(hb + 1) * 4, :], Xf[:, hb * 4:(hb + 1) * 4, :], a3[:, :, 64:128], ALU.add, r=pa[hb].all + Xf.all, w=Xf.all)
                        if not last:
                            for hb in range(2):
                                a3 = pa[hb][:, :].rearrange("p (c t) -> p c t", t=128)
                                if lev < 4:
                                    self.o_cp("act", LX[:, hb * 4:(hb + 1) * 4, 0:64], a3[:, :, 0:64], r=pa[hb].all, w=LX.all)
                                    self.o_cp("dve", L_bd[0:64, hb * 4:(hb + 1) * 4, 0:64], a3[0:64, :, 0:64], r=pa[hb].all, w=L_bd.all)
                                    self.o_cp("act", L_bd[64:128, hb * 4:(hb + 1) * 4, 64:128], a3[64:128, :, 0:64], r=pa[hb].all, w=L_bd.all)
                            m3 = pbm[:, :].rearrange("p (c t) -> p c t", t=64)
                            self.o_cp("act", M_st[:], m3, r=pbm.all, w=M_st.all)
                            bd_write(M_bd, m3[0:64], m3[64:128], pbm.all, "dve", "act")
                        self.o_cp("pool", LX[:, :, 64:128], Xf[:], r=Xf.all, w=LX.all)
                    if STEP <= 4:
                        continue
                    bd_write(U_bd, LX[0:64, :, 64:128], LX[64:128, :, 64:128], LX.all, "pool", "pool")
                    py = self.nps()
                    for c in range(8):
                        cs = slice(c * 64, (c + 1) * 64)
                        self.mm(py[:, cs], H_bd[:, c, :], ar[:, c, 64:128], True, False, r=H_bd.all + ar.all, w=py.all)
                        self.mm(py[:, cs], U_bd[:, c, :], ATB[:, c, 64:128], False, False, r=U_bd.all + ATB.all, w=py.all)
                        self.mm(py[:, cs], V_bd[:, c, :], ATK[:, c, 64:128], False, True, r=V_bd.all + ATK.all, w=py.all)
                    y3 = py[:, :].rearrange("p (c t) -> p c t", t=64)
                    if STEP <= 5:
                        continue
                    ph = self.nps()
                    for c in range(8):
                        cs = slice(c * 64, (c + 1) * 64)
                        self.mm(ph[:, cs], Bt_bd[:, c, :], LX[:, c, 64:128], True, False, r=Bt_bd.all + LX.all, w=ph.all)
                        self.mm(ph[:, cs], Kt_bd[:, c, :], V_st[:, c, :], False, True, r=Kt_bd.all + V_st.all, w=ph.all)
                    h3 = ph[:, :].rearrange("p (c v) -> p c v", v=64)
                    self.o_tt("dve", TMPH[:], h3, H_f[:], ALU.add, r=ph.all + H_f.all, w=TMPH.all)
                    self.o_tt("dve", H_f[:], TMPH[:], pc[:, :].unsqueeze(2).to_broadcast([128, 8, 64]), ALU.mult, r=TMPH.all + pc.all, w=H_f.all)
                    if dd == 0:
                        self.o_cp("act", Yt[:], y3, r=py.all, w=Yt.all)
                        P.dma(S_y[0][:, :, tsl], Yt[:], r=Yt.all, w=[S_y[1]])
                    else:
                        self.o_tt("dve", Yt[:], y3, I_y[buf][:], ALU.add, r=py.all + I_y[buf].all, w=Yt.all)
                        pm = self.nps()
                        self.mm(pm[:, :], BD64[:], Yt[:].rearrange("p c t -> p (c t)"), True, True, r=BD64.all + Yt.all, w=pm.all)
                        self.o_stt(Y2[:], pm[:, :].rearrange("p (c t) -> p c t", t=64), -1.0 / 64, Yt[:], ALU.mult, ALU.add, r=pm.all + Yt.all, w=Y2.all)
                        self.o_tt("pool", Y3[:], Y2[:], Y2[:], ALU.mult, r=Y2.all, w=Y3.all)
                        pv = self.nps()
                        self.mm(pv[:, :], BD64[:], Y3[:].rearrange("p c t -> p (c t)"), True, True, r=BD64.all + Y3.all, w=pv.all)
                        self.o_act(Y3[:].rearrange("p c t -> p (c t)"), pv[:, :], AF.Sqrt, r=pv.all, w=Y3.all, bias=64e-5, scale=1.0 / 64)
                        P.add("dve", lambda e: e.reciprocal(out=Y4[:], in_=Y3[:]), r=Y3.all, w=Y4.all)
                        self.o_tt("dve", Y2[:], Y2[:], Y4[:], ALU.mult, r=Y2.all + Y4.all, w=Y2.all)
                        self.o_tt("pool", Y2[:], Y2[:], LNW[:, :].unsqueeze(2).to_broadcast([128, 8, 64]), ALU.mult, r=Y2.all + LNW.all, w=Y2.all)
                        self.o_tt("dve", Y2[:], Y2[:], LNB[:, :].unsqueeze(2).to_broadcast([128, 8, 64]), ALU.add, r=Y2.all + LNB.all, w=Y2.all)
                        self.o_tt("pool", Y2[:], Y2[:], I_bn[buf][:], ALU.add, r=Y2.all + I_bn[buf].all, w=Y2.all)
                        self.o_tt("dve", Zt[:], Y2[:], I_g[buf][:], ALU.mult, r=Y2.all + I_g[buf].all, w=Zt.all)
                        po = self.nps()
                        for dc in range(8):
                            for c in range(8):
                                self.mm(po[:, dc * 64:(dc + 1) * 64], Wo[:, c, dc * 128:(dc + 1) * 128], Zt[:, c, :], c == 0, c == 7, r=Wo.all + Zt.all, w=po.all)
                        xs = self.X[:, :, tsl]
                        xres = [self.xr(c, ch // 8) for c in range(8)]
                        self.o_tt("dve", xs, po[:, :].rearrange("p (c t) -> p c t", t=64), xs, ALU.add, r=po.all + xres, w=xres)
                    self.o_cp("act", H_bf[:], H_f[:], r=H_f.all, w=H_bf.all)
                    bd_write(H_bd, H_f[0:64, :, :], H_f[64:128, :, :], H_f.all, "pool", "pool")
        P.barrier()


def build(stages, debug=False):
    nc = bass.Bass("TRN2", target_bir_lowering=False)
    dram = {}

    def din(name, shape):
        dram[name] = nc.dram_tensor(name, list(shape), F32, kind="ExternalInput").ap()

    din("x", [S, D])
    for name, shape in PARAM_SHAPES.items():
        din(name, shape)
    for name, shape in CONST_SHAPES.items():
        din(name, shape)
    dram["y"] = nc.dram_tensor("y", [S, D], F32, kind="ExternalOutput").ap()
    with ExitStack() as es:
        P = Prog(nc, es)
        kb = KB(nc, es, P, dram)
        kb.debug = debug
        with ExitStack() as es2:
            kb.load_x(es2)
        P.barrier()
        for st in stages:
            kind, li = st
            if kind == "ffn":
                kb.ffn(li)
            elif kind == "swa":
                kb.swa(li)
            elif kind == "mla":
                kb.mla(li)
            elif kind == "rwkv":
                kb.rwkv(li)
        with ExitStack() as es2:
            kb.store_x(es2)
        P.emit()
        print("prog stats", P.stats)
    return nc


PARAM_SHAPES = {
    "norm_tok": (4, 1024), "norm_ch": (4, 1024), "ffn_w_up": (4, 1024, 5632), "ffn_conv_w": (4, 3, 5632),
    "ffn_conv_b": (4, 5632), "ffn_w_down": (4, 2816, 1024),
    "swa_w_qkv": (2, 1024, 1536), "swa_q_gain": (2, 64), "swa_k_gain": (2, 64), "swa_sinks": (2, 16), "swa_w_o": (2, 1024, 1024),
    "rwkv_mu": (1, 6, 1024), "rwkv_w_r": (1, 1024, 1024), "rwkv_w_k": (1, 1024, 1024), "rwkv_w_v": (1, 1024, 1024),
    "rwkv_w0": (1, 2, 1024), "rwkv_w1": (1, 2, 1024, 64), "rwkv_w2": (1, 2, 64, 1024), "rwkv_a0": (1, 2, 1024),
    "rwkv_a1": (1, 2, 1024, 64), "rwkv_a2": (1, 2, 64, 1024), "rwkv_g1": (1, 1024, 160), "rwkv_g2": (1, 160, 1024),
    "rwkv_k_k": (1, 1024), "rwkv_k_a": (1, 1024), "rwkv_r_k": (1, 16, 64), "rwkv_lnx_w": (1, 1024), "rwkv_lnx_b": (1, 1024),
    "rwkv_w_o": (1, 1024, 1024),
    "mla_w_down": (1, 1024, 672), "mla_cq_gain": (1, 384), "mla_ckv_gain": (1, 256), "mla_w_uq": (1, 384, 1536),
    "mla_w_ukv": (1, 256, 2048), "mla_q_gain": (1, 96), "mla_k_gain": (1, 96), "mla_w_o": (1, 1024, 1024),
}


def make_consts():
    c = {}
    c["c_ident"] = np.eye(128, dtype=np.float32)
    theta = np.float32(500000.0)

    def tables(rot):
        inv = (theta ** (-np.arange(0, rot, 2, dtype=np.float32) / np.float32(rot))).astype(np.float32)
        ang = (np.arange(S, dtype=np.float32)[:, None] * inv[None, :]).astype(np.float32)
        return np.cos(ang).astype(np.float32), np.sin(ang).astype(np.float32)

    def rope_consts(dh, start, rot):
        cs, sn = tables(rot)
        half = rot // 2
        C = np.ones((dh, S), np.float32)
        Sn = np.zeros((dh, S), np.float32)
        Pm = np.zeros((dh, dh), np.float32)
        for i in range(half):
            C[start + i] = cs[:, i]
            C[start + half + i] = cs[:, i]
            Sn[start + i] = sn[:, i]
            Sn[start + half + i] = sn[:, i]
            Pm[start + i, start + half + i] = -1.0
            Pm[start + half + i, start + i] = 1.0
        return C, Sn, np.ascontiguousarray(Pm.T)

    c["c_swa_cos"], c["c_swa_sin"], c["c_swa_rot"] = rope_consts(64, 0, 16)
    c["c_mla_cos"], c["c_mla_sin"], r96 = rope_consts(96, 64, 32)
    rp = np.zeros((128, 128), np.float32)
    rp[:96, :96] = r96
    c["c_mla_rot"] = rp
    o96 = np.zeros((128, 128), np.float32)
    o96[:96, :96] = 1.0
    c["c_ones96"] = o96
    b = np.arange(128)[:, None]
    a = np.arange(128)[None, :]
    c["c_mlo"] = (b <= a).astype(np.float32)
    c["c_mhi"] = (a <= b).astype(np.float32)
    p = np.arange(128)
    c["c_bd64"] = (p[:, None] // 64 == p[None, :] // 64).astype(np.float32)
    same = (p[:, None] // 64 == p[None, :] // 64)
    c["c_tri_f"] = np.concatenate([(same & (p[:, None] <= p[None, :])), (same & (p[:, None] < p[None, :]))], axis=1).astype(np.float32)
    c["c_tri_b"] = np.concatenate([(same & (p[:, None] >= p[None, :])), (same & (p[:, None] > p[None, :]))], axis=1).astype(np.float32)
    s_ = (p % 64)[:, None]
    t_ = np.arange(64)[None, :]
    c["c_mask_f"] = np.tile(np.concatenate([s_ < t_, s_ <= t_], axis=1), (1, 4)).astype(np.float32)
    c["c_mask_b"] = np.tile(np.concatenate([s_ > t_, s_ >= t_], axis=1), (1, 4)).astype(np.float32)
    c["c_lmask_f"] = np.tile(t_ < s_, (1, 8)).astype(np.float32)
    c["c_lmask_b"] = np.tile(t_ > s_, (1, 8)).astype(np.float32)
    c["c_ist"] = (s_ == t_).astype(np.float32)
    ids = np.zeros((32, 96), np.float32)
    ids[np.arange(32), 64 + np.arange(32)] = 1.0
    c["c_mla_ids"] = ids
    return c


CONST_SHAPES = {k: v.shape for k, v in make_consts().items()}

FULL_STAGES = [("swa", 0), ("ffn", 0), ("rwkv", 0), ("ffn", 1), ("mla", 0), ("ffn", 2), ("swa", 1), ("ffn", 3)]


def run(inputs, stages, ncores=8, debug=False):
    nc = build(stages, debug)
    consts = make_consts()
    x = np.ascontiguousarray(inputs["x"], dtype=np.float32)
    in_maps = []
    for b in range(ncores):
        m = {"x": x[b]}
        for name in PARAM_SHAPES:
            m[name] = np.ascontiguousarray(inputs[name], dtype=np.float32)
        m.update(consts)
        in_maps.append(m)
    res = run_bass_kernel_spmd(nc, in_maps, core_ids=list(range(ncores)))
    return np.stack([r["y"] for r in res.results], axis=0)


def kernel(**inputs):
    return run(inputs, FULL_STAGES, 8).astype(np.float32)
```

```python
import numpy as np
import ml_dtypes
from contextlib import ExitStack
import concourse.bass as bass
import concourse.mybir as mybir
from concourse.bass_utils import run_bass_kernel_spmd

F32 = mybir.dt.float32
BF16 = mybir.dt.bfloat16
ALU = mybir.AluOpType
AF = mybir.ActivationFunctionType
AX = mybir.AxisListType

import os
SAME_ENGINE_SYNC = os.environ.get('SES', '0') == '1'
S = 2048
D = 1024
NT = 4
FF = 2816
EPS = 1e-6


class Res:
    __slots__ = ("last_w", "rc", "rd", "excl")

    def __init__(self):
        self.excl = False
        self.last_w = None
        self.rc = {}
        self.rd = []


class T:
    def __init__(self, h, nres=1):
        self.h = h
        self.rs = [Res() for _ in range(nres)]

    def __getitem__(self, idx):
        return self.h[idx]

    @property
    def all(self):
        return list(self.rs)


class Op:
    __slots__ = ("eng", "fn", "deps", "needed", "sig", "is_dma", "idx")


class Prog:
    ENGS = ("pe", "dve", "act", "pool", "sp")

    def __init__(self, nc, es, ndma_sems=8):
        self.nc = nc
        self.es = es
        self.ops = []
        self.ndma = ndma_sems
        self.n_alloc = 0

    def sb(self, shape, dt, nres=1, es=None):
        self.n_alloc += 1
        h = (es or self.es).enter_context(self.nc.sbuf_tensor(f"sb{self.n_alloc}", list(shape), dt))
        return T(h, nres)

    def ps(self, shape, dt=F32, nres=1, es=None):
        self.n_alloc += 1
        h = (es or self.es).enter_context(self.nc.psum_tensor(f"ps{self.n_alloc}", list(shape), dt))
        t = T(h, nres)
        for x in t.rs:
            x.excl = True
        return t

    def add(self, eng, fn, r=(), w=(), is_dma=False):
        op = Op()
        op.eng = eng
        op.fn = fn
        op.is_dma = is_dma
        op.needed = False
        op.sig = None
        op.idx = len(self.ops)
        deps = set()
        ex = [x for x in r if x.excl]
        if ex:
            r = [x for x in r if not x.excl]
            w = list(w) + [x for x in ex if x not in w]
        for x in r:
            if x.last_w is not None:
                deps.add(x.last_w)
        for x in w:
            if x.last_w is not None:
                deps.add(x.last_w)
            deps.update(x.rc.values())
            deps.update(x.rd)
        for x in r:
            if is_dma:
                x.rd.append(op.idx)
            else:
                x.rc[eng] = op.idx
        for x in w:
            x.last_w = op.idx
            x.rc = {}
            x.rd = []
        deps.discard(op.idx)
        op.deps = deps
        self.ops.append(op)
        return op

    def dma(self, out, in_, r=(), w=(), q="sp", **kw):
        return self.add(q, lambda e: e.dma_start(out=out, in_=in_, **kw), r, w, is_dma=True)

    def barrier(self):
        last = {}
        dmas = []
        for op in self.ops:
            if op.is_dma:
                dmas.append(op.idx)
            else:
                last[op.eng] = op.idx
        ids = set(last.values()) | set(dmas[-self.ndma:])
        for e in self.ENGS:
            op = self.add(e, lambda en: en.nop())
            op.deps = set(ids)

    def emit(self):
        nc = self.nc
        ops = self.ops
        for op in ops:
            for d in op.deps:
                ops[d].needed = True
        es = self.es
        EPOCH = 8000
        sems = {}
        dsems = [es.enter_context(nc.semaphore(f"s_dma{i}")) for i in range(self.ndma)]
        cnt = {e: 0 for e in self.ENGS}
        dcnt = [0] * self.ndma
        ndma = 0
        dma_prev = {}
        for op in ops:
            if op.is_dma:
                k = ndma % self.ndma
                ndma += 1
                dma_prev[op.idx] = (k, dcnt[k])
                dcnt[k] += 16
                op.sig = (("d", k), dcnt[k])
            elif op.needed:
                ep = cnt[op.eng] // EPOCH
                cnt[op.eng] += 1
                key = ("e", op.eng, ep)
                if key not in sems:
                    sems[key] = es.enter_context(nc.semaphore(f"s_{op.eng}_{ep}"))
                op.sig = (key, cnt[op.eng] - ep * EPOCH)
        self.stats = dict(cnt=dict(cnt), ndma=ndma, nops=len(ops), nsem=len(sems) + self.ndma)

        def semof(key):
            return dsems[key[1]] if key[0] == "d" else sems[key]

        byeng = {e: [op for op in ops if op.eng == e] for e in self.ENGS}

        def run(ename, eobj):
            known = {}
            for op in byeng[ename]:
                waits = {}
                for d in op.deps:
                    dop = ops[d]
                    if dop.eng == ename and not dop.is_dma and (ename == "pe" or not SAME_ENGINE_SYNC):
                        continue
                    key, v = dop.sig
                    if known.get(key, 0) >= v:
                        continue
                    if waits.get(key, 0) < v:
                        waits[key] = v
                if op.is_dma:
                    k, v = dma_prev[op.idx]
                    key = ("d", k)
                    if v > 0 and known.get(key, 0) < v and waits.get(key, 0) < v:
                        waits[key] = v
                for key, v in waits.items():
                    eobj.wait_ge(semof(key), v)
                    known[key] = v
                ins = op.fn(eobj)
                if op.sig is not None:
                    ins.then_inc(semof(op.sig[0]), 16 if op.is_dma else 1)
            if ename == "sp":
                for k in range(self.ndma):
                    if dcnt[k] > 0:
                        eobj.wait_ge(dsems[k], dcnt[k])

        with nc.Block() as block:
            @block.tensor
            def _(e):
                run("pe", e)

            @block.vector
            def _(e):
                run("dve", e)

            @block.scalar
            def _(e):
                run("act", e)

            @block.gpsimd
            def _(e):
                run("pool", e)

            @block.sync
            def _(e):
                run("sp", e)


class KB:
    def __init__(self, nc, es, P, dram):
        self.nc = nc
        self.es = es
        self.P = P
        self.d = dram
        P_ = P
        self.X = P_.sb([128, 8, S], F32, nres=8 * NT)
        self.ident = P_.sb([128, 128], F32)
        self.ones_bf = P_.sb([128, 128], BF16)
        self.ones_f = P_.sb([128, 128], F32)
        self.ps = [P_.ps([128, 512]) for _ in range(8)]
        self.psi = 0
        P_.dma(self.ident[:], dram["c_ident"], w=self.ident.all)
        P_.add("dve", lambda e: e.memset(self.ones_bf[:], 1.0), w=self.ones_bf.all)
        P_.add("dve", lambda e: e.memset(self.ones_f[:], 1.0), w=self.ones_f.all)
        self.cast_rr = 0

    def xr(self, c, tt):
        return self.X.rs[c * NT + tt]

    def xr_all(self):
        return self.X.all

    def nps(self):
        p = self.ps[self.psi % 6]
        self.psi += 1
        return p

    def mm(self, out, lhsT, rhs, start, stop, r, w):
        self.P.add("pe", lambda e: e.matmul(out, lhsT=lhsT, rhs=rhs, start=start, stop=stop), r=r, w=w)

    def cast(self, out, in_, r, w, eng=None):
        if eng is None:
            eng = ("pool", "act")[self.cast_rr % 2]
            self.cast_rr += 1
        if eng == "act":
            self.P.add("act", lambda e: e.copy(out=out, in_=in_), r=r, w=w)
        elif eng == "pool":
            self.P.add("pool", lambda e: e.tensor_copy(out=out, in_=in_), r=r, w=w)
        else:
            self.P.add("dve", lambda e: e.tensor_copy(out=out, in_=in_), r=r, w=w)

    def load_w(self, dst, dst_sl, src2d, kc, m, stg):
        P = self.P
        per = stg[0].h.shape[1]
        kstep = max(1, per // m)
        i = 0
        for k0 in range(0, kc, kstep):
            k1 = min(kc, k0 + kstep)
            st = stg[self.stg_rr % len(stg)]
            self.stg_rr += 1
            n = (k1 - k0) * m
            sv = st[:, 0:n].rearrange("p (k m) -> p k m", m=m)
            P.dma(sv, src2d[k0 * 128:k1 * 128, :].rearrange("(k p) m -> p k m", p=128), w=st.all)
            self.cast(dst_sl(k0, k1), sv, r=st.all, w=dst.all)
            i += 1

    stg_rr = 0

    def load_x(self, es):
        P = self.P
        xin = self.d["x"]
        tok = [P.sb([128, D], F32, es=es) for _ in range(2)]
        for ti in range(16):
            tk = tok[ti % 2]
            P.dma(tk[:], xin[ti * 128:(ti + 1) * 128, :], w=tk.all)
            for half in range(2):
                ps = self.nps()
                for j in range(4):
                    c = half * 4 + j
                    P.add("pe", lambda e, ps=ps, j=j, c=c, tk=tk: e.transpose(ps[:, j * 128:(j + 1) * 128], tk[:, c * 128:(c + 1) * 128], self.ident[:]),
                          r=tk.all + self.ident.all, w=ps.all)
                tt = ti // 4
                o = self.X[:, half * 4:(half + 1) * 4, ti * 128:(ti + 1) * 128]
                i = ps[:, :].rearrange("p (j t) -> p j t", t=128)
                eng = "dve" if half == 0 else "act"
                if eng == "dve":
                    P.add("dve", lambda e, o=o, i=i: e.tensor_copy(out=o, in_=i), r=ps.all, w=[self.xr(c, tt) for c in range(half * 4, half * 4 + 4)])
                else:
                    P.add("act", lambda e, o=o, i=i: e.copy(out=o, in_=i), r=ps.all, w=[self.xr(c, tt) for c in range(half * 4, half * 4 + 4)])

    def store_x(self, es):
        P = self.P
        yout = self.d["y"]
        tok = [P.sb([128, D], F32, es=es) for _ in range(2)]
        for ti in range(16):
            tk = tok[ti % 2]
            tt = ti // 4
            for half in range(2):
                ps = self.nps()
                for j in range(4):
                    c = half * 4 + j
                    P.add("pe", lambda e, ps=ps, j=j, c=c, ti=ti: e.transpose(ps[:, j * 128:(j + 1) * 128], self.X[:, c, ti * 128:(ti + 1) * 128], self.ident[:]),
                          r=[self.xr(c, tt)] + self.ident.all, w=ps.all)
                o = tk[:, half * 512:(half + 1) * 512]
                if half == 0:
                    P.add("dve", lambda e, o=o, ps=ps: e.tensor_copy(out=o, in_=ps[:, :]), r=ps.all, w=tk.all)
                else:
                    P.add("act", lambda e, o=o, ps=ps: e.copy(out=o, in_=ps[:, :]), r=ps.all, w=tk.all)
            P.dma(yout[ti * 128:(ti + 1) * 128, :], tk[:], r=tk.all)

    def rmsnorm(self, H, gain_ap, es, t0=0, t1=S, hoff=0, rstd_out=None):
        P = self.P
        if not hasattr(es, "_rn_tmp"):
            es._rn_tmp = (P.sb([128, 8, 512], BF16, es=es), P.sb([128, 512], F32, es=es), P.sb([128, 512], F32, es=es))
        sq, rs, rs2 = es._rn_tmp
        for a in range(t0, t1, 512):
            b = min(t1, a + 512)
            n = b - a
            tts = sorted(set([a // 512, (b - 1) // 512]))
            xr = [self.xr(c, tt) for c in range(8) for tt in tts]
            P.add("act", lambda e, a=a, b=b, n=n: e.activation(out=sq[:, :, 0:n], in_=self.X[:, :, a:b], func=AF.Square), r=xr, w=sq.all)
            ps = self.nps()
            for c in range(8):
                self.mm(ps[:, 0:n], self.ones_bf[:], sq[:, c, 0:n], c == 0, c == 7, r=sq.all + self.ones_bf.all, w=ps.all)
            P.add("act", lambda e, n=n, ps=ps: e.activation(out=rs[:, 0:n], in_=ps[:, 0:n], func=AF.Ln, bias=EPS, scale=1.0 / D), r=ps.all, w=rs.all)
            if rstd_out is not None:
                P.add("act", lambda e, n=n, a=a, b=b: e.activation(out=rstd_out[:, a:b], in_=rs[:, 0:n], func=AF.Exp, scale=-0.5), r=rs.all, w=rstd_out.all)
                continue
            P.add("act", lambda e, n=n: e.activation(out=rs2[:, 0:n], in_=rs[:, 0:n], func=AF.Exp, scale=-0.5), r=rs.all, w=rs2.all)
            for c in range(8):
                P.add("dve", lambda e, c=c, a=a, b=b, n=n: e.scalar_tensor_tensor(
                    out=H[:, c, hoff + a - t0:hoff + b - t0], in0=self.X[:, c, a:b], scalar=gain_ap[:, c:c + 1], in1=rs2[:, 0:n],
                    op0=ALU.mult, op1=ALU.mult), r=[self.xr(c, tt) for tt in tts] + rs2.all, w=H.all)

    def load_vec8(self, dst, src1d, q="sp"):
        self.P.dma(dst, src1d.rearrange("(c p) -> p c", p=128), w=[], q=q, allow_slow_non_contiguous=True)

    def ffn(self, li):
        P = self.P
        d = self.d
        with ExitStack() as es:
            gains = P.sb([128, 8], F32, es=es)
            P.dma(gains[:], d["norm_ch"][li].rearrange("(c p) -> p c", p=128), w=gains.all, allow_slow_non_contiguous=True)
            cw = P.sb([128, 3, 44], F32, es=es)
            cb = P.sb([128, 44], F32, es=es)
            for t in range(3):
                P.dma(cw[:, t, :], d["ffn_conv_w"][li, t].rearrange("(c p) -> p c", p=128), w=cw.all, allow_slow_non_contiguous=True)
            P.dma(cb[:], d["ffn_conv_b"][li].rearrange("(c p) -> p c", p=128), w=cb.all, allow_slow_non_contiguous=True)
            H = P.sb([128, 8, 1026], BF16, es=es)
            Hh = P.sb([128, 8, 2], BF16, es=es)
            G = P.sb([128, 22, 1024], BF16, es=es, nres=22)
            U = [P.sb([128, 1026], F32, es=es) for _ in range(2)]
            A = [P.sb([128, 1024], F32, es=es) for _ in range(2)]
            SG = P.sb([128, 1024], F32, es=es)
            wstg = [P.sb([128, 1024], F32, es=es) for _ in range(2)]
            wup = [[P.sb([128, 8, 128], BF16, es=es) for _ in range(2)] for _ in range(2)]
            wdn = [P.sb([128, 22, 128], BF16, es=es) for _ in range(2)]
            wdstg = P.sb([128, 22, 128], F32, es=es)
            w_up = d["ffn_w_up"][li]
            w_dn = d["ffn_w_down"][li]
            for half in range(2):
                hb = half * 1024
                lo = max(0, hb - 1)
                hi = min(S, hb + 1025)
                c0 = lo - (hb - 1)
                ncol = hi - lo
                if half == 0:
                    self.rmsnorm(H, gains, es, lo, hi, hoff=c0)
                    P.add("pool", lambda e: e.tensor_copy(out=Hh[:, :, 0:1], in_=H[:, :, 1024:1025]), r=H.all, w=Hh.all)
                else:
                    self.rmsnorm(H, gains, es, 1024, 2048, hoff=1)
                    P.add("pool", lambda e: e.tensor_copy(out=H[:, :, 0:1], in_=Hh[:, :, 0:1]), r=Hh.all, w=H.all)
                for j in range(22):
                    wb = wup[j % 2]
                    for part in range(2):
                        col0 = part * FF + j * 128
                        st = wstg[part]
                        sv = st[:, :].rearrange("p (k m) -> p k m", m=128)
                        if not (os.environ.get("FFN_SKIPDMA") == "1" and j % 4 != 0):
                            P.dma(sv, w_up[:, col0:col0 + 128].rearrange("(k p) m -> p k m", p=128), w=st.all)
                        self.cast(wb[part][:], sv, r=st.all, w=wb[part].all, eng=("dve", "act")[part])
                    for part in range(2):
                        fc = part * 22 + j
                        u = U[part]
                        if half == 0:
                            P.add("pool", lambda e, u=u: e.memset(u[:, 0:1], 0.0), w=u.all)
                        else:
                            P.add("pool", lambda e, u=u: e.memset(u[:, 1025:1026], 0.0), w=u.all)
                        segs = [(0, 512), (512, 1024), (1024, ncol)]
                        pss = [self.nps() for _ in segs]
                        for k in range(8):
                            for (a, b), ps in zip(segs, pss):
                                self.mm(ps[:, 0:b - a], wb[part][:, k, :], H[:, k, c0 + a:c0 + b], k == 0, k == 7,
                                        r=wb[part].all + H.all, w=ps.all)
                        for (a, b), ps in zip(segs, pss):
                            P.add("act", lambda e, u=u, a=a, b=b, ps=ps, c0=c0: e.copy(out=u[:, c0 + a:c0 + b], in_=ps[:, 0:b - a]), r=ps.all, w=u.all)
                        acc = A[part]
                        P.add("act", lambda e, u=u, acc=acc, fc=fc: e.activation(out=acc[:], in_=u[:, 1:1025], func=AF.Identity,
                                                                                 bias=cb[:, fc:fc + 1], scale=cw[:, 1, fc:fc + 1]),
                              r=u.all + cb.all + cw.all, w=acc.all)
                        P.add("dve", lambda e, u=u, acc=acc, fc=fc: e.scalar_tensor_tensor(out=acc[:], in0=u[:, 0:1024], scalar=cw[:, 0, fc:fc + 1], in1=acc[:],
                                                                                           op0=ALU.mult, op1=ALU.add), r=u.all + cw.all, w=acc.all)
                        P.add("dve", lambda e, u=u, acc=acc, fc=fc: e.scalar_tensor_tensor(out=acc[:], in0=u[:, 2:1026], scalar=cw[:, 2, fc:fc + 1], in1=acc[:],
                                                                                           op0=ALU.mult, op1=ALU.add), r=u.all + cw.all, w=acc.all)
                    P.add("act", lambda e: e.activation(out=SG[:], in_=A[0][:], func=AF.Silu), r=A[0].all, w=SG.all)
                    P.add("pool", lambda e, j=j: e.tensor_tensor(out=G[:, j, :], in0=SG[:], in1=A[1][:], op=ALU.mult), r=SG.all + A[1].all, w=[G.rs[j]])
                for dc in range(8):
                    wd = wdn[dc % 2]
                    P.dma(wdstg[:], w_dn[:, dc * 128:(dc + 1) * 128].rearrange("(k p) m -> p k m", p=128), w=wdstg.all)
                    self.cast(wd[:, 0:11, :], wdstg[:, 0:11, :], r=wdstg.all, w=wd.all, eng="act")
                    self.cast(wd[:, 11:22, :], wdstg[:, 11:22, :], r=wdstg.all, w=wd.all, eng="dve")
                    for t2 in range(2):
                        ps = self.nps()
                        for j in range(22):
                            self.mm(ps[:, :], wd[:, j, :], G[:, j, t2 * 512:(t2 + 1) * 512], j == 0, j == 21, r=wd.all + [G.rs[j]], w=ps.all)
                        tt = half * 2 + t2
                        xs = self.X[:, dc, tt * 512:(tt + 1) * 512]
                        P.add("dve", lambda e, xs=xs, ps=ps: e.tensor_tensor(out=xs, in0=ps[:, :], in1=xs, op=ALU.add), r=ps.all + [self.xr(dc, tt)], w=[self.xr(dc, tt)])
        P.barrier()


    def head_qk(self, *a, **kw):
        for _ in self.head_qk_gen(*a, **kw):
            pass

    def head_qk_gen(self, outT, terms, dh, gain_ap, PrT, Ct, St, wk, kd=None, ones_t=None):
        P = self.P
        raw, sq, rs, rs2, xn, t1, t2 = wk
        kd = kd or dh
        ones_t = ones_t or self.ones_f
        for tt in range(NT):
            sl = slice(tt * 512, (tt + 1) * 512)
            ps = self.nps()
            tl = terms(tt)
            for i, (lt, rh, rd) in enumerate(tl):
                self.mm(ps[0:dh, :], lt, rh, i == 0, i == len(tl) - 1, r=rd, w=ps.all)
            yield
            P.add("act", lambda e, ps=ps: e.copy(out=raw[0:dh, :], in_=ps[0:dh, :]), r=ps.all, w=raw.all)
            P.add("act", lambda e, ps=ps: e.activation(out=sq[0:dh, :], in_=ps[0:dh, :], func=AF.Square), r=ps.all, w=sq.all)
            yield
            ps2 = self.nps()
            self.mm(ps2[0:kd, :], ones_t[0:kd, 0:kd], sq[0:kd, :], True, True, r=sq.all + ones_t.all, w=ps2.all)
            yield
            P.add("act", lambda e, ps2=ps2: e.activation(out=rs[0:dh, :], in_=ps2[0:dh, :], func=AF.Ln, bias=EPS, scale=1.0 / dh), r=ps2.all, w=rs.all)
            P.add("act", lambda e: e.activation(out=rs2[0:dh, :], in_=rs[0:dh, :], func=AF.Exp, scale=-0.5), r=rs.all, w=rs2.all)
            yield
            P.add("dve", lambda e: e.scalar_tensor_tensor(out=xn[0:dh, :], in0=raw[0:dh, :], scalar=gain_ap, in1=rs2[0:dh, :], op0=ALU.mult, op1=ALU.mult),
                  r=raw.all + rs2.all, w=xn.all)
            yield
            ps3 = self.nps()
            self.mm(ps3[0:kd, :], PrT[0:kd, 0:kd], xn[0:kd, :], True, True, r=xn.all + PrT.all, w=ps3.all)
            P.add("pool", lambda e, sl=sl: e.tensor_tensor(out=t1[0:dh, :], in0=xn[0:dh, :], in1=Ct[0:dh, sl], op=ALU.mult), r=xn.all + Ct.all, w=t1.all)
            yield
            P.add("dve", lambda e, sl=sl, ps3=ps3: e.tensor_tensor(out=t2[0:dh, :], in0=ps3[0:dh, :], in1=St[0:dh, sl], op=ALU.mult), r=ps3.all + St.all, w=t2.all)
            yield
            P.add("pool", lambda e, sl=sl: e.tensor_tensor(out=outT[0:dh, sl], in0=t1[0:dh, :], in1=t2[0:dh, :], op=ALU.add), r=t1.all + t2.all, w=outT.all)
            yield

    @staticmethod
    def interleave(main, side, k=1):
        for _ in main:
            for _ in range(k):
                if side is not None and next(side, "END") == "END":
                    side = None
        if side is not None:
            for _ in side:
                pass

    def attn_norm(self, ps_o, ps_sum, out_ap, out_res, sink_ap, wk2):
        P = self.P
        den, bc = wk2
        if sink_ap is not None:
            P.add("act", lambda e: e.activation(out=den[0:64, :], in_=ps_sum[0:64, :], func=AF.Ln, bias=sink_ap), r=ps_sum.all, w=den.all)
        else:
            P.add("act", lambda e: e.activation(out=den[0:64, :], in_=ps_sum[0:64, :], func=AF.Ln), r=ps_sum.all, w=den.all)
        P.add("act", lambda e: e.activation(out=bc[0:64, :], in_=den[0:64, :], func=AF.Exp, scale=-1.0), r=den.all, w=bc.all)
        P.add("dve", lambda e: e.tensor_tensor(out=out_ap, in0=ps_o[0:64, :], in1=bc[0:64, :], op=ALU.mult), r=ps_o.all + bc.all, w=out_res)

    def oproj_accum(self, wo_bf, OTc, nk):
        P = self.P
        for dc in range(8):
            for tt in range(NT):
                ps = self.nps()
                for k in range(nk):
                    self.mm(ps[:, :], wo_bf[:, k, dc * 128:(dc + 1) * 128], OTc[:, k, tt * 512:(tt + 1) * 512], k == 0, k == nk - 1,
                            r=wo_bf.all + OTc.all, w=ps.all)
                xs = self.X[:, dc, tt * 512:(tt + 1) * 512]
                P.add("dve", lambda e, xs=xs, ps=ps: e.tensor_tensor(out=xs, in0=ps[:, :], in1=xs, op=ALU.add), r=ps.all + [self.xr(dc, tt)], w=[self.xr(dc, tt)])

    def dbg(self, ap, slot, npart, ncols):
        if not getattr(self, "debug", False):
            return
        self.P.barrier()
        self.P.add("dve", lambda e: e.tensor_copy(out=self.X[0:npart, slot, 0:ncols], in_=ap), r=[], w=self.X.all)
        self.P.barrier()

    def qk_work(self, es):
        P = self.P
        return [P.sb([128, 512], F32, es=es) for _ in range(7)]

    def swa(self, j):
        P = self.P
        d = self.d
        li = 3 * j
        with ExitStack() as es:
            gains = P.sb([128, 8], F32, es=es)
            P.dma(gains[:], d["norm_tok"][li].rearrange("(c p) -> p c", p=128), w=gains.all, allow_slow_non_contiguous=True)
            qg_t = P.sb([64, 1], F32, es=es)
            kg_t = P.sb([64, 1], F32, es=es)
            P.dma(qg_t[:], d["swa_q_gain"][j].rearrange("(p o) -> p o", o=1), w=qg_t.all, allow_slow_non_contiguous=True)
            P.dma(kg_t[:], d["swa_k_gain"][j].rearrange("(p o) -> p o", o=1), w=kg_t.all, allow_slow_non_contiguous=True)
            sk = P.sb([128, 16], F32, es=es)
            P.dma(sk[:], d["swa_sinks"][j:j + 1, :].to_broadcast([128, 16]), w=sk.all, allow_slow_non_contiguous=True)
            sk0 = sk
            sk = P.sb([128, 16], F32, es=es)
            P.add("act", lambda e: e.activation(out=sk[:], in_=sk0[:], func=AF.Exp), r=sk0.all, w=sk.all)
            Ct = P.sb([64, S], F32, es=es)
            St = P.sb([64, S], F32, es=es)
            PrT = P.sb([64, 64], F32, es=es)
            MLO = P.sb([128, 128], BF16, es=es)
            MHI = P.sb([128, 128], BF16, es=es)
            mstg = P.sb([128, 256], F32, es=es)
            P.dma(Ct[:], d["c_swa_cos"], w=Ct.all)
            P.dma(St[:], d["c_swa_sin"], w=St.all)
            P.dma(PrT[:], d["c_swa_rot"], w=PrT.all)
            P.dma(mstg[:, 0:128], d["c_mlo"], w=mstg.all)
            P.dma(mstg[:, 128:256], d["c_mhi"], w=mstg.all)
            P.add("dve", lambda e: e.tensor_copy(out=MLO[:], in_=mstg[:, 0:128]), r=mstg.all, w=MLO.all)
            P.add("dve", lambda e: e.tensor_copy(out=MHI[:], in_=mstg[:, 128:256]), r=mstg.all, w=MHI.all)
            H = P.sb([128, 8, S], BF16, es=es)
            self.rmsnorm(H, gains, es)
            stg = [P.sb([128, 1024], F32, es=es) for _ in range(2)]
            wqkv = d["swa_w_qkv"][j]
            wo = d["swa_w_o"][j]
            Wkv = P.sb([128, 8, 512], BF16, es=es)
            self.load_w(Wkv, lambda k0, k1: Wkv[:, k0:k1, :], wqkv[:, 1024:1536], 8, 512, stg)
            V = P.sb([128, 16, 4, 65], BF16, es=es)
            P.add("pool", lambda e: e.memset(V[:], 1.0), w=V.all)
            for ti in range(16):
                ps = self.nps()
                for k in range(8):
                    self.mm(ps[:, 0:256], H[:, k, ti * 128:(ti + 1) * 128], Wkv[:, k, 256:512], k == 0, k == 7, r=H.all + Wkv.all, w=ps.all)
                P.add("act", lambda e, ti=ti, ps=ps: e.copy(out=V[:, ti, :, 0:64], in_=ps[:, 0:256].rearrange("p (g v) -> p g v", v=64)), r=ps.all, w=V.all)
            wk = self.qk_work(es)
            den = P.sb([128, 512], F32, es=es)
            bc = P.sb([128, 512], F32, es=es)
            KTs = [P.sb([64, S], BF16, es=es) for _ in range(2)]
            QTs = [P.sb([64, S], BF16, es=es) for _ in range(2)]
            PT = [P.sb([128, 384], BF16, es=es) for _ in range(2)]
            OTc = P.sb([128, 2, S], BF16, es=es)
            Wq = P.sb([128, 8, 256], BF16, es=es)
            Wo = P.sb([128, 2, 1024], BF16, es=es)
            scale = 64 ** -0.5

            def prepK(g):
                return self.head_qk_gen(KTs[g % 2], lambda tt, g=g: [(Wkv[:, k, g * 64:(g + 1) * 64], H[:, k, tt * 512:(tt + 1) * 512], H.all + Wkv.all) for k in range(8)],
                                        64, kg_t[:, 0:1], PrT, Ct, St, wk)

            def prepQ(h):
                hh = h % 4
                return self.head_qk_gen(QTs[h % 2], lambda tt, hh=hh: [(Wq[:, k, hh * 64:(hh + 1) * 64], H[:, k, tt * 512:(tt + 1) * 512], H.all + Wq.all) for k in range(8)],
                                        64, qg_t[:, 0:1], PrT, Ct, St, wk)

            def chain(*gens):
                for g_ in gens:
                    if g_ is not None:
                        yield from g_

            def attn(h):
                g = h // 4
                hh = h % 4
                KT = KTs[g % 2]
                QT = QTs[h % 2]

                def smm(i):
                    js = [jj for jj in (i - 1, i, i + 1) if 0 <= jj < 16]
                    ps_s = self.nps()
                    for n, jj in enumerate(js):
                        self.mm(ps_s[:, n * 128:(n + 1) * 128], KT[:, jj * 128:(jj + 1) * 128], QT[:, i * 128:(i + 1) * 128], True, True,
                                r=KT.all + QT.all, w=ps_s.all)
                    return js, ps_s
                nxt = smm(0)
                for qg in range(4):
                    ps_o = self.ps[6]
                    ps_sum = self.ps[7]
                    for qi in range(4):
                        i = qg * 4 + qi
                        js, ps_s = nxt
                        if i + 1 < 16:
                            nxt = smm(i + 1)
                        pt = PT[i % 2]
                        nn = len(js) * 128
                        P.add("act", lambda e, pt=pt, ps_s=ps_s, nn=nn: e.activation(out=pt[:, 0:nn], in_=ps_s[:, 0:nn], func=AF.Exp, scale=scale), r=ps_s.all, w=pt.all)
                        for n, jj in enumerate(js):
                            if jj == i - 1:
                                P.add("dve", lambda e, pt=pt, n=n: e.tensor_tensor(out=pt[:, n * 128:(n + 1) * 128], in0=pt[:, n * 128:(n + 1) * 128], in1=MHI[:], op=ALU.mult),
                                      r=pt.all + MHI.all, w=pt.all)
                            elif jj == i + 1:
                                P.add("pool", lambda e, pt=pt, n=n: e.tensor_tensor(out=pt[:, n * 128:(n + 1) * 128], in0=pt[:, n * 128:(n + 1) * 128], in1=MLO[:], op=ALU.mult),
                                      r=pt.all + MLO.all, w=pt.all)
                        for n, jj in enumerate(js):
                            self.mm(ps_o[0:64, qi * 128:(qi + 1) * 128], V[:, jj, g, 0:64], pt[:, n * 128:(n + 1) * 128], n == 0, n == len(js) - 1,
                                    r=V.all + pt.all, w=ps_o.all)
                        for n, jj in enumerate(js):
                            self.mm(ps_sum[0:64, qi * 128:(qi + 1) * 128], self.ones_bf[:, 0:64], pt[:, n * 128:(n + 1) * 128], n == 0, n == len(js) - 1,
                                    r=self.ones_bf.all + pt.all, w=ps_sum.all)
                        yield
                    pb = (hh % 2) * 64
                    self.attn_norm(ps_o, ps_sum, OTc[pb:pb + 64, hh // 2, qg * 512:(qg + 1) * 512], OTc.all, sk[0:64, h:h + 1], (den, bc))
                    yield

            self.load_w(Wq, lambda k0, k1: Wq[:, k0:k1, :], wqkv[:, 0:256], 8, 256, stg)
            for _ in chain(prepK(0), prepQ(0)):
                pass
            for h in range(16):
                g = h // 4
                if h % 4 == 0:
                    self.load_w(Wo, lambda k0, k1: Wo[:, k0:k1, :], wo[g * 256:(g + 1) * 256, :], 2, 1024, stg)
                side = None
                if h + 1 < 16:
                    if (h + 1) % 4 == 0:
                        self.load_w(Wq, lambda k0, k1: Wq[:, k0:k1, :], wqkv[:, (g + 1) * 256:(g + 2) * 256], 8, 256, stg)
                        side = chain(prepK(g + 1), prepQ(h + 1))
                    else:
                        side = prepQ(h + 1)
                self.interleave(attn(h), side, k=3)
                if h % 4 == 3:
                    self.oproj_accum(Wo, OTc, 2)
        P.barrier()

    def mla(self, j):
        P = self.P
        d = self.d
        li = 2
        with ExitStack() as es:
            CQ = P.sb([128, 3, S], BF16, es=es)
            CKV = P.sb([128, 2, S], BF16, es=es)
            KR = P.sb([32, S], BF16, es=es)
            stg = [P.sb([128, 2048], F32, es=es) for _ in range(2)]
            with ExitStack() as esA:
                gains = P.sb([128, 8], F32, es=esA)
                P.dma(gains[:], d["norm_tok"][li].rearrange("(c p) -> p c", p=128), w=gains.all, allow_slow_non_contiguous=True)
                cg = P.sb([128, 5], F32, es=esA)
                P.dma(cg[:, 0:3], d["mla_cq_gain"][j].rearrange("(c p) -> p c", p=128), w=cg.all, allow_slow_non_contiguous=True)
                P.dma(cg[:, 3:5], d["mla_ckv_gain"][j].rearrange("(c p) -> p c", p=128), w=cg.all, allow_slow_non_contiguous=True)
                H = P.sb([128, 8, S], BF16, es=esA)
                self.rmsnorm(H, gains, esA)
                Wd = P.sb([128, 8, 672], BF16, es=esA)
                self.load_w(Wd, lambda k0, k1: Wd[:, k0:k1, :], d["mla_w_down"][j], 8, 672, stg)
                raw = [P.sb([128, 512], F32, es=esA) for _ in range(5)]
                sq = [P.sb([128, 512], F32, es=esA) for _ in range(5)]
                rs = P.sb([128, 512], F32, es=esA)
                rs2 = P.sb([128, 512], F32, es=esA)
                for tt in range(NT):
                    sl = slice(tt * 512, (tt + 1) * 512)
                    for c in range(6):
                        m = 128 if c < 5 else 32
                        ps = self.nps()
                        for k in range(8):
                            self.mm(ps[0:m, :], Wd[:, k, c * 128:c * 128 + m], H[:, k, sl], k == 0, k == 7, r=H.all + Wd.all, w=ps.all)
                        if c < 5:
                            P.add("act", lambda e, ps=ps, c=c: e.copy(out=raw[c][:], in_=ps[:, :]), r=ps.all, w=raw[c].all)
                            P.add("act", lambda e, ps=ps, c=c: e.activation(out=sq[c][:], in_=ps[:, :], func=AF.Square), r=ps.all, w=sq[c].all)
                        else:
                            P.add("act", lambda e, ps=ps, sl=sl: e.copy(out=KR[0:32, sl], in_=ps[0:32, :]), r=ps.all, w=KR.all)
                    for (c0, c1, dst, nf) in ((0, 3, CQ, 384), (3, 5, CKV, 256)):
                        ps = self.nps()
                        for c in range(c0, c1):
                            self.mm(ps[:, :], self.ones_f[:, :], sq[c][:], c == c0, c == c1 - 1, r=sq[c].all + self.ones_f.all, w=ps.all)
                        P.add("act", lambda e, ps=ps, nf=nf: e.activation(out=rs[:], in_=ps[:, :], func=AF.Sqrt, bias=EPS, scale=1.0 / nf), r=ps.all, w=rs.all)
                        P.add("dve", lambda e: e.reciprocal(out=rs2[:], in_=rs[:]), r=rs.all, w=rs2.all)
                        for c in range(c0, c1):
                            P.add("dve", lambda e, c=c, c0=c0, dst=dst, sl=sl: e.scalar_tensor_tensor(out=dst[:, c - c0, sl], in0=raw[c][:], scalar=cg[:, c:c + 1], in1=rs2[:],
                                                                                                 op0=ALU.mult, op1=ALU.mult), r=raw[c].all + rs2.all + cg.all, w=dst.all)
            P.barrier()
            qg_t = P.sb([96, 1], F32, es=es)
            kg_t = P.sb([96, 1], F32, es=es)
            P.dma(qg_t[:], d["mla_q_gain"][j].rearrange("(p o) -> p o", o=1), w=qg_t.all, allow_slow_non_contiguous=True)
            P.dma(kg_t[:], d["mla_k_gain"][j].rearrange("(p o) -> p o", o=1), w=kg_t.all, allow_slow_non_contiguous=True)
            Ct = P.sb([96, S], F32, es=es)
            St = P.sb([96, S], F32, es=es)
            PrT = P.sb([128, 128], F32, es=es)
            ones96 = P.sb([128, 128], F32, es=es)
            P.dma(ones96[:], d["c_ones96"], w=ones96.all)
            IdS = P.sb([32, 96], BF16, es=es)
            P.dma(Ct[:], d["c_mla_cos"], w=Ct.all)
            P.dma(St[:], d["c_mla_sin"], w=St.all)
            P.dma(PrT[:], d["c_mla_rot"], w=PrT.all)
            P.dma(stg[0][0:32, 0:96], d["c_mla_ids"], w=stg[0].all)
            P.add("dve", lambda e: e.tensor_copy(out=IdS[:], in_=stg[0][0:32, 0:96]), r=stg[0].all, w=IdS.all)
            Wuq = P.sb([128, 3, 1536], BF16, es=es)
            self.load_w(Wuq, lambda k0, k1: Wuq[:, k0:k1, :], d["mla_w_uq"][j], 3, 1536, stg)
            Wkn = P.sb([128, 2, 16, 96], BF16, es=es)
            Wv = P.sb([128, 2, 16, 64], BF16, es=es)
            P.add("pool", lambda e: e.memset(Wkn[:], 0.0), w=Wkn.all)
            wukv = d["mla_w_ukv"][j]
            for k in range(2):
                st = stg[k % 2]
                P.dma(st[:, 0:2048], wukv[k * 128:(k + 1) * 128, :], w=st.all)
                sv = st[:, 0:2048].rearrange("p (h t) -> p h t", t=128)
                P.add("act", lambda e, k=k, sv=sv: e.copy(out=Wkn[:, k, :, 0:64], in_=sv[:, :, 0:64]), r=st.all, w=Wkn.all)
                P.add("pool", lambda e, k=k, sv=sv: e.tensor_copy(out=Wv[:, k, :, :], in_=sv[:, :, 64:128]), r=st.all, w=Wv.all)
            wk = self.qk_work(es)
            for t_ in wk:
                P.add("pool", lambda e, t_=t_: e.memset(t_[:], 0.0), w=t_.all)
            den = P.sb([128, 512], F32, es=es)
            bc = P.sb([128, 512], F32, es=es)
            KTs = [P.sb([128, S], BF16, es=es) for _ in range(2)]
            QTs = [P.sb([128, S], BF16, es=es) for _ in range(2)]
            Vhs = [P.sb([128, 16, 64], BF16, es=es) for _ in range(2)]
            for t_ in KTs + QTs:
                P.add("pool", lambda e, t_=t_: e.memset(t_[:], 0.0), w=t_.all)
            PT = [P.sb([128, 512], BF16, es=es) for _ in range(3)]
            OTc = P.sb([128, 1, S], BF16, es=es)
            Wo = P.sb([128, 1, 1024], BF16, es=es)
            wo = d["mla_w_o"][j]
            scale = 96 ** -0.5

            def prepV(h):
                Vh = Vhs[h % 2]
                for ti in range(16):
                    ps = self.nps()
                    for k in range(2):
                        self.mm(ps[:, 0:64], CKV[:, k, ti * 128:(ti + 1) * 128], Wv[:, k, h, :], k == 0, k == 1, r=CKV.all + Wv.all, w=ps.all)
                    yield
                    P.add("act", lambda e, ti=ti, ps=ps, Vh=Vh: e.copy(out=Vh[:, ti, :], in_=ps[:, 0:64]), r=ps.all, w=Vh.all)
                    yield

            def prepK(h):
                return self.head_qk_gen(KTs[h % 2], lambda tt, h=h: [(Wkn[:, k, h, :], CKV[:, k, tt * 512:(tt + 1) * 512], CKV.all + Wkn.all) for k in range(2)]
                                        + [(IdS[0:32, :], KR[0:32, tt * 512:(tt + 1) * 512], IdS.all + KR.all)],
                                        96, kg_t[:, 0:1], PrT, Ct, St, wk, kd=128, ones_t=ones96)

            def prepQ(h):
                return self.head_qk_gen(QTs[h % 2], lambda tt, h=h: [(Wuq[:, k, h * 96:(h + 1) * 96], CQ[:, k, tt * 512:(tt + 1) * 512], CQ.all + Wuq.all) for k in range(3)],
                                        96, qg_t[:, 0:1], PrT, Ct, St, wk, kd=128, ones_t=ones96)

            def chain(*gens):
                for g_ in gens:
                    yield from g_

            def attn(h):
                KT, QT, Vh = KTs[h % 2], QTs[h % 2], Vhs[h % 2]
                pti = 0

                def smm(qg, jj):
                    ps_s = self.nps()
                    self.mm(ps_s[:, :], KT[:, jj * 128:(jj + 1) * 128], QT[:, qg * 512:(qg + 1) * 512], True, True, r=KT.all + QT.all, w=ps_s.all)
                    return ps_s
                seq = [(qg, jj) for qg in range(4) for jj in range(16)]
                nxt = smm(*seq[0])
                for n, (qg, jj) in enumerate(seq):
                    ps_o = self.ps[6]
                    ps_sum = self.ps[7]
                    ps_s = nxt
                    if n + 1 < len(seq):
                        nxt = smm(*seq[n + 1])
                    pt = PT[pti % 3]
                    pti += 1
                    P.add("act", lambda e, pt=pt, ps_s=ps_s: e.activation(out=pt[:], in_=ps_s[:, :], func=AF.Exp, scale=scale), r=ps_s.all, w=pt.all)
                    self.mm(ps_o[0:64, :], Vh[:, jj, :], pt[:], jj == 0, jj == 15, r=Vh.all + pt.all, w=ps_o.all)
                    self.mm(ps_sum[0:64, :], self.ones_bf[:, 0:64], pt[:], jj == 0, jj == 15, r=self.ones_bf.all + pt.all, w=ps_sum.all)
                    yield
                    if jj == 15:
                        pb = (h % 2) * 64
                        self.attn_norm(ps_o, ps_sum, OTc[pb:pb + 64, 0, qg * 512:(qg + 1) * 512], OTc.all, None, (den, bc))
                        yield

            for _ in chain(prepV(0), prepK(0), prepQ(0)):
                pass
            for h in range(16):
                if h % 2 == 0:
                    self.load_w(Wo, lambda k0, k1: Wo[:, k0:k1, :], wo[(h // 2) * 128:(h // 2 + 1) * 128, :], 1, 1024, stg)
                side = chain(prepV(h + 1), prepK(h + 1), prepQ(h + 1)) if h + 1 < 16 else None
                self.interleave(attn(h), side, k=2)
                if h % 2 == 1:
                    self.oproj_accum(Wo, OTc, 1)
        P.barrier()

    def o_tt(self, eng, out, in0, in1, op, r, w):
        self.P.add(eng, lambda e: e.tensor_tensor(out=out, in0=in0, in1=in1, op=op), r=r, w=w)

    def o_ts(self, eng, out, in0, s1, s2, op0, op1, r, w):
        if s2 is None:
            self.P.add(eng, lambda e: e.tensor_scalar(out=out, in0=in0, scalar1=s1, scalar2=None, op0=op0), r=r, w=w)
        else:
            self.P.add(eng, lambda e: e.tensor_scalar(out=out, in0=in0, scalar1=s1, scalar2=s2, op0=op0, op1=op1), r=r, w=w)

    def o_stt(self, out, in0, scalar, in1, op0, op1, r, w):
        self.P.add("dve", lambda e: e.scalar_tensor_tensor(out=out, in0=in0, scalar=scalar, in1=in1, op0=op0, op1=op1), r=r, w=w)

    def o_act(self, out, in_, func, r, w, bias=None, scale=None):
        kw = {}
        if bias is not None:
            kw["bias"] = bias
        if scale is not None:
            kw["scale"] = scale
        self.P.add("act", lambda e: e.activation(out=out, in_=in_, func=func, **kw), r=r, w=w)

    def o_cp(self, eng, out, in_, r, w):
        if eng == "act":
            self.P.add("act", lambda e: e.copy(out=out, in_=in_), r=r, w=w)
        else:
            self.P.add(eng, lambda e: e.tensor_copy(out=out, in_=in_), r=r, w=w)

    def rwkv(self, j):
        P = self.P
        d = self.d
        nc = self.nc
        li = 1
        NCH = S // 64
        def scr(name, shape, dt):
            t = nc.dram_tensor(name, list(shape), dt)
            return t.ap(), Res()
        S_ar = [scr(f"rw_ar{dd}", [128, 8, 2, S], BF16) for dd in range(2)]
        S_b = [scr(f"rw_b{dd}", [128, 8, S], BF16) for dd in range(2)]
        S_k = [scr(f"rw_k{dd}", [128, 8, S], BF16) for dd in range(2)]
        S_v = scr("rw_v", [128, 8, S], BF16)
        S_pc = [scr(f"rw_pc{dd}", [NCH, 128, 8], F32) for dd in range(2)]
        S_g = scr("rw_g", [128, 8, S], F32)
        S_bn = scr("rw_bn", [128, 8, S], F32)
        S_y = scr("rw_y", [128, 8, S], F32)

        with ExitStack() as es:
            gains = P.sb([128, 8], F32, es=es)
            P.dma(gains[:], d["norm_tok"][li].rearrange("(c p) -> p c", p=128), w=gains.all, allow_slow_non_contiguous=True)
            RSTD = P.sb([128, S], F32, es=es)
            Wr = P.sb([128, 8, 1024], BF16, es=es)
            Wk = P.sb([128, 8, 1024], BF16, es=es)
            Wv = P.sb([128, 8, 1024], BF16, es=es)
            W1 = P.sb([128, 8, 2, 64], BF16, es=es)
            A1 = P.sb([128, 8, 2, 64], BF16, es=es)
            G1 = P.sb([128, 8, 160], BF16, es=es)
            W2 = P.sb([64, 2, 1024], BF16, es=es)
            A2 = P.sb([64, 2, 1024], BF16, es=es)
            G2a = P.sb([128, 1024], BF16, es=es)
            G2b = P.sb([32, 1024], BF16, es=es)
            W0bc = P.sb([128, 2, 1024], F32, es=es)
            MU = P.sb([128, 6, 8], F32, es=es)
            A0 = P.sb([128, 2, 8], F32, es=es)
            KK_ = P.sb([128, 8], F32, es=es)
            KA_ = P.sb([128, 8], F32, es=es)
            RK_ = P.sb([128, 8], F32, es=es)
            BD64 = P.sb([128, 128], F32, es=es)
            TRI = [P.sb([128, 256], F32, es=es) for _ in range(2)]
            P.dma(MU[:], d["rwkv_mu"][j].rearrange("i (c p) -> p i c", p=128), w=MU.all, allow_slow_non_contiguous=True)
            P.dma(A0[:], d["rwkv_a0"][j].rearrange("i (c p) -> p i c", p=128), w=A0.all, allow_slow_non_contiguous=True)
            P.dma(KK_[:], d["rwkv_k_k"][j].rearrange("(c p) -> p c", p=128), w=KK_.all, allow_slow_non_contiguous=True)
            P.dma(KA_[:], d["rwkv_k_a"][j].rearrange("(c p) -> p c", p=128), w=KA_.all, allow_slow_non_contiguous=True)
            P.dma(RK_[:], d["rwkv_r_k"][j].rearrange("h k -> (h k)").rearrange("(c p) -> p c", p=128), w=RK_.all, allow_slow_non_contiguous=True)
            P.dma(BD64[:], d["c_bd64"], w=BD64.all)
            P.dma(TRI[0][:], d["c_tri_f"], w=TRI[0].all)
            P.dma(TRI[1][:], d["c_tri_b"], w=TRI[1].all)
            for dd in range(2):
                P.dma(W0bc[:, dd, :], d["rwkv_w0"][j, dd:dd + 1, :].to_broadcast([128, 1024]), w=W0bc.all, allow_slow_non_contiguous=True)
            with ExitStack() as esw:
                stg = [P.sb([128, 2048], F32, es=esw) for _ in range(2)]
                self.load_w(Wr, lambda k0, k1: Wr[:, k0:k1, :], d["rwkv_w_r"][j], 8, 1024, stg)
                self.load_w(Wk, lambda k0, k1: Wk[:, k0:k1, :], d["rwkv_w_k"][j], 8, 1024, stg)
                self.load_w(Wv, lambda k0, k1: Wv[:, k0:k1, :], d["rwkv_w_v"][j], 8, 1024, stg)
                for dd in range(2):
                    self.load_w(W1, lambda k0, k1, dd=dd: W1[:, k0:k1, dd, :], d["rwkv_w1"][j, dd], 8, 64, stg)
                    self.load_w(A1, lambda k0, k1, dd=dd: A1[:, k0:k1, dd, :], d["rwkv_a1"][j, dd], 8, 64, stg)
                self.load_w(G1, lambda k0, k1: G1[:, k0:k1, :], d["rwkv_g1"][j], 8, 160, stg)
                for dd in range(2):
                    for (dst, src) in ((W2, d["rwkv_w2"][j, dd]), (A2, d["rwkv_a2"][j, dd])):
                        st = stg[self.stg_rr % 2]
                        self.stg_rr += 1
                        P.dma(st[0:64, 0:1024], src, w=st.all)
                        self.cast(dst[:, dd, :], st[0:64, 0:1024], r=st.all, w=dst.all)
                st = stg[self.stg_rr % 2]
                self.stg_rr += 1
                P.dma(st[:, 0:1024], d["rwkv_g2"][j][0:128, :], w=st.all)
                self.cast(G2a[:], st[:, 0:1024], r=st.all, w=G2a.all)
                st = stg[self.stg_rr % 2]
                self.stg_rr += 1
                P.dma(st[0:32, 0:1024], d["rwkv_g2"][j][128:160, :], w=st.all)
                self.cast(G2b[:], st[0:32, 0:1024], r=st.all, w=G2b.all)
                self.rmsnorm(None, gains, esw, rstd_out=RSTD)
                P.barrier()
            HXf = P.sb([128, 2064], F32, es=es)
            HX = HXf
            Hh_ = HXf[:, 0:1040].rearrange("p (c t) -> p c t", t=130)
            XX_ = HXf[:, 1040:2064].rearrange("p (c t) -> p c t", t=128)
            TMPM = P.sb([128, 8, 128], F32, es=es)
            MIX = [P.sb([128, 8, 128], BF16, es=es) for _ in range(6)]
            O_ar = [P.sb([128, 8, 2, 128], BF16, es=es) for _ in range(2)]
            O_b = [P.sb([128, 8, 128], BF16, es=es) for _ in range(2)]
            O_k = [P.sb([128, 8, 128], BF16, es=es) for _ in range(2)]
            O_v = P.sb([128, 8, 128], BF16, es=es)
            O_pc = [P.sb([128, 2, 8], F32, es=es) for _ in range(2)]
            L1w = [P.sb([64, 128], BF16, es=es) for _ in range(2)]
            L1a = [P.sb([64, 128], BF16, es=es) for _ in range(2)]
            L1g = P.sb([128, 128], BF16, es=es)
            L1g2 = P.sb([32, 128], BF16, es=es)
            sm = [P.sb([128, 128], F32, es=es) for _ in range(22)]
            gq = 0
            (t_r, t_k, t_v, t_kq, t_sq, t_nr, t_kk, t_rr, t_a, t_t, t_kd, t_b, t_ep, t_em, t_epv, t_sb, t_sb2, t_x1, t_x2, t_x3, t_x4, t_x5) = sm
            for ti in range(16):
                t0 = ti * 128
                tt = ti // 4
                lo = max(0, t0 - 1)
                hi = min(S, t0 + 129)
                c0 = lo - (t0 - 1)
                n = hi - lo
                xrs = [self.xr(c, q) for c in range(8) for q in sorted(set([lo // 512, (hi - 1) // 512]))]
                if t0 == 0:
                    P.add("pool", lambda e: e.memset(Hh_[:, :, 0:1], 0.0), w=HX.all)
                if t0 + 129 > S:
                    P.add("pool", lambda e: e.memset(Hh_[:, :, 129:130], 0.0), w=HX.all)
                for c in range(8):
                    self.o_stt(Hh_[:, c, c0:c0 + n], self.X[:, c, lo:hi], gains[:, c:c + 1], RSTD[:, lo:hi], ALU.mult, ALU.mult, r=xrs + RSTD.all, w=HX.all)
                hc = Hh_[:, :, 1:129]
                xx = XX_
                self.o_tt("pool", TMPM[:], Hh_[:, :, 0:128], Hh_[:, :, 2:130], ALU.add, r=HX.all, w=TMPM.all)
                self.o_stt(xx, TMPM[:], 0.5, hc, ALU.mult, ALU.subtract, r=TMPM.all + HX.all, w=HX.all)
                for i in range(6):
                    self.o_tt("dve", TMPM[:], xx, MU[:, i, :].unsqueeze(2).to_broadcast([128, 8, 128]), ALU.mult, r=HX.all + MU.all, w=TMPM.all)
                    self.o_tt("dve", MIX[i][:], TMPM[:], hc, ALU.add, r=TMPM.all + HX.all, w=MIX[i].all)
                m_r, m_w, m_k, m_v, m_a, m_g = MIX
                for dd in range(2):
                    ps = self.nps()
                    for k in range(8):
                        self.mm(ps[0:64, 0:128], W1[:, k, dd, :], m_w[:, k, :], k == 0, k == 7, r=W1.all + m_w.all, w=ps.all)
                    self.o_act(L1w[dd][:], ps[0:64, 0:128], AF.Tanh, r=ps.all, w=L1w[dd].all)
                    ps = self.nps()
                    for k in range(8):
                        self.mm(ps[0:64, 0:128], A1[:, k, dd, :], m_a[:, k, :], k == 0, k == 7, r=A1.all + m_a.all, w=ps.all)
                    self.o_cp("act", L1a[dd][:], ps[0:64, 0:128], r=ps.all, w=L1a[dd].all)
                ps = self.nps()
                for k in range(8):
                    self.mm(ps[:, 0:128], G1[:, k, 0:128], m_g[:, k, :], k == 0, k == 7, r=G1.all + m_g.all, w=ps.all)
                self.o_act(L1g[:], ps[:, 0:128], AF.Sigmoid, r=ps.all, w=L1g.all)
                ps = self.nps()
                for k in range(8):
                    self.mm(ps[0:32, 0:128], G1[:, k, 128:160], m_g[:, k, :], k == 0, k == 7, r=G1.all + m_g.all, w=ps.all)
                self.o_act(L1g2[:], ps[0:32, 0:128], AF.Sigmoid, r=ps.all, w=L1g2.all)
                LW = HXf[:, 0:2048]
                for dd in range(2):
                    for hf in range(2):
                        ps = self.nps()
                        self.mm(ps[:, :], L1w[dd][:], W2[:, dd, hf * 512:(hf + 1) * 512], True, True, r=L1w[dd].all + W2.all, w=ps.all)
                        sl = slice(dd * 1024 + hf * 512, dd * 1024 + (hf + 1) * 512)
                        self.o_tt("dve", LW[:, sl], ps[:, :], W0bc[:, dd, hf * 512:(hf + 1) * 512], ALU.add, r=ps.all + W0bc.all, w=HX.all)
                    sl = slice(dd * 1024, (dd + 1) * 1024)
                    self.o_act(LW[:, sl], LW[:, sl], AF.Sigmoid, r=HX.all, w=HX.all)
                for oc in range(8):
                    fs = slice(oc * 128, (oc + 1) * 128)
                    for (wt, mx, dst) in ((Wr, m_r, t_r), (Wk, m_k, t_k), (Wv, m_v, t_v)):
                        ps = self.nps()
                        for k in range(8):
                            self.mm(ps[:, 0:128], wt[:, k, fs], mx[:, k, :], k == 0, k == 7, r=wt.all + mx.all, w=ps.all)
                        self.o_cp("act", dst[:], ps[:, 0:128], r=ps.all, w=dst.all)
                    self.o_cp("act", O_v[:, oc, :], t_v[:], r=t_v.all, w=O_v.all)
                    self.o_ts("dve", t_kq[:], t_k[:], KK_[:, oc:oc + 1], None, ALU.mult, None, r=t_k.all + KK_.all, w=t_kq.all)
                    self.o_act(t_sq[:], t_kq[:], AF.Square, r=t_kq.all, w=t_sq.all)
                    ps = self.nps()
                    self.mm(ps[:, 0:128], BD64[:], t_sq[:], True, True, r=BD64.all + t_sq.all, w=ps.all)
                    self.o_ts("dve", t_nr[:], ps[:, 0:128], 1e-24, None, ALU.max, None, r=ps.all, w=t_nr.all)
                    self.o_act(t_nr[:], t_nr[:], AF.Ln, r=t_nr.all, w=t_nr.all)
                    self.o_act(t_sq[:], t_nr[:], AF.Exp, r=t_nr.all, w=t_sq.all, scale=-0.5)
                    self.o_tt("dve", t_kk[:], t_kq[:], t_sq[:], ALU.mult, r=t_kq.all + t_sq.all, w=t_kk.all)
                    self.o_ts("dve", t_rr[:], t_r[:], RK_[:, oc:oc + 1], None, ALU.mult, None, r=t_r.all + RK_.all, w=t_rr.all)
                    ps = self.nps()
                    self.mm(ps[:, 0:128], G2a[:, fs], L1g[:], True, False, r=G2a.all + L1g.all, w=ps.all)
                    self.mm(ps[:, 0:128], G2b[:, fs], L1g2[:], False, True, r=G2b.all + L1g2.all, w=ps.all)
                    tg = (t_x1, t_x2)[oc % 2]
                    self.o_cp("act", tg[:], ps[:, 0:128], r=ps.all, w=tg.all)
                    P.dma(S_g[0][:, oc, t0:t0 + 128], tg[:], r=tg.all, w=[S_g[1]])
                    for dd in range(2):
                        ps = self.nps()
                        self.mm(ps[:, 0:128], A2[:, dd, fs], L1a[dd][:], True, True, r=A2.all + L1a[dd].all, w=ps.all)
                        self.o_act(t_a[:], ps[:, 0:128], AF.Sigmoid, r=ps.all + A0.all, w=t_a.all, bias=A0[:, dd, oc:oc + 1])
                        self.o_ts("dve", t_t[:], t_a[:], 1.0, KA_[:, oc:oc + 1], ALU.subtract, ALU.mult, r=t_a.all + KA_.all, w=t_t.all)
                        self.o_stt(t_kd[:], t_t[:], 1.0, t_k[:], ALU.add, ALU.mult, r=t_t.all + t_k.all, w=t_kd.all)
                        self.o_tt("dve", t_b[:], t_kk[:], t_a[:], ALU.mult, r=t_kk.all + t_a.all, w=t_b.all)
                        ps = self.nps()
                        self.mm(ps[:, 0:256], LW[:, dd * 1024 + oc * 128:dd * 1024 + (oc + 1) * 128], TRI[dd][:], True, True, r=HX.all + TRI[dd].all, w=ps.all)
                        self.o_act(t_ep[:], ps[:, 0:128], AF.Exp, r=ps.all, w=t_ep.all)
                        self.o_act(t_em[:], ps[:, 0:128], AF.Exp, r=ps.all, w=t_em.all, scale=-1.0)
                        self.o_act(t_epv[:], ps[:, 128:256], AF.Exp, r=ps.all, w=t_epv.all)
                        for cc in range(2):
                            col = cc * 64 + (63 if dd == 0 else 0)
                            self.o_cp("act", O_pc[dd][:, cc, oc:oc + 1], t_ep[:, col:col + 1], r=t_ep.all, w=O_pc[dd].all)
                        self.o_stt(O_ar[dd][:, oc, 0, :], t_kk[:], -1.0, t_epv[:], ALU.mult, ALU.mult, r=t_kk.all + t_epv.all, w=O_ar[dd].all)
                        self.o_tt("dve", O_ar[dd][:, oc, 1, :], t_r[:], t_ep[:], ALU.mult, r=t_r.all + t_ep.all, w=O_ar[dd].all)
                        self.o_tt("dve", O_b[dd][:, oc, :], t_b[:], t_em[:], ALU.mult, r=t_b.all + t_em.all, w=O_b[dd].all)
                        self.o_tt("pool", O_k[dd][:, oc, :], t_kd[:], t_em[:], ALU.mult, r=t_kd.all + t_em.all, w=O_k[dd].all)
                        if dd == 0:
                            self.o_tt("dve", t_sb[:], t_rr[:], t_kd[:], ALU.mult, r=t_rr.all + t_kd.all, w=t_sb.all)
                        else:
                            self.o_tt("dve", t_sb2[:], t_rr[:], t_kd[:], ALU.mult, r=t_rr.all + t_kd.all, w=t_sb2.all)
                            self.o_tt("pool", t_sb[:], t_sb[:], t_sb2[:], ALU.add, r=t_sb.all + t_sb2.all, w=t_sb.all)
                    ps = self.nps()
                    self.mm(ps[:, 0:128], BD64[:], t_sb[:], True, True, r=BD64.all + t_sb.all, w=ps.all)
                    tb = (t_x3, t_x4)[oc % 2]
                    self.o_tt("dve", tb[:], ps[:, 0:128], t_v[:], ALU.mult, r=ps.all + t_v.all, w=tb.all)
                    P.dma(S_bn[0][:, oc, t0:t0 + 128], tb[:], r=tb.all, w=[S_bn[1]])
                ts_ = slice(t0, t0 + 128)
                for dd in range(2):
                    P.dma(S_ar[dd][0][:, :, :, ts_], O_ar[dd][:], r=O_ar[dd].all, w=[S_ar[dd][1]])
                    P.dma(S_b[dd][0][:, :, ts_], O_b[dd][:], r=O_b[dd].all, w=[S_b[dd][1]])
                    P.dma(S_k[dd][0][:, :, ts_], O_k[dd][:], r=O_k[dd].all, w=[S_k[dd][1]])
                    for cc in range(2):
                        P.dma(S_pc[dd][0][2 * ti + cc], O_pc[dd][:, cc, :], r=O_pc[dd].all, w=[S_pc[dd][1]])
                P.dma(S_v[0][:, :, ts_], O_v[:], r=O_v.all, w=[S_v[1]])
        P.barrier()
        import os
        if os.environ.get("RWKV_STOP") == "A":
            return

        with ExitStack() as es:
            IST = P.sb([128, 64], BF16, es=es)
            MSK = [P.sb([128, 512], BF16, es=es) for _ in range(2)]
            LMSK = [P.sb([128, 512], BF16, es=es) for _ in range(2)]
            BD64 = P.sb([128, 128], F32, es=es)
            LNW = P.sb([128, 8], F32, es=es)
            LNB = P.sb([128, 8], F32, es=es)
            Wo = P.sb([128, 8, 1024], BF16, es=es)
            P.dma(BD64[:], d["c_bd64"], w=BD64.all)
            P.dma(LNW[:], d["rwkv_lnx_w"][j].rearrange("(c p) -> p c", p=128), w=LNW.all, allow_slow_non_contiguous=True)
            P.dma(LNB[:], d["rwkv_lnx_b"][j].rearrange("(c p) -> p c", p=128), w=LNB.all, allow_slow_non_contiguous=True)
            with ExitStack() as esw:
                stg = [P.sb([128, 2048], F32, es=esw) for _ in range(2)]
                self.load_w(Wo, lambda k0, k1: Wo[:, k0:k1, :], d["rwkv_w_o"][j], 8, 1024, stg)
                P.dma(stg[0][:, 0:64], d["c_ist"], w=stg[0].all)
                self.o_cp("dve", IST[:], stg[0][:, 0:64], r=stg[0].all, w=IST.all)
                for dd, (mk, lk) in enumerate((("c_mask_f", "c_lmask_f"), ("c_mask_b", "c_lmask_b"))):
                    P.dma(stg[1][:, 0:512], d[mk], w=stg[1].all)
                    self.o_cp("dve", MSK[dd][:], stg[1][:, 0:512], r=stg[1].all, w=MSK[dd].all)
                    P.dma(stg[1][:, 512:1024], d[lk], w=stg[1].all)
                    self.o_cp("dve", LMSK[dd][:], stg[1][:, 512:1024], r=stg[1].all, w=LMSK[dd].all)
                P.barrier()
            def bdtile(es_):
                t = P.sb([128, 8, 128], BF16, es=es_)
                P.add("pool", lambda e: e.memset(t[:], 0.0), w=t.all)
                return t

            S_y2 = [(S_y[0], S_y[1]), scr("rw_y2", [128, 8, S], F32)]
            ess = ExitStack()
            TS = []
            for dd in range(2):
                t = {}
                t["I_ar"] = [P.sb([128, 8, 128], BF16, es=ess) for _ in range(2)]
                for nm in ("I_abd", "I_bbd", "I_kbd", "I_vbd"):
                    t[nm] = [bdtile(ess) for _ in range(2)]
                t["I_b"] = [P.sb([128, 8, 64], BF16, es=ess) for _ in range(2)]
                t["I_pc"] = [P.sb([128, 8], F32, es=ess) for _ in range(2)]
                t["V_st"] = P.sb([128, 8, 64], BF16, es=ess)
                for nm in ("V_bd", "Bt_bd", "Kt_bd", "M_bd", "Aak_bd", "L_bd", "U_bd", "H_bd"):
                    t[nm] = bdtile(ess)
                t["ATB"] = P.sb([128, 8, 128], BF16, es=ess)
                t["ATK"] = P.sb([128, 8, 128], BF16, es=ess)
                t["LX"] = P.sb([128, 8, 128], BF16, es=ess)
                t["M_st"] = P.sb([128, 8, 64], BF16, es=ess)
                t["Xf"] = P.sb([128, 8, 64], F32, es=ess)
                t["H_f"] = P.sb([128, 8, 64], F32, es=ess)
                t["H_bf"] = P.sb([128, 8, 64], BF16, es=ess)
                t["TMPH"] = P.sb([128, 8, 64], F32, es=ess)
                t["Yt"] = P.sb([128, 8, 64], F32, es=ess)
                TS.append(t)

            def bd_write(dst, src_lo, src_hi, r, eng0="dve", eng1="act"):
                self.o_cp(eng0, dst[0:64, :, 0:64], src_lo, r=r, w=dst.all)
                self.o_cp(eng1, dst[64:128, :, 64:128], src_hi, r=r, w=dst.all)

            def load_chunk(dd, ch, buf):
                t = TS[dd]
                ts_ = slice(ch * 64, (ch + 1) * 64)
                P.dma(t["I_ar"][buf][:].rearrange("p c (e t) -> p c e t", e=2), S_ar[dd][0][:, :, :, ts_], r=[S_ar[dd][1]], w=t["I_ar"][buf].all)
                for e_ in range(2):
                    ps_ = slice(e_ * 64, (e_ + 1) * 64)
                    fs_ = slice(e_ * 64, (e_ + 1) * 64)
                    P.dma(t["I_abd"][buf][ps_, :, fs_], S_ar[dd][0][ps_, :, 0, ts_], r=[S_ar[dd][1]], w=t["I_abd"][buf].all)
                    P.dma(t["I_bbd"][buf][ps_, :, fs_], S_b[dd][0][ps_, :, ts_], r=[S_b[dd][1]], w=t["I_bbd"][buf].all)
                    P.dma(t["I_kbd"][buf][ps_, :, fs_], S_k[dd][0][ps_, :, ts_], r=[S_k[dd][1]], w=t["I_kbd"][buf].all)
                    P.dma(t["I_vbd"][buf][ps_, :, fs_], S_v[0][ps_, :, ts_], r=[S_v[1]], w=t["I_vbd"][buf].all)
                P.dma(t["I_b"][buf][:], S_b[dd][0][:, :, ts_], r=[S_b[dd][1]], w=t["I_b"][buf].all)
                P.dma(t["I_pc"][buf][:], S_pc[dd][0][ch], r=[S_pc[dd][1]], w=t["I_pc"][buf].all)

            bank_ctr = [0, 0]

            def chunk_gen(dd, ch, buf):
                t = TS[dd]
                ar, abd, bbd, kbd, vbd, bst, pc = (t["I_ar"][buf], t["I_abd"][buf], t["I_bbd"][buf], t["I_kbd"][buf], t["I_vbd"][buf], t["I_b"][buf], t["I_pc"][buf])
                V_st, V_bd, Bt_bd, Kt_bd, M_bd, Aak_bd, L_bd, U_bd, H_bd = (t[k_] for k_ in ("V_st", "V_bd", "Bt_bd", "Kt_bd", "M_bd", "Aak_bd", "L_bd", "U_bd", "H_bd"))
                ATB, ATK, LX, M_st, Xf, H_f, H_bf, TMPH, Yt = (t[k_] for k_ in ("ATB", "ATK", "LX", "M_st", "Xf", "H_f", "H_bf", "TMPH", "Yt"))
                tsl = slice(ch * 64, (ch + 1) * 64)

                def nb():
                    bank_ctr[dd] += 1
                    return self.ps[dd * 4 + bank_ctr[dd] % 4]
                psv, psb, psk = nb(), nb(), nb()
                for c in range(8):
                    self.mm(psv[:, c * 64:(c + 1) * 64], vbd[:, c, :], IST[:], True, True, r=vbd.all + IST.all, w=psv.all)
                    self.mm(psb[:, c * 64:(c + 1) * 64], bbd[:, c, :], IST[:], True, True, r=bbd.all + IST.all, w=psb.all)
                    self.mm(psk[:, c * 64:(c + 1) * 64], kbd[:, c, :], IST[:], True, True, r=kbd.all + IST.all, w=psk.all)
                yield
                v3 = psv[:, :].rearrange("p (c v) -> p c v", v=64)
                self.o_cp("act", V_st[:], v3, r=psv.all, w=V_st.all)
                bd_write(V_bd, v3[0:64], v3[64:128], psv.all, "dve", "act")
                b3 = psb[:, :].rearrange("p (c v) -> p c v", v=64)
                bd_write(Bt_bd, b3[0:64], b3[64:128], psb.all, "dve", "act")
                k3 = psk[:, :].rearrange("p (c v) -> p c v", v=64)
                bd_write(Kt_bd, k3[0:64], k3[64:128], psk.all, "dve", "act")
                yield
                pb = [nb(), nb()]
                pl = nb()
                for c in range(8):
                    cs = slice((c % 4) * 128, (c % 4 + 1) * 128)
                    self.mm(pb[c // 4][:, cs], bbd[:, c, :], ar[:, c, :], True, True, r=bbd.all + ar.all, w=pb[c // 4].all)
                    self.mm(pl[:, c * 64:(c + 1) * 64], abd[:, c, :], bst[:, c, :], True, True, r=abd.all + bst.all, w=pl.all)
                yield
                for hb in range(2):
                    self.o_tt("dve", ATB[:, hb * 4:(hb + 1) * 4, :], pb[hb][:, :].rearrange("p (c t) -> p c t", t=128), MSK[dd][:, :].rearrange("p (c t) -> p c t", t=128),
                              ALU.mult, r=pb[hb].all + MSK[dd].all, w=ATB.all)
                self.o_tt("dve", LX[:, :, 0:64], pl[:, :].rearrange("p (c t) -> p c t", t=64), LMSK[dd][:, :].rearrange("p (c t) -> p c t", t=64), ALU.mult,
                          r=pl.all + LMSK[dd].all, w=LX.all)
                yield
                pk = [nb(), nb()]
                for c in range(8):
                    cs = slice((c % 4) * 128, (c % 4 + 1) * 128)
                    self.mm(pk[c // 4][:, cs], kbd[:, c, :], ar[:, c, :], True, True, r=kbd.all + ar.all, w=pk[c // 4].all)
                yield
                for hb in range(2):
                    self.o_tt("dve", ATK[:, hb * 4:(hb + 1) * 4, :], pk[hb][:, :].rearrange("p (c t) -> p c t", t=128), MSK[dd][:, :].rearrange("p (c t) -> p c t", t=128),
                              ALU.mult, r=pk[hb].all + MSK[dd].all, w=ATK.all)
                yield
                bd_write(M_bd, ATB[0:64, :, 0:64], ATB[64:128, :, 0:64], ATB.all, "act", "pool")
                bd_write(Aak_bd, ATK[0:64, :, 0:64], ATK[64:128, :, 0:64], ATK.all, "act", "dve")
                self.o_cp("act", M_st[:], ATB[:, :, 0:64], r=ATB.all, w=M_st.all)
                bd_write(L_bd, LX[0:64, :, 0:64], LX[64:128, :, 0:64], LX.all, "dve", "act")
                yield
                pw = nb()
                for c in range(8):
                    self.mm(pw[:, c * 64:(c + 1) * 64], abd[:, c, :], H_bf[:, c, :], True, False, r=abd.all + H_bf.all, w=pw.all)
                    self.mm(pw[:, c * 64:(c + 1) * 64], Aak_bd[:, c, :], V_st[:, c, :], False, True, r=Aak_bd.all + V_st.all, w=pw.all)
                yield
                w3 = pw[:, :].rearrange("p (c v) -> p c v", v=64)
                self.o_cp("act", Xf[:], w3, r=pw.all, w=Xf.all)
                self.o_cp("dve", LX[:, :, 64:128], w3, r=pw.all, w=LX.all)
                yield
                for lev in range(6):
                    last = lev == 5
                    pa = [nb(), nb()]
                    pbm = nb()
                    for c in range(8):
                        cs = slice((c % 4) * 128, (c % 4 + 1) * 128)
                        if last:
                            self.mm(pa[c // 4][:, (c % 4) * 128 + 64:(c % 4 + 1) * 128], M_bd[:, c, :], LX[:, c, 64:128], True, True, r=M_bd.all + LX.all, w=pa[c // 4].all)
                        else:
                            self.mm(pa[c // 4][:, cs], M_bd[:, c, :], LX[:, c, :], True, True, r=M_bd.all + LX.all, w=pa[c // 4].all)
                            self.mm(pbm[:, c * 64:(c + 1) * 64], L_bd[:, c, :], M_st[:, c, :], True, True, r=L_bd.all + M_st.all, w=pbm.all)
                    yield
                    for hb in range(2):
                        a3 = pa[hb][:, :].rearrange("p (c t) -> p c t", t=128)
                        self.o_tt("dve", Xf[:, hb * 4:(hb + 1) * 4, :], Xf[:, hb * 4:(hb + 1) * 4, :], a3[:, :, 64:128], ALU.add, r=pa[hb].all + Xf.all, w=Xf.all)
                    if not last:
                        for hb in range(2):
                            a3 = pa[hb][:, :].rearrange("p (c t) -> p c t", t=128)
                            if lev < 4:
                                self.o_cp("act", LX[:, hb * 4:(hb + 1) * 4, 0:64], a3[:, :, 0:64], r=pa[hb].all, w=LX.all)
                                self.o_cp("dve", L_bd[0:64, hb * 4:(hb + 1) * 4, 0:64], a3[0:64, :, 0:64], r=pa[hb].all, w=L_bd.all)
                                self.o_cp("act", L_bd[64:128, hb * 4:(hb + 1) * 4, 64:128], a3[64:128, :, 0:64], r=pa[hb].all, w=L_bd.all)
                        m3 = pbm[:, :].rearrange("p (c t) -> p c t", t=64)
                        self.o_cp("act", M_st[:], m3, r=pbm.all, w=M_st.all)
                        bd_write(M_bd, m3[0:64], m3[64:128], pbm.all, "dve", "act")
                    self.o_cp("act", LX[:, :, 64:128], Xf[:], r=Xf.all, w=LX.all)
                    yield
                bd_write(U_bd, LX[0:64, :, 64:128], LX[64:128, :, 64:128], LX.all, "dve", "act")
                yield
                py = nb()
                for c in range(8):
                    cs = slice(c * 64, (c + 1) * 64)
                    self.mm(py[:, cs], H_bd[:, c, :], ar[:, c, 64:128], True, False, r=H_bd.all + ar.all, w=py.all)
                    self.mm(py[:, cs], U_bd[:, c, :], ATB[:, c, 64:128], False, False, r=U_bd.all + ATB.all, w=py.all)
                    self.mm(py[:, cs], V_bd[:, c, :], ATK[:, c, 64:128], False, True, r=V_bd.all + ATK.all, w=py.all)
                y3 = py[:, :].rearrange("p (c t) -> p c t", t=64)
                ph = nb()
                for c in range(8):
                    cs = slice(c * 64, (c + 1) * 64)
                    self.mm(ph[:, cs], Bt_bd[:, c, :], LX[:, c, 64:128], True, False, r=Bt_bd.all + LX.all, w=ph.all)
                    self.mm(ph[:, cs], Kt_bd[:, c, :], V_st[:, c, :], False, True, r=Kt_bd.all + V_st.all, w=ph.all)
                yield
                h3 = ph[:, :].rearrange("p (c v) -> p c v", v=64)
                self.o_cp("act", Yt[:], y3, r=py.all, w=Yt.all)
                P.dma(S_y2[dd][0][:, :, tsl], Yt[:], r=Yt.all, w=[S_y2[dd][1]])
                self.o_tt("dve", TMPH[:], h3, H_f[:], ALU.add, r=ph.all + H_f.all, w=TMPH.all)
                self.o_tt("dve", H_f[:], TMPH[:], pc[:, :].unsqueeze(2).to_broadcast([128, 8, 64]), ALU.mult, r=TMPH.all + pc.all, w=H_f.all)
                yield
                self.o_cp("act", H_bf[:], H_f[:], r=H_f.all, w=H_bf.all)
                bd_write(H_bd, H_f[0:64, :, :], H_f[64:128, :, :], H_f.all, "dve", "pool")
                yield

            from itertools import zip_longest
            nlim = int(os.environ.get("RWKV_NCH", str(NCH)))
            orders = [list(range(NCH))[:nlim], list(range(NCH - 1, -1, -1))[:nlim]]
            for dd in range(2):
                t = TS[dd]
                P.add("pool", lambda e, t=t: e.memset(t["H_f"][:], 0.0), w=t["H_f"].all)
                P.add("pool", lambda e, t=t: e.memset(t["H_bf"][:], 0.0), w=t["H_bf"].all)
                load_chunk(dd, orders[dd][0], 0)
            for oi in range(len(orders[0])):
                buf = oi % 2
                gens = []
                for dd in range(2):
                    if oi + 1 < len(orders[dd]):
                        load_chunk(dd, orders[dd][oi + 1], 1 - buf)
                    gens.append(chunk_gen(dd, orders[dd][oi], buf))
                for _ in zip_longest(*gens):
                    pass
            P.barrier()
            ess.close()

            NB = 2
            I_yf = [P.sb([128, 8, 64], F32, es=es) for _ in range(NB)]
            I_yb = [P.sb([128, 8, 64], F32, es=es) for _ in range(NB)]
            I_g = [P.sb([128, 8, 64], F32, es=es) for _ in range(NB)]
            I_bn = [P.sb([128, 8, 64], F32, es=es) for _ in range(NB)]
            W_ = [[P.sb([128, 8, 64], F32, es=es) for _ in range(4)] for _ in range(NB)]
            Z_ = [P.sb([128, 8, 64], BF16, es=es) for _ in range(NB)]

            def load_post(ch, b_):
                ts_ = slice(ch * 64, (ch + 1) * 64)
                P.dma(I_yf[b_][:], S_y2[0][0][:, :, ts_], r=[S_y2[0][1]], w=I_yf[b_].all)
                P.dma(I_yb[b_][:], S_y2[1][0][:, :, ts_], r=[S_y2[1][1]], w=I_yb[b_].all)
                P.dma(I_g[b_][:], S_g[0][:, :, ts_], r=[S_g[1]], w=I_g[b_].all)
                P.dma(I_bn[b_][:], S_bn[0][:, :, ts_], r=[S_bn[1]], w=I_bn[b_].all)

            load_post(0, 0)
            for ch in range(NCH):
                b_ = ch % NB
                if ch + 1 < NCH:
                    load_post(ch + 1, (ch + 1) % NB)
                Yt, Y2, Y3, Y4 = W_[b_]
                Zt = Z_[b_]
                tsl = slice(ch * 64, (ch + 1) * 64)
                self.o_tt("dve", Yt[:], I_yf[b_][:], I_yb[b_][:], ALU.add, r=I_yf[b_].all + I_yb[b_].all, w=Yt.all)
                pm = self.nps()
                self.mm(pm[:, :], BD64[:], Yt[:].rearrange("p c t -> p (c t)"), True, True, r=BD64.all + Yt.all, w=pm.all)
                self.o_stt(Y2[:], pm[:, :].rearrange("p (c t) -> p c t", t=64), -1.0 / 64, Yt[:], ALU.mult, ALU.add, r=pm.all + Yt.all, w=Y2.all)
                self.o_act(Y3[:], Y2[:], AF.Square, r=Y2.all, w=Y3.all)
                pv = self.nps()
                self.mm(pv[:, :], BD64[:], Y3[:].rearrange("p c t -> p (c t)"), True, True, r=BD64.all + Y3.all, w=pv.all)
                self.o_act(Y3[:].rearrange("p c t -> p (c t)"), pv[:, :], AF.Ln, r=pv.all, w=Y3.all, bias=64e-5, scale=1.0 / 64)
                self.o_act(Y4[:], Y3[:], AF.Exp, r=Y3.all, w=Y4.all, scale=-0.5)
                self.o_tt("dve", Y2[:], Y2[:], Y4[:], ALU.mult, r=Y2.all + Y4.all, w=Y2.all)
                self.o_tt("pool", Y2[:], Y2[:], LNW[:, :].unsqueeze(2).to_broadcast([128, 8, 64]), ALU.mult, r=Y2.all + LNW.all, w=Y2.all)
                self.o_tt("dve", Y2[:], Y2[:], LNB[:, :].unsqueeze(2).to_broadcast([128, 8, 64]), ALU.add, r=Y2.all + LNB.all, w=Y2.all)
                self.o_tt("pool", Y2[:], Y2[:], I_bn[b_][:], ALU.add, r=Y2.all + I_bn[b_].all, w=Y2.all)
                self.o_tt("dve", Zt[:], Y2[:], I_g[b_][:], ALU.mult, r=Y2.all + I_g[b_].all, w=Zt.all)
                po = self.nps()
                for dc in range(8):
                    for c in range(8):
                        self.mm(po[:, dc * 64:(dc + 1) * 64], Wo[:, c, dc * 128:(dc + 1) * 128], Zt[:, c, :], c == 0, c == 7, r=Wo.all + Zt.all, w=po.all)
                xs = self.X[:, :, tsl]
                xres = [self.xr(c, ch // 8) for c in range(8)]
                self.o_tt("dve", xs, po[:, :].rearrange("p (c t) -> p c t", t=64), xs, ALU.add, r=po.all + xres, w=xres)
        P.barrier()


def build(stages, debug=False):
    nc = bass.Bass("TRN2", target_bir_lowering=False)
    dram = {}

    def din(name, shape):
        dram[name] = nc.dram_tensor(name, list(shape), F32, kind="ExternalInput").ap()

    din("x", [S, D])
    for name, shape in PARAM_SHAPES.items():
        din(name, shape)
    for name, shape in CONST_SHAPES.items():
        din(name, shape)
    dram["y"] = nc.dram_tensor("y", [S, D], F32, kind="ExternalOutput").ap()
    with ExitStack() as es:
        P = Prog(nc, es)
        kb = KB(nc, es, P, dram)
        kb.debug = debug
        with ExitStack() as es2:
            kb.load_x(es2)
        P.barrier()
        for st in stages:
            kind, li = st
            if kind == "ffn":
                kb.ffn(li)
            elif kind == "swa":
                kb.swa(li)
            elif kind == "mla":
                kb.mla(li)
            elif kind == "rwkv":
                kb.rwkv(li)
        with ExitStack() as es2:
            kb.store_x(es2)
        P.emit()
        print("prog stats", P.stats)
    return nc


PARAM_SHAPES = {
    "norm_tok": (4, 1024), "norm_ch": (4, 1024), "ffn_w_up": (4, 1024, 5632), "ffn_conv_w": (4, 3, 5632),
    "ffn_conv_b": (4, 5632), "ffn_w_down": (4, 2816, 1024),
    "swa_w_qkv": (2, 1024, 1536), "swa_q_gain": (2, 64), "swa_k_gain": (2, 64), "swa_sinks": (2, 16), "swa_w_o": (2, 1024, 1024),
    "rwkv_mu": (1, 6, 1024), "rwkv_w_r": (1, 1024, 1024), "rwkv_w_k": (1, 1024, 1024), "rwkv_w_v": (1, 1024, 1024),
    "rwkv_w0": (1, 2, 1024), "rwkv_w1": (1, 2, 1024, 64), "rwkv_w2": (1, 2, 64, 1024), "rwkv_a0": (1, 2, 1024),
    "rwkv_a1": (1, 2, 1024, 64), "rwkv_a2": (1, 2, 64, 1024), "rwkv_g1": (1, 1024, 160), "rwkv_g2": (1, 160, 1024),
    "rwkv_k_k": (1, 1024), "rwkv_k_a": (1, 1024), "rwkv_r_k": (1, 16, 64), "rwkv_lnx_w": (1, 1024), "rwkv_lnx_b": (1, 1024),
    "rwkv_w_o": (1, 1024, 1024),
    "mla_w_down": (1, 1024, 672), "mla_cq_gain": (1, 384), "mla_ckv_gain": (1, 256), "mla_w_uq": (1, 384, 1536),
    "mla_w_ukv": (1, 256, 2048), "mla_q_gain": (1, 96), "mla_k_gain": (1, 96), "mla_w_o": (1, 1024, 1024),
}


def make_consts():
    c = {}
    c["c_ident"] = np.eye(128, dtype=np.float32)
    theta = np.float32(500000.0)

    def tables(rot):
        inv = (theta ** (-np.arange(0, rot, 2, dtype=np.float32) / np.float32(rot))).astype(np.float32)
        ang = (np.arange(S, dtype=np.float32)[:, None] * inv[None, :]).astype(np.float32)
        return np.cos(ang).astype(np.float32), np.sin(ang).astype(np.float32)

    def rope_consts(dh, start, rot):
        cs, sn = tables(rot)
        half = rot // 2
        C = np.ones((dh, S), np.float32)
        Sn = np.zeros((dh, S), np.float32)
        Pm = np.zeros((dh, dh), np.float32)
        for i in range(half):
            C[start + i] = cs[:, i]
            C[start + half + i] = cs[:, i]
            Sn[start + i] = sn[:, i]
            Sn[start + half + i] = sn[:, i]
            Pm[start + i, start + half + i] = -1.0
            Pm[start + half + i, start + i] = 1.0
        return C, Sn, np.ascontiguousarray(Pm.T)

    c["c_swa_cos"], c["c_swa_sin"], c["c_swa_rot"] = rope_consts(64, 0, 16)
    c["c_mla_cos"], c["c_mla_sin"], r96 = rope_consts(96, 64, 32)
    rp = np.zeros((128, 128), np.float32)
    rp[:96, :96] = r96
    c["c_mla_rot"] = rp
    o96 = np.zeros((128, 128), np.float32)
    o96[:96, :96] = 1.0
    c["c_ones96"] = o96
    b = np.arange(128)[:, None]
    a = np.arange(128)[None, :]
    c["c_mlo"] = (b <= a).astype(np.float32)
    c["c_mhi"] = (a <= b).astype(np.float32)
    p = np.arange(128)
    c["c_bd64"] = (p[:, None] // 64 == p[None, :] // 64).astype(np.float32)
    same = (p[:, None] // 64 == p[None, :] // 64)
    dec = np.float32(-0.6065306597126334)
    c["c_tri_f"] = np.concatenate([(same & (p[:, None] <= p[None, :])), (same & (p[:, None] < p[None, :]))], axis=1).astype(np.float32) * dec
    c["c_tri_b"] = np.concatenate([(same & (p[:, None] >= p[None, :])), (same & (p[:, None] > p[None, :]))], axis=1).astype(np.float32) * dec
    s_ = (p % 64)[:, None]
    t_ = np.arange(64)[None, :]
    c["c_mask_f"] = np.tile(np.concatenate([s_ < t_, s_ <= t_], axis=1), (1, 4)).astype(np.float32)
    c["c_mask_b"] = np.tile(np.concatenate([s_ > t_, s_ >= t_], axis=1), (1, 4)).astype(np.float32)
    c["c_lmask_f"] = np.tile(t_ < s_, (1, 8)).astype(np.float32)
    c["c_lmask_b"] = np.tile(t_ > s_, (1, 8)).astype(np.float32)
    c["c_ist"] = (s_ == t_).astype(np.float32)
    ids = np.zeros((32, 96), np.float32)
    ids[np.arange(32), 64 + np.arange(32)] = 1.0
    c["c_mla_ids"] = ids
    return c


CONST_SHAPES = {k: v.shape for k, v in make_consts().items()}

FULL_STAGES = [("swa", 0), ("ffn", 0), ("rwkv", 0), ("ffn", 1), ("mla", 0), ("ffn", 2), ("swa", 1), ("ffn", 3)]


def run(inputs, stages, ncores=8, debug=False):
    nc = build(stages, debug)
    consts = make_consts()
    x = np.ascontiguousarray(inputs["x"], dtype=np.float32)
    in_maps = []
    for b in range(ncores):
        m = {"x": x[b]}
        for name in PARAM_SHAPES:
            m[name] = np.ascontiguousarray(inputs[name], dtype=np.float32)
        m.update(consts)
        in_maps.append(m)
    res = run_bass_kernel_spmd(nc, in_maps, core_ids=list(range(ncores)))
    return np.stack([r["y"] for r in res.results], axis=0)


def kernel(**inputs):
    return run(inputs, FULL_STAGES, 8).astype(np.float32)
```

```python
import numpy as np
import ml_dtypes
from contextlib import ExitStack
import concourse.bass as bass
import concourse.mybir as mybir
from concourse.bass_utils import run_bass_kernel_spmd

F32 = mybir.dt.float32
BF16 = mybir.dt.bfloat16
ALU = mybir.AluOpType
AF = mybir.ActivationFunctionType
AX = mybir.AxisListType

import os
SAME_ENGINE_SYNC = os.environ.get('SES', '0') == '1'
S = 2048
D = 1024
NT = 4
FF = 2816
EPS = 1e-6


class Res:
    __slots__ = ("last_w", "rc", "rd", "excl")

    def __init__(self):
        self.excl = False
        self.last_w = None
        self.rc = {}
        self.rd = []


class T:
    def __init__(self, h, nres=1):
        self.h = h
        self.rs = [Res() for _ in range(nres)]

    def __getitem__(self, idx):
        return self.h[idx]

    @property
    def all(self):
        return list(self.rs)


class Op:
    __slots__ = ("eng", "fn", "deps", "needed", "sig", "is_dma", "idx")


class Prog:
    ENGS = ("pe", "dve", "act", "pool", "sp")

    def __init__(self, nc, es, ndma_sems=8):
        self.nc = nc
        self.es = es
        self.ops = []
        self.ndma = ndma_sems
        self.n_alloc = 0

    def sb(self, shape, dt, nres=1, es=None):
        self.n_alloc += 1
        h = (es or self.es).enter_context(self.nc.sbuf_tensor(f"sb{self.n_alloc}", list(shape), dt))
        return T(h, nres)

    def ps(self, shape, dt=F32, nres=1, es=None):
        self.n_alloc += 1
        h = (es or self.es).enter_context(self.nc.psum_tensor(f"ps{self.n_alloc}", list(shape), dt))
        t = T(h, nres)
        for x in t.rs:
            x.excl = True
        return t

    def add(self, eng, fn, r=(), w=(), is_dma=False):
        op = Op()
        op.eng = eng
        op.fn = fn
        op.is_dma = is_dma
        op.needed = False
        op.sig = None
        op.idx = len(self.ops)
        deps = set()
        ex = [x for x in r if x.excl]
        if ex:
            r = [x for x in r if not x.excl]
            w = list(w) + [x for x in ex if x not in w]
        for x in r:
            if x.last_w is not None:
                deps.add(x.last_w)
        for x in w:
            if x.last_w is not None:
                deps.add(x.last_w)
            deps.update(x.rc.values())
            deps.update(x.rd)
        for x in r:
            if is_dma:
                x.rd.append(op.idx)
            else:
                x.rc[eng] = op.idx
        for x in w:
            x.last_w = op.idx
            x.rc = {}
            x.rd = []
        deps.discard(op.idx)
        op.deps = deps
        self.ops.append(op)
        return op

    def dma(self, out, in_, r=(), w=(), q="sp", **kw):
        return self.add(q, lambda e: e.dma_start(out=out, in_=in_, **kw), r, w, is_dma=True)

    def barrier(self):
        last = {}
        dmas = []
        for op in self.ops:
            if op.is_dma:
                dmas.append(op.idx)
            else:
                last[op.eng] = op.idx
        ids = set(last.values()) | set(dmas[-self.ndma:])
        for e in self.ENGS:
            op = self.add(e, lambda en: en.nop())
            op.deps = set(ids)

    def emit(self):
        nc = self.nc
        ops = self.ops
        for op in ops:
            for d in op.deps:
                ops[d].needed = True
        es = self.es
        EPOCH = 8000
        sems = {}
        dsems = [es.enter_context(nc.semaphore(f"s_dma{i}")) for i in range(self.ndma)]
        cnt = {e: 0 for e in self.ENGS}
        dcnt = [0] * self.ndma
        ndma = 0
        dma_prev = {}
        for op in ops:
            if op.is_dma:
                k = ndma % self.ndma
                ndma += 1
                dma_prev[op.idx] = (k, dcnt[k])
                dcnt[k] += 16
                op.sig = (("d", k), dcnt[k])
            elif op.needed:
                ep = cnt[op.eng] // EPOCH
                cnt[op.eng] += 1
                key = ("e", op.eng, ep)
                if key not in sems:
                    sems[key] = es.enter_context(nc.semaphore(f"s_{op.eng}_{ep}"))
                op.sig = (key, cnt[op.eng] - ep * EPOCH)
        self.stats = dict(cnt=dict(cnt), ndma=ndma, nops=len(ops), nsem=len(sems) + self.ndma)

        def semof(key):
            return dsems[key[1]] if key[0] == "d" else sems[key]

        byeng = {e: [op for op in ops if op.eng == e] for e in self.ENGS}

        def run(ename, eobj):
            known = {}
            for op in byeng[ename]:
                waits = {}
                for d in op.deps:
                    dop = ops[d]
                    if dop.eng == ename and not dop.is_dma and (ename == "pe" or not SAME_ENGINE_SYNC):
                        continue
                    key, v = dop.sig
                    if known.get(key, 0) >= v:
                        continue
                    if waits.get(key, 0) < v:
                        waits[key] = v
                if op.is_dma:
                    k, v = dma_prev[op.idx]
                    key = ("d", k)
                    if v > 0 and known.get(key, 0) < v and waits.get(key, 0) < v:
                        waits[key] = v
                for key, v in waits.items():
                    eobj.wait_ge(semof(key), v)
                    known[key] = v
                ins = op.fn(eobj)
                if op.sig is not None:
                    ins.then_inc(semof(op.sig[0]), 16 if op.is_dma else 1)
            if ename == "sp":
                for k in range(self.ndma):
                    if dcnt[k] > 0:
                        eobj.wait_ge(dsems[k], dcnt[k])

        with nc.Block() as block:
            @block.tensor
            def _(e):
                run("pe", e)

            @block.vector
            def _(e):
                run("dve", e)

            @block.scalar
            def _(e):
                run("act", e)

            @block.gpsimd
            def _(e):
                run("pool", e)

            @block.sync
            def _(e):
                run("sp", e)


class KB:
    def __init__(self, nc, es, P, dram):
        self.nc = nc
        self.es = es
        self.P = P
        self.d = dram
        P_ = P
        self.X = P_.sb([128, 8, S], F32, nres=8 * NT)
        self.ident = P_.sb([128, 128], F32)
        self.ones_bf = P_.sb([128, 128], BF16)
        self.ones_f = P_.sb([128, 128], F32)
        self.ps = [P_.ps([128, 512]) for _ in range(8)]
        self.psi = 0
        P_.dma(self.ident[:], dram["c_ident"], w=self.ident.all)
        P_.add("dve", lambda e: e.memset(self.ones_bf[:], 1.0), w=self.ones_bf.all)
        P_.add("dve", lambda e: e.memset(self.ones_f[:], 1.0), w=self.ones_f.all)
        self.cast_rr = 0

    def xr(self, c, tt):
        return self.X.rs[c * NT + tt]

    def xr_all(self):
        return self.X.all

    def nps(self):
        p = self.ps[self.psi % 6]
        self.psi += 1
        return p

    def mm(self, out, lhsT, rhs, start, stop, r, w):
        self.P.add("pe", lambda e: e.matmul(out, lhsT=lhsT, rhs=rhs, start=start, stop=stop), r=r, w=w)

    def cast(self, out, in_, r, w, eng=None):
        if eng is None:
            eng = ("pool", "act")[self.cast_rr % 2]
            self.cast_rr += 1
        if eng == "act":
            self.P.add("act", lambda e: e.copy(out=out, in_=in_), r=r, w=w)
        elif eng == "pool":
            self.P.add("pool", lambda e: e.tensor_copy(out=out, in_=in_), r=r, w=w)
        else:
            self.P.add("dve", lambda e: e.tensor_copy(out=out, in_=in_), r=r, w=w)

    def load_w(self, dst, dst_sl, src2d, kc, m, stg):
        P = self.P
        per = stg[0].h.shape[1]
        kstep = max(1, per // m)
        i = 0
        for k0 in range(0, kc, kstep):
            k1 = min(kc, k0 + kstep)
            st = stg[self.stg_rr % len(stg)]
            self.stg_rr += 1
            n = (k1 - k0) * m
            sv = st[:, 0:n].rearrange("p (k m) -> p k m", m=m)
            P.dma(sv, src2d[k0 * 128:k1 * 128, :].rearrange("(k p) m -> p k m", p=128), w=st.all)
            self.cast(dst_sl(k0, k1), sv, r=st.all, w=dst.all)
            i += 1

    stg_rr = 0

    def load_x(self, es):
        P = self.P
        xin = self.d["x"]
        tok = [P.sb([128, D], F32, es=es) for _ in range(2)]
        for ti in range(16):
            tk = tok[ti % 2]
            P.dma(tk[:], xin[ti * 128:(ti + 1) * 128, :], w=tk.all)
            for half in range(2):
                ps = self.nps()
                for j in range(4):
                    c = half * 4 + j
                    P.add("pe", lambda e, ps=ps, j=j, c=c, tk=tk: e.transpose(ps[:, j * 128:(j + 1) * 128], tk[:, c * 128:(c + 1) * 128], self.ident[:]),
                          r=tk.all + self.ident.all, w=ps.all)
                tt = ti // 4
                o = self.X[:, half * 4:(half + 1) * 4, ti * 128:(ti + 1) * 128]
                i = ps[:, :].rearrange("p (j t) -> p j t", t=128)
                eng = "dve" if half == 0 else "act"
                if eng == "dve":
                    P.add("dve", lambda e, o=o, i=i: e.tensor_copy(out=o, in_=i), r=ps.all, w=[self.xr(c, tt) for c in range(half * 4, half * 4 + 4)])
                else:
                    P.add("act", lambda e, o=o, i=i: e.copy(out=o, in_=i), r=ps.all, w=[self.xr(c, tt) for c in range(half * 4, half * 4 + 4)])

    def store_x(self, es):
        P = self.P
        yout = self.d["y"]
        tok = [P.sb([128, D], F32, es=es) for _ in range(2)]
        for ti in range(16):
            tk = tok[ti % 2]
            tt = ti // 4
            for half in range(2):
                ps = self.nps()
                for j in range(4):
                    c = half * 4 + j
                    P.add("pe", lambda e, ps=ps, j=j, c=c, ti=ti: e.transpose(ps[:, j * 128:(j + 1) * 128], self.X[:, c, ti * 128:(ti + 1) * 128], self.ident[:]),
                          r=[self.xr(c, tt)] + self.ident.all, w=ps.all)
                o = tk[:, half * 512:(half + 1) * 512]
                if half == 0:
                    P.add("dve", lambda e, o=o, ps=ps: e.tensor_copy(out=o, in_=ps[:, :]), r=ps.all, w=tk.all)
                else:
                    P.add("act", lambda e, o=o, ps=ps: e.copy(out=o, in_=ps[:, :]), r=ps.all, w=tk.all)
            P.dma(yout[ti * 128:(ti + 1) * 128, :], tk[:], r=tk.all)

    def rmsnorm(self, H, gain_ap, es, t0=0, t1=S, hoff=0, rstd_out=None):
        P = self.P
        CW = getattr(es, "_rn_cw", 512)
        if not hasattr(es, "_rn_tmp"):
            es._rn_tmp = (P.sb([128, 8, CW], BF16, es=es), P.sb([128, CW], F32, es=es), P.sb([128, CW], F32, es=es))
        sq, rs, rs2 = es._rn_tmp
        for a in range(t0, t1, CW):
            b = min(t1, a + CW)
            n = b - a
            tts = sorted(set([a // 512, (b - 1) // 512]))
            xr = [self.xr(c, tt) for c in range(8) for tt in tts]
            P.add("act", lambda e, a=a, b=b, n=n: e.activation(out=sq[:, :, 0:n], in_=self.X[:, :, a:b], func=AF.Square), r=xr, w=sq.all)
            ps = self.nps()
            for c in range(8):
                self.mm(ps[:, 0:n], self.ones_bf[:], sq[:, c, 0:n], c == 0, c == 7, r=sq.all + self.ones_bf.all, w=ps.all)
            P.add("act", lambda e, n=n, ps=ps: e.activation(out=rs[:, 0:n], in_=ps[:, 0:n], func=AF.Ln, bias=EPS, scale=1.0 / D), r=ps.all, w=rs.all)
            if rstd_out is not None:
                P.add("act", lambda e, n=n, a=a, b=b: e.activation(out=rstd_out[:, a:b], in_=rs[:, 0:n], func=AF.Exp, scale=-0.5), r=rs.all, w=rstd_out.all)
                continue
            P.add("act", lambda e, n=n: e.activation(out=rs2[:, 0:n], in_=rs[:, 0:n], func=AF.Exp, scale=-0.5), r=rs.all, w=rs2.all)
            for c in range(8):
                P.add("dve", lambda e, c=c, a=a, b=b, n=n: e.scalar_tensor_tensor(
                    out=H[:, c, hoff + a - t0:hoff + b - t0], in0=self.X[:, c, a:b], scalar=gain_ap[:, c:c + 1], in1=rs2[:, 0:n],
                    op0=ALU.mult, op1=ALU.mult), r=[self.xr(c, tt) for tt in tts] + rs2.all, w=H.all)

    def load_vec8(self, dst, src1d, q="sp"):
        self.P.dma(dst, src1d.rearrange("(c p) -> p c", p=128), w=[], q=q, allow_slow_non_contiguous=True)

    def ffn(self, li):
        P = self.P
        d = self.d
        with ExitStack() as es:
            gains = P.sb([128, 8], F32, es=es)
            P.dma(gains[:], d["norm_ch"][li].rearrange("(c p) -> p c", p=128), w=gains.all, allow_slow_non_contiguous=True)
            cw = P.sb([128, 3, 44], F32, es=es)
            cb = P.sb([128, 44], F32, es=es)
            for t in range(3):
                P.dma(cw[:, t, :], d["ffn_conv_w"][li, t].rearrange("(c p) -> p c", p=128), w=cw.all, allow_slow_non_contiguous=True)
            P.dma(cb[:], d["ffn_conv_b"][li].rearrange("(c p) -> p c", p=128), w=cb.all, allow_slow_non_contiguous=True)
            es._rn_cw = 256
            H = P.sb([128, 8, 1026], BF16, es=es)
            Hh = P.sb([128, 8, 2], BF16, es=es)
            G = P.sb([128, 22, 1024], BF16, es=es, nres=22)
            U2 = [[P.sb([128, 1026], F32, es=es) for _ in range(2)] for _ in range(2)]
            A2 = [[P.sb([128, 1024], F32, es=es) for _ in range(2)] for _ in range(2)]
            wstg = [P.sb([128, 1024], F32, es=es) for _ in range(2)]
            wup = [[P.sb([128, 8, 128], BF16, es=es) for _ in range(2)] for _ in range(2)]
            wdn = [P.sb([128, 22, 128], BF16, es=es) for _ in range(2)]
            wdstg = P.sb([128, 11, 128], F32, es=es)
            w_up = d["ffn_w_up"][li]
            w_dn = d["ffn_w_down"][li]
            for half in range(2):
                hb = half * 1024
                lo = max(0, hb - 1)
                hi = min(S, hb + 1025)
                c0 = lo - (hb - 1)
                ncol = hi - lo
                if half == 0:
                    self.rmsnorm(H, gains, es, lo, hi, hoff=c0)
                    P.add("pool", lambda e: e.tensor_copy(out=Hh[:, :, 0:1], in_=H[:, :, 1024:1025]), r=H.all, w=Hh.all)
                else:
                    self.rmsnorm(H, gains, es, 1024, 2048, hoff=1)
                    P.add("pool", lambda e: e.tensor_copy(out=H[:, :, 0:1], in_=Hh[:, :, 0:1]), r=Hh.all, w=H.all)
                for j in range(22):
                    wb = wup[j % 2]
                    U = U2[j % 2]
                    A = A2[j % 2]
                    for part in range(2):
                        col0 = part * FF + j * 128
                        st = wstg[part]
                        sv = st[:, :].rearrange("p (k m) -> p k m", m=128)
                        if not (os.environ.get("FFN_SKIPDMA") == "1" and j % 4 != 0):
                            P.dma(sv, w_up[:, col0:col0 + 128].rearrange("(k p) m -> p k m", p=128), w=st.all)
                        self.cast(wb[part][:], sv, r=st.all, w=wb[part].all, eng=("dve", "act")[part])
                    for part in range(2):
                        fc = part * 22 + j
                        u = U[part]
                        if half == 0:
                            P.add("pool", lambda e, u=u: e.memset(u[:, 0:1], 0.0), w=u.all)
                        else:
                            P.add("pool", lambda e, u=u: e.memset(u[:, 1025:1026], 0.0), w=u.all)
                        segs = [(0, 512), (512, 1024), (1024, ncol)]
                        pss = [self.nps() for _ in segs]
                        for k in range(8):
                            for (a, b), ps in zip(segs, pss):
                                self.mm(ps[:, 0:b - a], wb[part][:, k, :], H[:, k, c0 + a:c0 + b], k == 0, k == 7,
                                        r=wb[part].all + H.all, w=ps.all)
                        for (a, b), ps in zip(segs, pss):
                            P.add("act", lambda e, u=u, a=a, b=b, ps=ps, c0=c0: e.copy(out=u[:, c0 + a:c0 + b], in_=ps[:, 0:b - a]), r=ps.all, w=u.all)
                        acc = A[part]
                        P.add("act", lambda e, u=u, acc=acc, fc=fc: e.activation(out=acc[:], in_=u[:, 1:1025], func=AF.Identity,
                                                                                 bias=cb[:, fc:fc + 1], scale=cw[:, 1, fc:fc + 1]),
                              r=u.all + cb.all + cw.all, w=acc.all)
                        P.add("dve", lambda e, u=u, acc=acc, fc=fc: e.scalar_tensor_tensor(out=acc[:], in0=u[:, 0:1024], scalar=cw[:, 0, fc:fc + 1], in1=acc[:],
                                                                                           op0=ALU.mult, op1=ALU.add), r=u.all + cw.all, w=acc.all)
                        P.add("dve", lambda e, u=u, acc=acc, fc=fc: e.scalar_tensor_tensor(out=acc[:], in0=u[:, 2:1026], scalar=cw[:, 2, fc:fc + 1], in1=acc[:],
                                                                                           op0=ALU.mult, op1=ALU.add), r=u.all + cw.all, w=acc.all)
                    P.add("act", lambda e, A=A: e.activation(out=A[0][:], in_=A[0][:], func=AF.Silu), r=A[0].all, w=A[0].all)
                    P.add("pool", lambda e, j=j, A=A: e.tensor_tensor(out=G[:, j, :], in0=A[0][:], in1=A[1][:], op=ALU.mult), r=A[0].all + A[1].all, w=[G.rs[j]])
                for dc in range(8):
                    wd = wdn[dc % 2]
                    for hh_ in range(2):
                        P.dma(wdstg[:], w_dn[hh_ * 1408:(hh_ + 1) * 1408, dc * 128:(dc + 1) * 128].rearrange("(k p) m -> p k m", p=128), w=wdstg.all)
                        self.cast(wd[:, hh_ * 11:(hh_ + 1) * 11, :], wdstg[:], r=wdstg.all, w=wd.all, eng=("act", "dve")[hh_])
                    for t2 in range(2):
                        ps = self.nps()
                        for j in range(22):
                            self.mm(ps[:, :], wd[:, j, :], G[:, j, t2 * 512:(t2 + 1) * 512], j == 0, j == 21, r=wd.all + [G.rs[j]], w=ps.all)
                        tt = half * 2 + t2
                        xs = self.X[:, dc, tt * 512:(tt + 1) * 512]
                        P.add("dve", lambda e, xs=xs, ps=ps: e.tensor_tensor(out=xs, in0=ps[:, :], in1=xs, op=ALU.add), r=ps.all + [self.xr(dc, tt)], w=[self.xr(dc, tt)])
        P.barrier()


    def head_qk(self, *a, **kw):
        for _ in self.head_qk_gen(*a, **kw):
            pass

    def head_qk_gen(self, outT, terms, dh, gain_ap, PrT, Ct, St, wk, kd=None, ones_t=None):
        P = self.P
        raw, sq, rs, rs2, xn, t1, t2 = wk
        kd = kd or dh
        ones_t = ones_t or self.ones_bf
        for tt in range(NT):
            sl = slice(tt * 512, (tt + 1) * 512)
            ps = self.nps()
            tl = terms(tt)
            for i, (lt, rh, rd) in enumerate(tl):
                self.mm(ps[0:dh, :], lt, rh, i == 0, i == len(tl) - 1, r=rd, w=ps.all)
            yield
            P.add("act", lambda e, ps=ps: e.copy(out=raw[0:dh, :], in_=ps[0:dh, :]), r=ps.all, w=raw.all)
            P.add("act", lambda e, ps=ps: e.activation(out=sq[0:dh, :], in_=ps[0:dh, :], func=AF.Square), r=ps.all, w=sq.all)
            yield
            ps2 = self.nps()
            self.mm(ps2[0:kd, :], ones_t[0:kd, 0:kd], sq[0:kd, :], True, True, r=sq.all + ones_t.all, w=ps2.all)
            yield
            P.add("act", lambda e, ps2=ps2: e.activation(out=rs[0:dh, :], in_=ps2[0:dh, :], func=AF.Ln, bias=EPS, scale=1.0 / dh), r=ps2.all, w=rs.all)
            P.add("act", lambda e: e.activation(out=rs2[0:dh, :], in_=rs[0:dh, :], func=AF.Exp, scale=-0.5), r=rs.all, w=rs2.all)
            yield
            P.add("dve", lambda e: e.scalar_tensor_tensor(out=xn[0:dh, :], in0=raw[0:dh, :], scalar=gain_ap, in1=rs2[0:dh, :], op0=ALU.mult, op1=ALU.mult),
                  r=raw.all + rs2.all, w=xn.all)
            yield
            ps3 = self.nps()
            self.mm(ps3[0:kd, :], PrT[0:kd, 0:kd], xn[0:kd, :], True, True, r=xn.all + PrT.all, w=ps3.all)
            P.add("pool", lambda e, sl=sl: e.tensor_tensor(out=t1[0:dh, :], in0=xn[0:dh, :], in1=Ct[0:dh, sl], op=ALU.mult), r=xn.all + Ct.all, w=t1.all)
            yield
            P.add("dve", lambda e, sl=sl, ps3=ps3: e.tensor_tensor(out=t2[0:dh, :], in0=ps3[0:dh, :], in1=St[0:dh, sl], op=ALU.mult), r=ps3.all + St.all, w=t2.all)
            yield
            P.add("pool", lambda e, sl=sl: e.tensor_tensor(out=outT[0:dh, sl], in0=t1[0:dh, :], in1=t2[0:dh, :], op=ALU.add), r=t1.all + t2.all, w=outT.all)
            yield

    @staticmethod
    def interleave(main, side, k=1):
        for _ in main:
            for _ in range(k):
                if side is not None and next(side, "END") == "END":
                    side = None
        if side is not None:
            for _ in side:
                pass

    def attn_norm(self, ps_o, ps_sum, out_ap, out_res, sink_ap, wk2):
        P = self.P
        den, bc = wk2
        if sink_ap is not None:
            P.add("act", lambda e: e.activation(out=den[0:64, :], in_=ps_sum[0:64, :], func=AF.Ln, bias=sink_ap), r=ps_sum.all, w=den.all)
        else:
            P.add("act", lambda e: e.activation(out=den[0:64, :], in_=ps_sum[0:64, :], func=AF.Ln), r=ps_sum.all, w=den.all)
        P.add("act", lambda e: e.activation(out=bc[0:64, :], in_=den[0:64, :], func=AF.Exp, scale=-1.0), r=den.all, w=bc.all)
        P.add("dve", lambda e: e.tensor_tensor(out=out_ap, in0=ps_o[0:64, :], in1=bc[0:64, :], op=ALU.mult), r=ps_o.all + bc.all, w=out_res)

    def oproj_accum(self, wo_bf, OTc, nk):
        P = self.P
        for dc in range(8):
            for tt in range(NT):
                ps = self.nps()
                for k in range(nk):
                    self.mm(ps[:, :], wo_bf[:, k, dc * 128:(dc + 1) * 128], OTc[:, k, tt * 512:(tt + 1) * 512], k == 0, k == nk - 1,
                            r=wo_bf.all + OTc.all, w=ps.all)
                xs = self.X[:, dc, tt * 512:(tt + 1) * 512]
                P.add("dve", lambda e, xs=xs, ps=ps: e.tensor_tensor(out=xs, in0=ps[:, :], in1=xs, op=ALU.add), r=ps.all + [self.xr(dc, tt)], w=[self.xr(dc, tt)])

    def dbg(self, ap, slot, npart, ncols):
        if not getattr(self, "debug", False):
            return
        self.P.barrier()
        self.P.add("dve", lambda e: e.tensor_copy(out=self.X[0:npart, slot, 0:ncols], in_=ap), r=[], w=self.X.all)
        self.P.barrier()

    def qk_work(self, es):
        P = self.P
        return [P.sb([128, 512], (BF16 if i == 1 else F32), es=es) for i in range(7)]

    def swa(self, j):
        P = self.P
        d = self.d
        li = 3 * j
        with ExitStack() as es:
            gains = P.sb([128, 8], F32, es=es)
            P.dma(gains[:], d["norm_tok"][li].rearrange("(c p) -> p c", p=128), w=gains.all, allow_slow_non_contiguous=True)
            qg_t = P.sb([64, 1], F32, es=es)
            kg_t = P.sb([64, 1], F32, es=es)
            P.dma(qg_t[:], d["swa_q_gain"][j].rearrange("(p o) -> p o", o=1), w=qg_t.all, allow_slow_non_contiguous=True)
            P.dma(kg_t[:], d["swa_k_gain"][j].rearrange("(p o) -> p o", o=1), w=kg_t.all, allow_slow_non_contiguous=True)
            sk = P.sb([128, 16], F32, es=es)
            P.dma(sk[:], d["swa_sinks"][j:j + 1, :].to_broadcast([128, 16]), w=sk.all, allow_slow_non_contiguous=True)
            sk0 = sk
            sk = P.sb([128, 16], F32, es=es)
            P.add("act", lambda e: e.activation(out=sk[:], in_=sk0[:], func=AF.Exp), r=sk0.all, w=sk.all)
            Ct = P.sb([64, S], F32, es=es)
            St = P.sb([64, S], F32, es=es)
            PrT = P.sb([64, 64], F32, es=es)
            MLO = P.sb([128, 128], BF16, es=es)
            MHI = P.sb([128, 128], BF16, es=es)
            mstg = P.sb([128, 256], F32, es=es)
            P.dma(Ct[:], d["c_swa_cos"], w=Ct.all)
            P.dma(St[:], d["c_swa_sin"], w=St.all)
            P.dma(PrT[:], d["c_swa_rot"], w=PrT.all)
            P.dma(mstg[:, 0:128], d["c_mlo"], w=mstg.all)
            P.dma(mstg[:, 128:256], d["c_mhi"], w=mstg.all)
            P.add("dve", lambda e: e.tensor_copy(out=MLO[:], in_=mstg[:, 0:128]), r=mstg.all, w=MLO.all)
            P.add("dve", lambda e: e.tensor_copy(out=MHI[:], in_=mstg[:, 128:256]), r=mstg.all, w=MHI.all)
            H = P.sb([128, 8, S], BF16, es=es)
            self.rmsnorm(H, gains, es)
            stg = [P.sb([128, 1024], F32, es=es) for _ in range(2)]
            wqkv = d["swa_w_qkv"][j]
            wo = d["swa_w_o"][j]
            Wkv = P.sb([128, 8, 512], BF16, es=es)
            self.load_w(Wkv, lambda k0, k1: Wkv[:, k0:k1, :], wqkv[:, 1024:1536], 8, 512, stg)
            V = P.sb([128, 16, 4, 65], BF16, es=es)
            P.add("pool", lambda e: e.memset(V[:], 1.0), w=V.all)
            for ti in range(16):
                ps = self.nps()
                for k in range(8):
                    self.mm(ps[:, 0:256], H[:, k, ti * 128:(ti + 1) * 128], Wkv[:, k, 256:512], k == 0, k == 7, r=H.all + Wkv.all, w=ps.all)
                P.add("act", lambda e, ti=ti, ps=ps: e.copy(out=V[:, ti, :, 0:64], in_=ps[:, 0:256].rearrange("p (g v) -> p g v", v=64)), r=ps.all, w=V.all)
            wk = self.qk_work(es)
            den = P.sb([128, 512], F32, es=es)
            bc = P.sb([128, 512], F32, es=es)
            KTs = [P.sb([64, S], BF16, es=es) for _ in range(2)]
            QTs = [P.sb([64, S], BF16, es=es) for _ in range(2)]
            PT = [P.sb([128, 384], BF16, es=es) for _ in range(2)]
            OTc = P.sb([128, 2, S], BF16, es=es)
            Wq = P.sb([128, 8, 256], BF16, es=es)
            Wo = P.sb([128, 2, 1024], BF16, es=es)
            scale = 64 ** -0.5

            def prepK(g):
                return self.head_qk_gen(KTs[g % 2], lambda tt, g=g: [(Wkv[:, k, g * 64:(g + 1) * 64], H[:, k, tt * 512:(tt + 1) * 512], H.all + Wkv.all) for k in range(8)],
                                        64, kg_t[:, 0:1], PrT, Ct, St, wk)

            def prepQ(h):
                hh = h % 4
                return self.head_qk_gen(QTs[h % 2], lambda tt, hh=hh: [(Wq[:, k, hh * 64:(hh + 1) * 64], H[:, k, tt * 512:(tt + 1) * 512], H.all + Wq.all) for k in range(8)],
                                        64, qg_t[:, 0:1], PrT, Ct, St, wk)

            def chain(*gens):
                for g_ in gens:
                    if g_ is not None:
                        yield from g_

            def attn(h):
                g = h // 4
                hh = h % 4
                KT = KTs[g % 2]
                QT = QTs[h % 2]

                def smm(i):
                    js = [jj for jj in (i - 1, i, i + 1) if 0 <= jj < 16]
                    ps_s = self.nps()
                    for n, jj in enumerate(js):
                        self.mm(ps_s[:, n * 128:(n + 1) * 128], KT[:, jj * 128:(jj + 1) * 128], QT[:, i * 128:(i + 1) * 128], True, True,
                                r=KT.all + QT.all, w=ps_s.all)
                    return js, ps_s
                nxt = smm(0)
                for qg in range(4):
                    ps_o = self.ps[6]
                    ps_sum = self.ps[7]
                    for qi in range(4):
                        i = qg * 4 + qi
                        js, ps_s = nxt
                        if i + 1 < 16:
                            nxt = smm(i + 1)
                        pt = PT[i % 2]
                        nn = len(js) * 128
                        P.add("act", lambda e, pt=pt, ps_s=ps_s, nn=nn: e.activation(out=pt[:, 0:nn], in_=ps_s[:, 0:nn], func=AF.Exp, scale=scale), r=ps_s.all, w=pt.all)
                        for n, jj in enumerate(js):
                            if jj == i - 1:
                                P.add("dve", lambda e, pt=pt, n=n: e.tensor_tensor(out=pt[:, n * 128:(n + 1) * 128], in0=pt[:, n * 128:(n + 1) * 128], in1=MHI[:], op=ALU.mult),
                                      r=pt.all + MHI.all, w=pt.all)
                            elif jj == i + 1:
                                P.add("pool", lambda e, pt=pt, n=n: e.tensor_tensor(out=pt[:, n * 128:(n + 1) * 128], in0=pt[:, n * 128:(n + 1) * 128], in1=MLO[:], op=ALU.mult),
                                      r=pt.all + MLO.all, w=pt.all)
                        for n, jj in enumerate(js):
                            self.mm(ps_o[0:64, qi * 128:(qi + 1) * 128], V[:, jj, g, 0:64], pt[:, n * 128:(n + 1) * 128], n == 0, n == len(js) - 1,
                                    r=V.all + pt.all, w=ps_o.all)
                        for n, jj in enumerate(js):
                            self.mm(ps_sum[0:64, qi * 128:(qi + 1) * 128], self.ones_bf[:, 0:64], pt[:, n * 128:(n + 1) * 128], n == 0, n == len(js) - 1,
                                    r=self.ones_bf.all + pt.all, w=ps_sum.all)
                        yield
                    pb = (hh % 2) * 64
                    self.attn_norm(ps_o, ps_sum, OTc[pb:pb + 64, hh // 2, qg * 512:(qg + 1) * 512], OTc.all, sk[0:64, h:h + 1], (den, bc))
                    yield

            self.load_w(Wq, lambda k0, k1: Wq[:, k0:k1, :], wqkv[:, 0:256], 8, 256, stg)
            for _ in chain(prepK(0), prepQ(0)):
                pass
            for h in range(16):
                g = h // 4
                if h % 4 == 0:
                    self.load_w(Wo, lambda k0, k1: Wo[:, k0:k1, :], wo[g * 256:(g + 1) * 256, :], 2, 1024, stg)
                side = None
                if h + 1 < 16:
                    if (h + 1) % 4 == 0:
                        self.load_w(Wq, lambda k0, k1: Wq[:, k0:k1, :], wqkv[:, (g + 1) * 256:(g + 2) * 256], 8, 256, stg)
                        side = chain(prepK(g + 1), prepQ(h + 1))
                    else:
                        side = prepQ(h + 1)
                self.interleave(attn(h), side, k=3)
                if h % 4 == 3:
                    self.oproj_accum(Wo, OTc, 2)
        P.barrier()

    def mla(self, j):
        P = self.P
        d = self.d
        li = 2
        with ExitStack() as es:
            CQ = P.sb([128, 3, S], BF16, es=es)
            CKV = P.sb([128, 2, S], BF16, es=es)
            KR = P.sb([32, S], BF16, es=es)
            stg = [P.sb([128, 2048], F32, es=es) for _ in range(2)]
            with ExitStack() as esA:
                gains = P.sb([128, 8], F32, es=esA)
                P.dma(gains[:], d["norm_tok"][li].rearrange("(c p) -> p c", p=128), w=gains.all, allow_slow_non_contiguous=True)
                cg = P.sb([128, 5], F32, es=esA)
                P.dma(cg[:, 0:3], d["mla_cq_gain"][j].rearrange("(c p) -> p c", p=128), w=cg.all, allow_slow_non_contiguous=True)
                P.dma(cg[:, 3:5], d["mla_ckv_gain"][j].rearrange("(c p) -> p c", p=128), w=cg.all, allow_slow_non_contiguous=True)
                H = P.sb([128, 8, S], BF16, es=esA)
                self.rmsnorm(H, gains, esA)
                Wd = P.sb([128, 8, 672], BF16, es=esA)
                self.load_w(Wd, lambda k0, k1: Wd[:, k0:k1, :], d["mla_w_down"][j], 8, 672, stg)
                raw = [P.sb([128, 512], F32, es=esA) for _ in range(5)]
                sq = [P.sb([128, 512], F32, es=esA) for _ in range(5)]
                rs = P.sb([128, 512], F32, es=esA)
                rs2 = P.sb([128, 512], F32, es=esA)
                for tt in range(NT):
                    sl = slice(tt * 512, (tt + 1) * 512)
                    for c in range(6):
                        m = 128 if c < 5 else 32
                        ps = self.nps()
                        for k in range(8):
                            self.mm(ps[0:m, :], Wd[:, k, c * 128:c * 128 + m], H[:, k, sl], k == 0, k == 7, r=H.all + Wd.all, w=ps.all)
                        if c < 5:
                            P.add("act", lambda e, ps=ps, c=c: e.copy(out=raw[c][:], in_=ps[:, :]), r=ps.all, w=raw[c].all)
                            P.add("act", lambda e, ps=ps, c=c: e.activation(out=sq[c][:], in_=ps[:, :], func=AF.Square), r=ps.all, w=sq[c].all)
                        else:
                            P.add("act", lambda e, ps=ps, sl=sl: e.copy(out=KR[0:32, sl], in_=ps[0:32, :]), r=ps.all, w=KR.all)
                    for (c0, c1, dst, nf) in ((0, 3, CQ, 384), (3, 5, CKV, 256)):
                        ps = self.nps()
                        for c in range(c0, c1):
                            self.mm(ps[:, :], self.ones_f[:, :], sq[c][:], c == c0, c == c1 - 1, r=sq[c].all + self.ones_f.all, w=ps.all)
                        P.add("act", lambda e, ps=ps, nf=nf: e.activation(out=rs[:], in_=ps[:, :], func=AF.Sqrt, bias=EPS, scale=1.0 / nf), r=ps.all, w=rs.all)
                        P.add("dve", lambda e: e.reciprocal(out=rs2[:], in_=rs[:]), r=rs.all, w=rs2.all)
                        for c in range(c0, c1):
                            P.add("dve", lambda e, c=c, c0=c0, dst=dst, sl=sl: e.scalar_tensor_tensor(out=dst[:, c - c0, sl], in0=raw[c][:], scalar=cg[:, c:c + 1], in1=rs2[:],
                                                                                                 op0=ALU.mult, op1=ALU.mult), r=raw[c].all + rs2.all + cg.all, w=dst.all)
            P.barrier()
            qg_t = P.sb([96, 1], F32, es=es)
            kg_t = P.sb([96, 1], F32, es=es)
            P.dma(qg_t[:], d["mla_q_gain"][j].rearrange("(p o) -> p o", o=1), w=qg_t.all, allow_slow_non_contiguous=True)
            P.dma(kg_t[:], d["mla_k_gain"][j].rearrange("(p o) -> p o", o=1), w=kg_t.all, allow_slow_non_contiguous=True)
            Ct = P.sb([96, S], F32, es=es)
            St = P.sb([96, S], F32, es=es)
            PrT = P.sb([128, 128], F32, es=es)
            ones96f = P.sb([128, 128], F32, es=es)
            P.dma(ones96f[:], d["c_ones96"], w=ones96f.all)
            ones96 = P.sb([128, 128], BF16, es=es)
            self.o_cp("dve", ones96[:], ones96f[:], r=ones96f.all, w=ones96.all)
            IdS = P.sb([32, 96], BF16, es=es)
            P.dma(Ct[:], d["c_mla_cos"], w=Ct.all)
            P.dma(St[:], d["c_mla_sin"], w=St.all)
            P.dma(PrT[:], d["c_mla_rot"], w=PrT.all)
            P.dma(stg[0][0:32, 0:96], d["c_mla_ids"], w=stg[0].all)
            P.add("dve", lambda e: e.tensor_copy(out=IdS[:], in_=stg[0][0:32, 0:96]), r=stg[0].all, w=IdS.all)
            Wuq = P.sb([128, 3, 1536], BF16, es=es)
            self.load_w(Wuq, lambda k0, k1: Wuq[:, k0:k1, :], d["mla_w_uq"][j], 3, 1536, stg)
            Wkn = P.sb([128, 2, 16, 96], BF16, es=es)
            Wv = P.sb([128, 2, 16, 64], BF16, es=es)
            P.add("pool", lambda e: e.memset(Wkn[:], 0.0), w=Wkn.all)
            wukv = d["mla_w_ukv"][j]
            for k in range(2):
                st = stg[k % 2]
                P.dma(st[:, 0:2048], wukv[k * 128:(k + 1) * 128, :], w=st.all)
                sv = st[:, 0:2048].rearrange("p (h t) -> p h t", t=128)
                P.add("act", lambda e, k=k, sv=sv: e.copy(out=Wkn[:, k, :, 0:64], in_=sv[:, :, 0:64]), r=st.all, w=Wkn.all)
                P.add("pool", lambda e, k=k, sv=sv: e.tensor_copy(out=Wv[:, k, :, :], in_=sv[:, :, 64:128]), r=st.all, w=Wv.all)
            wk = self.qk_work(es)
            for t_ in wk:
                P.add("pool", lambda e, t_=t_: e.memset(t_[:], 0.0), w=t_.all)
            den = P.sb([128, 512], F32, es=es)
            bc = P.sb([128, 512], F32, es=es)
            KTs = [P.sb([128, S], BF16, es=es) for _ in range(2)]
            QTs = [P.sb([128, S], BF16, es=es) for _ in range(2)]
            Vhs = [P.sb([128, 16, 64], BF16, es=es) for _ in range(2)]
            for t_ in KTs + QTs:
                P.add("pool", lambda e, t_=t_: e.memset(t_[:], 0.0), w=t_.all)
            PT = [P.sb([128, 512], BF16, es=es) for _ in range(3)]
            OTc = P.sb([128, 1, S], BF16, es=es)
            Wo = P.sb([128, 1, 1024], BF16, es=es)
            wo = d["mla_w_o"][j]
            scale = 96 ** -0.5

            def prepV(h):
                Vh = Vhs[h % 2]
                for ti in range(16):
                    ps = self.nps()
                    for k in range(2):
                        self.mm(ps[:, 0:64], CKV[:, k, ti * 128:(ti + 1) * 128], Wv[:, k, h, :], k == 0, k == 1, r=CKV.all + Wv.all, w=ps.all)
                    yield
                    P.add("act", lambda e, ti=ti, ps=ps, Vh=Vh: e.copy(out=Vh[:, ti, :], in_=ps[:, 0:64]), r=ps.all, w=Vh.all)
                    yield

            def prepK(h):
                return self.head_qk_gen(KTs[h % 2], lambda tt, h=h: [(Wkn[:, k, h, :], CKV[:, k, tt * 512:(tt + 1) * 512], CKV.all + Wkn.all) for k in range(2)]
                                        + [(IdS[0:32, :], KR[0:32, tt * 512:(tt + 1) * 512], IdS.all + KR.all)],
                                        96, kg_t[:, 0:1], PrT, Ct, St, wk, kd=128, ones_t=ones96)

            def prepQ(h):
                return self.head_qk_gen(QTs[h % 2], lambda tt, h=h: [(Wuq[:, k, h * 96:(h + 1) * 96], CQ[:, k, tt * 512:(tt + 1) * 512], CQ.all + Wuq.all) for k in range(3)],
                                        96, qg_t[:, 0:1], PrT, Ct, St, wk, kd=128, ones_t=ones96)

            def chain(*gens):
                for g_ in gens:
                    yield from g_

            def attn(h):
                KT, QT, Vh = KTs[h % 2], QTs[h % 2], Vhs[h % 2]
                pti = 0

                def smm(qg, jj):
                    ps_s = self.nps()
                    self.mm(ps_s[:, :], KT[:, jj * 128:(jj + 1) * 128], QT[:, qg * 512:(qg + 1) * 512], True, True, r=KT.all + QT.all, w=ps_s.all)
                    return ps_s
                seq = [(qg, jj) for qg in range(4) for jj in range(16)]
                nxt = smm(*seq[0])
                for n, (qg, jj) in enumerate(seq):
                    ps_o = self.ps[6]
                    ps_sum = self.ps[7]
                    ps_s = nxt
                    if n + 1 < len(seq):
                        nxt = smm(*seq[n + 1])
                    pt = PT[pti % 3]
                    pti += 1
                    P.add("act", lambda e, pt=pt, ps_s=ps_s: e.activation(out=pt[:], in_=ps_s[:, :], func=AF.Exp, scale=scale), r=ps_s.all, w=pt.all)
                    self.mm(ps_o[0:64, :], Vh[:, jj, :], pt[:], jj == 0, jj == 15, r=Vh.all + pt.all, w=ps_o.all)
                    self.mm(ps_sum[0:64, :], self.ones_bf[:, 0:64], pt[:], jj == 0, jj == 15, r=self.ones_bf.all + pt.all, w=ps_sum.all)
                    yield
                    if jj == 15:
                        pb = (h % 2) * 64
                        self.attn_norm(ps_o, ps_sum, OTc[pb:pb + 64, 0, qg * 512:(qg + 1) * 512], OTc.all, None, (den, bc))
                        yield

            for _ in chain(prepV(0), prepK(0), prepQ(0)):
                pass
            for h in range(16):
                if h % 2 == 0:
                    self.load_w(Wo, lambda k0, k1: Wo[:, k0:k1, :], wo[(h // 2) * 128:(h // 2 + 1) * 128, :], 1, 1024, stg)
                side = chain(prepV(h + 1), prepK(h + 1), prepQ(h + 1)) if h + 1 < 16 else None
                self.interleave(attn(h), side, k=2)
                if h % 2 == 1:
                    self.oproj_accum(Wo, OTc, 1)
        P.barrier()

    def o_tt(self, eng, out, in0, in1, op, r, w):
        self.P.add(eng, lambda e: e.tensor_tensor(out=out, in0=in0, in1=in1, op=op), r=r, w=w)

    def o_ts(self, eng, out, in0, s1, s2, op0, op1, r, w):
        if s2 is None:
            self.P.add(eng, lambda e: e.tensor_scalar(out=out, in0=in0, scalar1=s1, scalar2=None, op0=op0), r=r, w=w)
        else:
            self.P.add(eng, lambda e: e.tensor_scalar(out=out, in0=in0, scalar1=s1, scalar2=s2, op0=op0, op1=op1), r=r, w=w)

    def o_stt(self, out, in0, scalar, in1, op0, op1, r, w):
        self.P.add("dve", lambda e: e.scalar_tensor_tensor(out=out, in0=in0, scalar=scalar, in1=in1, op0=op0, op1=op1), r=r, w=w)

    def o_act(self, out, in_, func, r, w, bias=None, scale=None):
        kw = {}
        if bias is not None:
            kw["bias"] = bias
        if scale is not None:
            kw["scale"] = scale
        self.P.add("act", lambda e: e.activation(out=out, in_=in_, func=func, **kw), r=r, w=w)

    def o_cp(self, eng, out, in_, r, w):
        if eng == "act":
            self.P.add("act", lambda e: e.copy(out=out, in_=in_), r=r, w=w)
        else:
            self.P.add(eng, lambda e: e.tensor_copy(out=out, in_=in_), r=r, w=w)

    def rwkv(self, j):
        P = self.P
        d = self.d
        nc = self.nc
        li = 1
        NCH = S // 64
        def scr(name, shape, dt):
            t = nc.dram_tensor(name, list(shape), dt)
            return t.ap(), Res()
        S_ar = [scr(f"rw_ar{dd}", [128, 8, 2, S], BF16) for dd in range(2)]
        S_b = [scr(f"rw_b{dd}", [128, 8, S], BF16) for dd in range(2)]
        S_k = [scr(f"rw_k{dd}", [128, 8, S], BF16) for dd in range(2)]
        S_v = scr("rw_v", [128, 8, S], BF16)
        S_pc = [scr(f"rw_pc{dd}", [NCH, 128, 8], F32) for dd in range(2)]
        S_g = scr("rw_g", [128, 8, S], F32)
        S_bn = scr("rw_bn", [128, 8, S], F32)
        S_y = scr("rw_y", [128, 8, S], F32)

        with ExitStack() as es:
            gains = P.sb([128, 8], F32, es=es)
            P.dma(gains[:], d["norm_tok"][li].rearrange("(c p) -> p c", p=128), w=gains.all, allow_slow_non_contiguous=True)
            RSTD = P.sb([128, S], F32, es=es)
            Wr = P.sb([128, 8, 1024], BF16, es=es)
            Wk = P.sb([128, 8, 1024], BF16, es=es)
            Wv = P.sb([128, 8, 1024], BF16, es=es)
            W1 = P.sb([128, 8, 2, 64], BF16, es=es)
            A1 = P.sb([128, 8, 2, 64], BF16, es=es)
            G1 = P.sb([128, 8, 160], BF16, es=es)
            W2 = P.sb([64, 2, 1024], BF16, es=es)
            A2 = P.sb([64, 2, 1024], BF16, es=es)
            G2a = P.sb([128, 1024], BF16, es=es)
            G2b = P.sb([32, 1024], BF16, es=es)
            W0bc = P.sb([128, 2, 1024], F32, es=es)
            MU = P.sb([128, 6, 8], F32, es=es)
            A0 = P.sb([128, 2, 8], F32, es=es)
            KK_ = P.sb([128, 8], F32, es=es)
            KA_ = P.sb([128, 8], F32, es=es)
            RK_ = P.sb([128, 8], F32, es=es)
            BD64 = P.sb([128, 128], F32, es=es)
            TRI = [P.sb([128, 256], F32, es=es) for _ in range(2)]
            P.dma(MU[:], d["rwkv_mu"][j].rearrange("i (c p) -> p i c", p=128), w=MU.all, allow_slow_non_contiguous=True)
            P.dma(A0[:], d["rwkv_a0"][j].rearrange("i (c p) -> p i c", p=128), w=A0.all, allow_slow_non_contiguous=True)
            NA0 = P.sb([128, 2, 8], F32, es=es)
            self.o_ts("dve", NA0[:], A0[:], -1.0, None, ALU.mult, None, r=A0.all, w=NA0.all)
            P.dma(KK_[:], d["rwkv_k_k"][j].rearrange("(c p) -> p c", p=128), w=KK_.all, allow_slow_non_contiguous=True)
            P.dma(KA_[:], d["rwkv_k_a"][j].rearrange("(c p) -> p c", p=128), w=KA_.all, allow_slow_non_contiguous=True)
            P.dma(RK_[:], d["rwkv_r_k"][j].rearrange("h k -> (h k)").rearrange("(c p) -> p c", p=128), w=RK_.all, allow_slow_non_contiguous=True)
            P.dma(BD64[:], d["c_bd64"], w=BD64.all)
            P.dma(TRI[0][:], d["c_tri_f"], w=TRI[0].all)
            P.dma(TRI[1][:], d["c_tri_b"], w=TRI[1].all)
            for dd in range(2):
                P.dma(W0bc[:, dd, :], d["rwkv_w0"][j, dd:dd + 1, :].to_broadcast([128, 1024]), w=W0bc.all, allow_slow_non_contiguous=True)
            with ExitStack() as esw:
                stg = [P.sb([128, 2048], F32, es=esw) for _ in range(2)]
                self.load_w(Wr, lambda k0, k1: Wr[:, k0:k1, :], d["rwkv_w_r"][j], 8, 1024, stg)
                self.load_w(Wk, lambda k0, k1: Wk[:, k0:k1, :], d["rwkv_w_k"][j], 8, 1024, stg)
                self.load_w(Wv, lambda k0, k1: Wv[:, k0:k1, :], d["rwkv_w_v"][j], 8, 1024, stg)
                for dd in range(2):
                    self.load_w(W1, lambda k0, k1, dd=dd: W1[:, k0:k1, dd, :], d["rwkv_w1"][j, dd], 8, 64, stg)
                    self.load_w(A1, lambda k0, k1, dd=dd: A1[:, k0:k1, dd, :], d["rwkv_a1"][j, dd], 8, 64, stg)
                self.load_w(G1, lambda k0, k1: G1[:, k0:k1, :], d["rwkv_g1"][j], 8, 160, stg)
                for dd in range(2):
                    for (dst, src) in ((W2, d["rwkv_w2"][j, dd]), (A2, d["rwkv_a2"][j, dd])):
                        st = stg[self.stg_rr % 2]
                        self.stg_rr += 1
                        P.dma(st[0:64, 0:1024], src, w=st.all)
                        self.cast(dst[:, dd, :], st[0:64, 0:1024], r=st.all, w=dst.all)
                st = stg[self.stg_rr % 2]
                self.stg_rr += 1
                P.dma(st[:, 0:1024], d["rwkv_g2"][j][0:128, :], w=st.all)
                self.cast(G2a[:], st[:, 0:1024], r=st.all, w=G2a.all)
                st = stg[self.stg_rr % 2]
                self.stg_rr += 1
                P.dma(st[0:32, 0:1024], d["rwkv_g2"][j][128:160, :], w=st.all)
                self.cast(G2b[:], st[0:32, 0:1024], r=st.all, w=G2b.all)
                self.rmsnorm(None, gains, esw, rstd_out=RSTD)
                P.barrier()
            HXf = P.sb([128, 2064], F32, es=es)
            HX = HXf
            Hh_ = HXf[:, 0:1040].rearrange("p (c t) -> p c t", t=130)
            XX_ = HXf[:, 1040:2064].rearrange("p (c t) -> p c t", t=128)
            TMPM = P.sb([128, 8, 128], F32, es=es)
            MIX = [P.sb([128, 8, 128], BF16, es=es) for _ in range(6)]
            O_ar = [P.sb([128, 8, 2, 128], BF16, es=es) for _ in range(2)]
            O_b = [P.sb([128, 8, 128], BF16, es=es) for _ in range(2)]
            O_k = [P.sb([128, 8, 128], BF16, es=es) for _ in range(2)]
            O_v = P.sb([128, 8, 128], BF16, es=es)
            O_pc = [P.sb([128, 2, 8], F32, es=es) for _ in range(2)]
            L1w = [P.sb([64, 128], BF16, es=es) for _ in range(2)]
            L1a = [P.sb([64, 128], BF16, es=es) for _ in range(2)]
            L1g = P.sb([128, 128], BF16, es=es)
            L1g2 = P.sb([32, 128], BF16, es=es)
            sm = [P.sb([128, 128], F32, es=es) for _ in range(22)]
            gq = 0
            (t_r, t_k, t_v, t_kq, t_sq, t_nr, t_kk, t_rr, t_a, t_t, t_kd, t_b, t_ep, t_em, t_epv, t_sb, t_sb2, t_x1, t_x2, t_x3, t_x4, t_x5) = sm
            for ti in range(16):
                t0 = ti * 128
                tt = ti // 4
                lo = max(0, t0 - 1)
                hi = min(S, t0 + 129)
                c0 = lo - (t0 - 1)
                n = hi - lo
                xrs = [self.xr(c, q) for c in range(8) for q in sorted(set([lo // 512, (hi - 1) // 512]))]
                if t0 == 0:
                    P.add("pool", lambda e: e.memset(Hh_[:, :, 0:1], 0.0), w=HX.all)
                if t0 + 129 > S:
                    P.add("pool", lambda e: e.memset(Hh_[:, :, 129:130], 0.0), w=HX.all)
                for c in range(8):
                    self.o_stt(Hh_[:, c, c0:c0 + n], self.X[:, c, lo:hi], gains[:, c:c + 1], RSTD[:, lo:hi], ALU.mult, ALU.mult, r=xrs + RSTD.all, w=HX.all)
                hc = Hh_[:, :, 1:129]
                xx = XX_
                self.o_tt("pool", TMPM[:], Hh_[:, :, 0:128], Hh_[:, :, 2:130], ALU.add, r=HX.all, w=TMPM.all)
                self.o_stt(xx, TMPM[:], 0.5, hc, ALU.mult, ALU.subtract, r=TMPM.all + HX.all, w=HX.all)
                for i in range(6):
                    self.o_tt("dve", TMPM[:], xx, MU[:, i, :].unsqueeze(2).to_broadcast([128, 8, 128]), ALU.mult, r=HX.all + MU.all, w=TMPM.all)
                    self.o_tt("dve", MIX[i][:], TMPM[:], hc, ALU.add, r=TMPM.all + HX.all, w=MIX[i].all)
                m_r, m_w, m_k, m_v, m_a, m_g = MIX
                for dd in range(2):
                    ps = self.nps()
                    for k in range(8):
                        self.mm(ps[0:64, 0:128], W1[:, k, dd, :], m_w[:, k, :], k == 0, k == 7, r=W1.all + m_w.all, w=ps.all)
                    self.o_act(L1w[dd][:], ps[0:64, 0:128], AF.Tanh, r=ps.all, w=L1w[dd].all)
                    ps = self.nps()
                    for k in range(8):
                        self.mm(ps[0:64, 0:128], A1[:, k, dd, :], m_a[:, k, :], k == 0, k == 7, r=A1.all + m_a.all, w=ps.all)
                    self.o_cp("act", L1a[dd][:], ps[0:64, 0:128], r=ps.all, w=L1a[dd].all)
                ps = self.nps()
                for k in range(8):
                    self.mm(ps[:, 0:128], G1[:, k, 0:128], m_g[:, k, :], k == 0, k == 7, r=G1.all + m_g.all, w=ps.all)
                self.o_act(L1g[:], ps[:, 0:128], AF.Sigmoid, r=ps.all, w=L1g.all)
                ps = self.nps()
                for k in range(8):
                    self.mm(ps[0:32, 0:128], G1[:, k, 128:160], m_g[:, k, :], k == 0, k == 7, r=G1.all + m_g.all, w=ps.all)
                self.o_act(L1g2[:], ps[0:32, 0:128], AF.Sigmoid, r=ps.all, w=L1g2.all)
                LW = HXf[:, 0:2048]
                for dd in range(2):
                    for hf in range(2):
                        ps = self.nps()
                        self.mm(ps[:, :], L1w[dd][:], W2[:, dd, hf * 512:(hf + 1) * 512], True, True, r=L1w[dd].all + W2.all, w=ps.all)
                        sl = slice(dd * 1024 + hf * 512, dd * 1024 + (hf + 1) * 512)
                        self.o_tt("dve", LW[:, sl], ps[:, :], W0bc[:, dd, hf * 512:(hf + 1) * 512], ALU.add, r=ps.all + W0bc.all, w=HX.all)
                    sl = slice(dd * 1024, (dd + 1) * 1024)
                    self.o_act(LW[:, sl], LW[:, sl], AF.Sigmoid, r=HX.all, w=HX.all)
                for oc in range(8):
                    fs = slice(oc * 128, (oc + 1) * 128)
                    for (wt, mx, dst) in ((Wr, m_r, t_r), (Wk, m_k, t_k), (Wv, m_v, t_v)):
                        ps = self.nps()
                        for k in range(8):
                            self.mm(ps[:, 0:128], wt[:, k, fs], mx[:, k, :], k == 0, k == 7, r=wt.all + mx.all, w=ps.all)
                        self.o_cp("act", dst[:], ps[:, 0:128], r=ps.all, w=dst.all)
                    self.o_cp("act", O_v[:, oc, :], t_v[:], r=t_v.all, w=O_v.all)
                    self.o_ts("dve", t_kq[:], t_k[:], KK_[:, oc:oc + 1], None, ALU.mult, None, r=t_k.all + KK_.all, w=t_kq.all)
                    self.o_act(t_sq[:], t_kq[:], AF.Square, r=t_kq.all, w=t_sq.all)
                    ps = self.nps()
                    self.mm(ps[:, 0:128], BD64[:], t_sq[:], True, True, r=BD64.all + t_sq.all, w=ps.all)
                    self.o_ts("dve", t_nr[:], ps[:, 0:128], 1e-24, None, ALU.max, None, r=ps.all, w=t_nr.all)
                    self.o_act(t_nr[:], t_nr[:], AF.Ln, r=t_nr.all, w=t_nr.all)
                    self.o_act(t_sq[:], t_nr[:], AF.Exp, r=t_nr.all, w=t_sq.all, scale=-0.5)
                    self.o_tt("dve", t_kk[:], t_kq[:], t_sq[:], ALU.mult, r=t_kq.all + t_sq.all, w=t_kk.all)
                    self.o_ts("dve", t_rr[:], t_r[:], RK_[:, oc:oc + 1], None, ALU.mult, None, r=t_r.all + RK_.all, w=t_rr.all)
                    ps = self.nps()
                    self.mm(ps[:, 0:128], G2a[:, fs], L1g[:], True, False, r=G2a.all + L1g.all, w=ps.all)
                    self.mm(ps[:, 0:128], G2b[:, fs], L1g2[:], False, True, r=G2b.all + L1g2.all, w=ps.all)
                    tg = (t_x1, t_x2)[oc % 2]
                    self.o_cp("act", tg[:], ps[:, 0:128], r=ps.all, w=tg.all)
                    P.dma(S_g[0][:, oc, t0:t0 + 128], tg[:], r=tg.all, w=[S_g[1]])
                    for dd in range(2):
                        ps = self.nps()
                        self.mm(ps[:, 0:128], A2[:, dd, fs], L1a[dd][:], True, True, r=A2.all + L1a[dd].all, w=ps.all)
                        self.o_act(t_a[:], ps[:, 0:128], AF.Exp, r=ps.all + NA0.all, w=t_a.all, bias=NA0[:, dd, oc:oc + 1], scale=-1.0)
                        self.o_ts("dve", t_a[:], t_a[:], 1.0, None, ALU.add, None, r=t_a.all, w=t_a.all)
                        self.o_act(t_a[:], t_a[:], AF.Ln, r=t_a.all, w=t_a.all)
                        self.o_act(t_a[:], t_a[:], AF.Exp, r=t_a.all, w=t_a.all, scale=-1.0)
                        self.o_ts("dve", t_t[:], t_a[:], 1.0, KA_[:, oc:oc + 1], ALU.subtract, ALU.mult, r=t_a.all + KA_.all, w=t_t.all)
                        self.o_stt(t_kd[:], t_t[:], 1.0, t_k[:], ALU.add, ALU.mult, r=t_t.all + t_k.all, w=t_kd.all)
                        self.o_tt("dve", t_b[:], t_kk[:], t_a[:], ALU.mult, r=t_kk.all + t_a.all, w=t_b.all)
                        ps = self.nps()
                        self.mm(ps[:, 0:256], LW[:, dd * 1024 + oc * 128:dd * 1024 + (oc + 1) * 128], TRI[dd][:], True, True, r=HX.all + TRI[dd].all, w=ps.all)
                        self.o_act(t_ep[:], ps[:, 0:128], AF.Exp, r=ps.all, w=t_ep.all)
                        self.o_act(t_em[:], ps[:, 0:128], AF.Exp, r=ps.all, w=t_em.all, scale=-1.0)
                        self.o_act(t_epv[:], ps[:, 128:256], AF.Exp, r=ps.all, w=t_epv.all)
                        for cc in range(2):
                            col = cc * 64 + (63 if dd == 0 else 0)
                            self.o_cp("act", O_pc[dd][:, cc, oc:oc + 1], t_ep[:, col:col + 1], r=t_ep.all, w=O_pc[dd].all)
                        self.o_stt(O_ar[dd][:, oc, 0, :], t_kk[:], -1.0, t_epv[:], ALU.mult, ALU.mult, r=t_kk.all + t_epv.all, w=O_ar[dd].all)
                        self.o_tt("dve", O_ar[dd][:, oc, 1, :], t_r[:], t_ep[:], ALU.mult, r=t_r.all + t_ep.all, w=O_ar[dd].all)
                        self.o_tt("dve", O_b[dd][:, oc, :], t_b[:], t_em[:], ALU.mult, r=t_b.all + t_em.all, w=O_b[dd].all)
                        self.o_tt("pool", O_k[dd][:, oc, :], t_kd[:], t_em[:], ALU.mult, r=t_kd.all + t_em.all, w=O_k[dd].all)
                        if dd == 0:
                            self.o_tt("dve", t_sb[:], t_rr[:], t_kd[:], ALU.mult, r=t_rr.all + t_kd.all, w=t_sb.all)
                        else:
                            self.o_tt("dve", t_sb2[:], t_rr[:], t_kd[:], ALU.mult, r=t_rr.all + t_kd.all, w=t_sb2.all)
                            self.o_tt("pool", t_sb[:], t_sb[:], t_sb2[:], ALU.add, r=t_sb.all + t_sb2.all, w=t_sb.all)
                    ps = self.nps()
                    self.mm(ps[:, 0:128], BD64[:], t_sb[:], True, True, r=BD64.all + t_sb.all, w=ps.all)
                    tb = (t_x3, t_x4)[oc % 2]
                    self.o_tt("dve", tb[:], ps[:, 0:128], t_v[:], ALU.mult, r=ps.all + t_v.all, w=tb.all)
                    P.dma(S_bn[0][:, oc, t0:t0 + 128], tb[:], r=tb.all, w=[S_bn[1]])
                ts_ = slice(t0, t0 + 128)
                for dd in range(2):
                    P.dma(S_ar[dd][0][:, :, :, ts_], O_ar[dd][:], r=O_ar[dd].all, w=[S_ar[dd][1]])
                    P.dma(S_b[dd][0][:, :, ts_], O_b[dd][:], r=O_b[dd].all, w=[S_b[dd][1]])
                    P.dma(S_k[dd][0][:, :, ts_], O_k[dd][:], r=O_k[dd].all, w=[S_k[dd][1]])
                    for cc in range(2):
                        P.dma(S_pc[dd][0][2 * ti + cc], O_pc[dd][:, cc, :], r=O_pc[dd].all, w=[S_pc[dd][1]])
                P.dma(S_v[0][:, :, ts_], O_v[:], r=O_v.all, w=[S_v[1]])
        P.barrier()
        import os
        if os.environ.get("RWKV_STOP") == "A":
            return

        with ExitStack() as es:
            IST = P.sb([128, 64], BF16, es=es)
            MSK = [P.sb([128, 512], BF16, es=es) for _ in range(2)]
            LMSK = [P.sb([128, 512], BF16, es=es) for _ in range(2)]
            BD64 = P.sb([128, 128], F32, es=es)
            LNW = P.sb([128, 8], F32, es=es)
            LNB = P.sb([128, 8], F32, es=es)
            Wo = P.sb([128, 8, 1024], BF16, es=es)
            P.dma(BD64[:], d["c_bd64"], w=BD64.all)
            P.dma(LNW[:], d["rwkv_lnx_w"][j].rearrange("(c p) -> p c", p=128), w=LNW.all, allow_slow_non_contiguous=True)
            P.dma(LNB[:], d["rwkv_lnx_b"][j].rearrange("(c p) -> p c", p=128), w=LNB.all, allow_slow_non_contiguous=True)
            with ExitStack() as esw:
                stg = [P.sb([128, 2048], F32, es=esw) for _ in range(2)]
                self.load_w(Wo, lambda k0, k1: Wo[:, k0:k1, :], d["rwkv_w_o"][j], 8, 1024, stg)
                P.dma(stg[0][:, 0:64], d["c_ist"], w=stg[0].all)
                self.o_cp("dve", IST[:], stg[0][:, 0:64], r=stg[0].all, w=IST.all)
                for dd, (mk, lk) in enumerate((("c_mask_f", "c_lmask_f"), ("c_mask_b", "c_lmask_b"))):
                    P.dma(stg[1][:, 0:512], d[mk], w=stg[1].all)
                    self.o_cp("dve", MSK[dd][:], stg[1][:, 0:512], r=stg[1].all, w=MSK[dd].all)
                    P.dma(stg[1][:, 512:1024], d[lk], w=stg[1].all)
                    self.o_cp("dve", LMSK[dd][:], stg[1][:, 512:1024], r=stg[1].all, w=LMSK[dd].all)
                P.barrier()
            def bdtile(es_):
                t = P.sb([128, 8, 128], BF16, es=es_)
                P.add("pool", lambda e: e.memset(t[:], 0.0), w=t.all)
                return t

            S_y2 = [(S_y[0], S_y[1]), scr("rw_y2", [128, 8, S], F32)]
            ess = ExitStack()
            TS = []
            for dd in range(2):
                t = {}
                t["I_ar"] = [P.sb([128, 8, 128], BF16, es=ess) for _ in range(2)]
                for nm in ("I_abd", "I_bbd", "I_kbd", "I_vbd"):
                    t[nm] = [bdtile(ess) for _ in range(2)]
                t["I_b"] = [P.sb([128, 8, 64], BF16, es=ess) for _ in range(2)]
                t["I_pc"] = [P.sb([128, 8], F32, es=ess) for _ in range(2)]
                t["V_st"] = P.sb([128, 8, 64], BF16, es=ess)
                for nm in ("V_bd", "Bt_bd", "Kt_bd", "M_bd", "Aak_bd", "L_bd", "U_bd", "H_bd"):
                    t[nm] = bdtile(ess)
                t["ATB"] = P.sb([128, 8, 128], BF16, es=ess)
                t["ATK"] = P.sb([128, 8, 128], BF16, es=ess)
                t["LX"] = P.sb([128, 8, 128], BF16, es=ess)
                t["M_st"] = P.sb([128, 8, 64], BF16, es=ess)
                t["Xf"] = P.sb([128, 8, 64], F32, es=ess)
                t["H_f"] = P.sb([128, 8, 64], F32, es=ess)
                t["H_bf"] = P.sb([128, 8, 64], BF16, es=ess)
                t["TMPH"] = P.sb([128, 8, 64], F32, es=ess)
                t["Yt"] = P.sb([128, 8, 64], F32, es=ess)
                TS.append(t)

            def bd_write(dst, src_lo, src_hi, r, eng0="dve", eng1="act"):
                self.o_cp(eng0, dst[0:64, :, 0:64], src_lo, r=r, w=dst.all)
                self.o_cp(eng1, dst[64:128, :, 64:128], src_hi, r=r, w=dst.all)

            def load_chunk(dd, ch, buf):
                t = TS[dd]
                ts_ = slice(ch * 64, (ch + 1) * 64)
                P.dma(t["I_ar"][buf][:].rearrange("p c (e t) -> p c e t", e=2), S_ar[dd][0][:, :, :, ts_], r=[S_ar[dd][1]], w=t["I_ar"][buf].all)
                for e_ in range(2):
                    ps_ = slice(e_ * 64, (e_ + 1) * 64)
                    fs_ = slice(e_ * 64, (e_ + 1) * 64)
                    P.dma(t["I_abd"][buf][ps_, :, fs_], S_ar[dd][0][ps_, :, 0, ts_], r=[S_ar[dd][1]], w=t["I_abd"][buf].all)
                    P.dma(t["I_bbd"][buf][ps_, :, fs_], S_b[dd][0][ps_, :, ts_], r=[S_b[dd][1]], w=t["I_bbd"][buf].all)
                    P.dma(t["I_kbd"][buf][ps_, :, fs_], S_k[dd][0][ps_, :, ts_], r=[S_k[dd][1]], w=t["I_kbd"][buf].all)
                    P.dma(t["I_vbd"][buf][ps_, :, fs_], S_v[0][ps_, :, ts_], r=[S_v[1]], w=t["I_vbd"][buf].all)
                P.dma(t["I_b"][buf][:], S_b[dd][0][:, :, ts_], r=[S_b[dd][1]], w=t["I_b"][buf].all)
                P.dma(t["I_pc"][buf][:], S_pc[dd][0][ch], r=[S_pc[dd][1]], w=t["I_pc"][buf].all)

            bank_ctr = [0, 0]

            def chunk_gen(dd, ch, buf):
                t = TS[dd]
                ar, abd, bbd, kbd, vbd, bst, pc = (t["I_ar"][buf], t["I_abd"][buf], t["I_bbd"][buf], t["I_kbd"][buf], t["I_vbd"][buf], t["I_b"][buf], t["I_pc"][buf])
                V_st, V_bd, Bt_bd, Kt_bd, M_bd, Aak_bd, L_bd, U_bd, H_bd = (t[k_] for k_ in ("V_st", "V_bd", "Bt_bd", "Kt_bd", "M_bd", "Aak_bd", "L_bd", "U_bd", "H_bd"))
                ATB, ATK, LX, M_st, Xf, H_f, H_bf, TMPH, Yt = (t[k_] for k_ in ("ATB", "ATK", "LX", "M_st", "Xf", "H_f", "H_bf", "TMPH", "Yt"))
                tsl = slice(ch * 64, (ch + 1) * 64)

                def nb():
                    bank_ctr[dd] += 1
                    return self.ps[dd * 4 + bank_ctr[dd] % 4]
                psv, psb, psk = nb(), nb(), nb()
                for c in range(8):
                    self.mm(psv[:, c * 64:(c + 1) * 64], vbd[:, c, :], IST[:], True, True, r=vbd.all + IST.all, w=psv.all)
                    self.mm(psb[:, c * 64:(c + 1) * 64], bbd[:, c, :], IST[:], True, True, r=bbd.all + IST.all, w=psb.all)
                    self.mm(psk[:, c * 64:(c + 1) * 64], kbd[:, c, :], IST[:], True, True, r=kbd.all + IST.all, w=psk.all)
                yield
                v3 = psv[:, :].rearrange("p (c v) -> p c v", v=64)
                self.o_cp("act", V_st[:], v3, r=psv.all, w=V_st.all)
                bd_write(V_bd, v3[0:64], v3[64:128], psv.all, "dve", "act")
                b3 = psb[:, :].rearrange("p (c v) -> p c v", v=64)
                bd_write(Bt_bd, b3[0:64], b3[64:128], psb.all, "dve", "act")
                k3 = psk[:, :].rearrange("p (c v) -> p c v", v=64)
                bd_write(Kt_bd, k3[0:64], k3[64:128], psk.all, "dve", "act")
                yield
                pb = [nb(), nb()]
                pl = nb()
                for c in range(8):
                    cs = slice((c % 4) * 128, (c % 4 + 1) * 128)
                    self.mm(pb[c // 4][:, cs], bbd[:, c, :], ar[:, c, :], True, True, r=bbd.all + ar.all, w=pb[c // 4].all)
                    self.mm(pl[:, c * 64:(c + 1) * 64], abd[:, c, :], bst[:, c, :], True, True, r=abd.all + bst.all, w=pl.all)
                yield
                for hb in range(2):
                    self.o_tt("dve", ATB[:, hb * 4:(hb + 1) * 4, :], pb[hb][:, :].rearrange("p (c t) -> p c t", t=128), MSK[dd][:, :].rearrange("p (c t) -> p c t", t=128),
                              ALU.mult, r=pb[hb].all + MSK[dd].all, w=ATB.all)
                self.o_tt("dve", LX[:, :, 0:64], pl[:, :].rearrange("p (c t) -> p c t", t=64), LMSK[dd][:, :].rearrange("p (c t) -> p c t", t=64), ALU.mult,
                          r=pl.all + LMSK[dd].all, w=LX.all)
                yield
                pk = [nb(), nb()]
                for c in range(8):
                    cs = slice((c % 4) * 128, (c % 4 + 1) * 128)
                    self.mm(pk[c // 4][:, cs], kbd[:, c, :], ar[:, c, :], True, True, r=kbd.all + ar.all, w=pk[c // 4].all)
                yield
                for hb in range(2):
                    self.o_tt("dve", ATK[:, hb * 4:(hb + 1) * 4, :], pk[hb][:, :].rearrange("p (c t) -> p c t", t=128), MSK[dd][:, :].rearrange("p (c t) -> p c t", t=128),
                              ALU.mult, r=pk[hb].all + MSK[dd].all, w=ATK.all)
                yield
                bd_write(M_bd, ATB[0:64, :, 0:64], ATB[64:128, :, 0:64], ATB.all, "act", "pool")
                bd_write(Aak_bd, ATK[0:64, :, 0:64], ATK[64:128, :, 0:64], ATK.all, "act", "dve")
                self.o_cp("act", M_st[:], ATB[:, :, 0:64], r=ATB.all, w=M_st.all)
                bd_write(L_bd, LX[0:64, :, 0:64], LX[64:128, :, 0:64], LX.all, "dve", "act")
                yield
                pw = nb()
                for c in range(8):
                    self.mm(pw[:, c * 64:(c + 1) * 64], abd[:, c, :], H_bf[:, c, :], True, False, r=abd.all + H_bf.all, w=pw.all)
                    self.mm(pw[:, c * 64:(c + 1) * 64], Aak_bd[:, c, :], V_st[:, c, :], False, True, r=Aak_bd.all + V_st.all, w=pw.all)
                yield
                w3 = pw[:, :].rearrange("p (c v) -> p c v", v=64)
                self.o_cp("act", Xf[:], w3, r=pw.all, w=Xf.all)
                self.o_cp("dve", LX[:, :, 64:128], w3, r=pw.all, w=LX.all)
                yield
                for lev in range(6):
                    last = lev == 5
                    pa = [nb(), nb()]
                    pbm = nb()
                    for c in range(8):
                        cs = slice((c % 4) * 128, (c % 4 + 1) * 128)
                        if last:
                            self.mm(pa[c // 4][:, (c % 4) * 128 + 64:(c % 4 + 1) * 128], M_bd[:, c, :], LX[:, c, 64:128], True, True, r=M_bd.all + LX.all, w=pa[c // 4].all)
                        else:
                            self.mm(pa[c // 4][:, cs], M_bd[:, c, :], LX[:, c, :], True, True, r=M_bd.all + LX.all, w=pa[c // 4].all)
                            self.mm(pbm[:, c * 64:(c + 1) * 64], L_bd[:, c, :], M_st[:, c, :], True, True, r=L_bd.all + M_st.all, w=pbm.all)
                    yield
                    for hb in range(2):
                        a3 = pa[hb][:, :].rearrange("p (c t) -> p c t", t=128)
                        self.o_tt("dve", Xf[:, hb * 4:(hb + 1) * 4, :], Xf[:, hb * 4:(hb + 1) * 4, :], a3[:, :, 64:128], ALU.add, r=pa[hb].all + Xf.all, w=Xf.all)
                    if not last:
                        for hb in range(2):
                            a3 = pa[hb][:, :].rearrange("p (c t) -> p c t", t=128)
                            if lev < 4:
                                self.o_cp("act", LX[:, hb * 4:(hb + 1) * 4, 0:64], a3[:, :, 0:64], r=pa[hb].all, w=LX.all)
                                self.o_cp("dve", L_bd[0:64, hb * 4:(hb + 1) * 4, 0:64], a3[0:64, :, 0:64], r=pa[hb].all, w=L_bd.all)
                                self.o_cp("act", L_bd[64:128, hb * 4:(hb + 1) * 4, 64:128], a3[64:128, :, 0:64], r=pa[hb].all, w=L_bd.all)
                        m3 = pbm[:, :].rearrange("p (c t) -> p c t", t=64)
                        self.o_cp("act", M_st[:], m3, r=pbm.all, w=M_st.all)
                        bd_write(M_bd, m3[0:64], m3[64:128], pbm.all, "dve", "act")
                    self.o_cp("act", LX[:, :, 64:128], Xf[:], r=Xf.all, w=LX.all)
                    yield
                bd_write(U_bd, LX[0:64, :, 64:128], LX[64:128, :, 64:128], LX.all, "dve", "act")
                yield
                py = nb()
                for c in range(8):
                    cs = slice(c * 64, (c + 1) * 64)
                    self.mm(py[:, cs], H_bd[:, c, :], ar[:, c, 64:128], True, False, r=H_bd.all + ar.all, w=py.all)
                    self.mm(py[:, cs], U_bd[:, c, :], ATB[:, c, 64:128], False, False, r=U_bd.all + ATB.all, w=py.all)
                    self.mm(py[:, cs], V_bd[:, c, :], ATK[:, c, 64:128], False, True, r=V_bd.all + ATK.all, w=py.all)
                y3 = py[:, :].rearrange("p (c t) -> p c t", t=64)
                ph = nb()
                for c in range(8):
                    cs = slice(c * 64, (c + 1) * 64)
                    self.mm(ph[:, cs], Bt_bd[:, c, :], LX[:, c, 64:128], True, False, r=Bt_bd.all + LX.all, w=ph.all)
                    self.mm(ph[:, cs], Kt_bd[:, c, :], V_st[:, c, :], False, True, r=Kt_bd.all + V_st.all, w=ph.all)
                yield
                h3 = ph[:, :].rearrange("p (c v) -> p c v", v=64)
                self.o_cp("act", Yt[:], y3, r=py.all, w=Yt.all)
                P.dma(S_y2[dd][0][:, :, tsl], Yt[:], r=Yt.all, w=[S_y2[dd][1]])
                self.o_tt("dve", TMPH[:], h3, H_f[:], ALU.add, r=ph.all + H_f.all, w=TMPH.all)
                self.o_tt("dve", H_f[:], TMPH[:], pc[:, :].unsqueeze(2).to_broadcast([128, 8, 64]), ALU.mult, r=TMPH.all + pc.all, w=H_f.all)
                yield
                self.o_cp("act", H_bf[:], H_f[:], r=H_f.all, w=H_bf.all)
                bd_write(H_bd, H_f[0:64, :, :], H_f[64:128, :, :], H_f.all, "dve", "pool")
                yield

            from itertools import zip_longest
            nlim = int(os.environ.get("RWKV_NCH", str(NCH)))
            orders = [list(range(NCH))[:nlim], list(range(NCH - 1, -1, -1))[:nlim]]
            for dd in range(2):
                t = TS[dd]
                P.add("pool", lambda e, t=t: e.memset(t["H_f"][:], 0.0), w=t["H_f"].all)
                P.add("pool", lambda e, t=t: e.memset(t["H_bf"][:], 0.0), w=t["H_bf"].all)
                load_chunk(dd, orders[dd][0], 0)
            for oi in range(len(orders[0])):
                buf = oi % 2
                gens = []
                for dd in range(2):
                    if oi + 1 < len(orders[dd]):
                        load_chunk(dd, orders[dd][oi + 1], 1 - buf)
                    gens.append(chunk_gen(dd, orders[dd][oi], buf))
                for _ in zip_longest(*gens):
                    pass
            P.barrier()
            ess.close()

            NB = 2
            I_yf = [P.sb([128, 8, 64], F32, es=es) for _ in range(NB)]
            I_yb = [P.sb([128, 8, 64], F32, es=es) for _ in range(NB)]
            I_g = [P.sb([128, 8, 64], F32, es=es) for _ in range(NB)]
            I_bn = [P.sb([128, 8, 64], F32, es=es) for _ in range(NB)]
            W_ = [[P.sb([128, 8, 64], F32, es=es) for _ in range(4)] for _ in range(NB)]
            Z_ = [P.sb([128, 8, 64], BF16, es=es) for _ in range(NB)]

            def load_post(ch, b_):
                ts_ = slice(ch * 64, (ch + 1) * 64)
                P.dma(I_yf[b_][:], S_y2[0][0][:, :, ts_], r=[S_y2[0][1]], w=I_yf[b_].all)
                P.dma(I_yb[b_][:], S_y2[1][0][:, :, ts_], r=[S_y2[1][1]], w=I_yb[b_].all)
                P.dma(I_g[b_][:], S_g[0][:, :, ts_], r=[S_g[1]], w=I_g[b_].all)
                P.dma(I_bn[b_][:], S_bn[0][:, :, ts_], r=[S_bn[1]], w=I_bn[b_].all)

            load_post(0, 0)
            for ch in range(NCH):
                b_ = ch % NB
                if ch + 1 < NCH:
                    load_post(ch + 1, (ch + 1) % NB)
                Yt, Y2, Y3, Y4 = W_[b_]
                Zt = Z_[b_]
                tsl = slice(ch * 64, (ch + 1) * 64)
                self.o_tt("dve", Yt[:], I_yf[b_][:], I_yb[b_][:], ALU.add, r=I_yf[b_].all + I_yb[b_].all, w=Yt.all)
                pm = self.nps()
                self.mm(pm[:, :], BD64[:], Yt[:].rearrange("p c t -> p (c t)"), True, True, r=BD64.all + Yt.all, w=pm.all)
                self.o_stt(Y2[:], pm[:, :].rearrange("p (c t) -> p c t", t=64), -1.0 / 64, Yt[:], ALU.mult, ALU.add, r=pm.all + Yt.all, w=Y2.all)
                self.o_act(Y3[:], Y2[:], AF.Square, r=Y2.all, w=Y3.all)
                pv = self.nps()
                self.mm(pv[:, :], BD64[:], Y3[:].rearrange("p c t -> p (c t)"), True, True, r=BD64.all + Y3.all, w=pv.all)
                self.o_act(Y3[:].rearrange("p c t -> p (c t)"), pv[:, :], AF.Ln, r=pv.all, w=Y3.all, bias=64e-5, scale=1.0 / 64)
                self.o_act(Y4[:], Y3[:], AF.Exp, r=Y3.all, w=Y4.all, scale=-0.5)
                self.o_tt("dve", Y2[:], Y2[:], Y4[:], ALU.mult, r=Y2.all + Y4.all, w=Y2.all)
                self.o_tt("pool", Y2[:], Y2[:], LNW[:, :].unsqueeze(2).to_broadcast([128, 8, 64]), ALU.mult, r=Y2.all + LNW.all, w=Y2.all)
                self.o_tt("dve", Y2[:], Y2[:], LNB[:, :].unsqueeze(2).to_broadcast([128, 8, 64]), ALU.add, r=Y2.all + LNB.all, w=Y2.all)
                self.o_tt("pool", Y2[:], Y2[:], I_bn[b_][:], ALU.add, r=Y2.all + I_bn[b_].all, w=Y2.all)
                self.o_tt("dve", Zt[:], Y2[:], I_g[b_][:], ALU.mult, r=Y2.all + I_g[b_].all, w=Zt.all)
                po = self.nps()
                for dc in range(8):
                    for c in range(8):
                        self.mm(po[:, dc * 64:(dc + 1) * 64], Wo[:, c, dc * 128:(dc + 1) * 128], Zt[:, c, :], c == 0, c == 7, r=Wo.all + Zt.all, w=po.all)
                xs = self.X[:, :, tsl]
                xres = [self.xr(c, ch // 8) for c in range(8)]
                self.o_tt("dve", xs, po[:, :].rearrange("p (c t) -> p c t", t=64), xs, ALU.add, r=po.all + xres, w=xres)
        P.barrier()


def build(stages, debug=False):
    nc = bass.Bass("TRN2", target_bir_lowering=False)
    dram = {}

    def din(name, shape):
        dram[name] = nc.dram_tensor(name, list(shape), F32, kind="ExternalInput").ap()

    din("x", [S, D])
    for name, shape in PARAM_SHAPES.items():
        din(name, shape)
    for name, shape in CONST_SHAPES.items():
        din(name, shape)
    dram["y"] = nc.dram_tensor("y", [S, D], F32, kind="ExternalOutput").ap()
    with ExitStack() as es:
        P = Prog(nc, es)
        kb = KB(nc, es, P, dram)
        kb.debug = debug
        with ExitStack() as es2:
            kb.load_x(es2)
        P.barrier()
        for st in stages:
            kind, li = st
            if kind == "ffn":
                kb.ffn(li)
            elif kind == "swa":
                kb.swa(li)
            elif kind == "mla":
                kb.mla(li)
            elif kind == "rwkv":
                kb.rwkv(li)
        with ExitStack() as es2:
            kb.store_x(es2)
        P.emit()
        print("prog stats", P.stats)
    return nc


PARAM_SHAPES = {
    "norm_tok": (4, 1024), "norm_ch": (4, 1024), "ffn_w_up": (4, 1024, 5632), "ffn_conv_w": (4, 3, 5632),
    "ffn_conv_b": (4, 5632), "ffn_w_down": (4, 2816, 1024),
    "swa_w_qkv": (2, 1024, 1536), "swa_q_gain": (2, 64), "swa_k_gain": (2, 64), "swa_sinks": (2, 16), "swa_w_o": (2, 1024, 1024),
    "rwkv_mu": (1, 6, 1024), "rwkv_w_r": (1, 1024, 1024), "rwkv_w_k": (1, 1024, 1024), "rwkv_w_v": (1, 1024, 1024),
    "rwkv_w0": (1, 2, 1024), "rwkv_w1": (1, 2, 1024, 64), "rwkv_w2": (1, 2, 64, 1024), "rwkv_a0": (1, 2, 1024),
    "rwkv_a1": (1, 2, 1024, 64), "rwkv_a2": (1, 2, 64, 1024), "rwkv_g1": (1, 1024, 160), "rwkv_g2": (1, 160, 1024),
    "rwkv_k_k": (1, 1024), "rwkv_k_a": (1, 1024), "rwkv_r_k": (1, 16, 64), "rwkv_lnx_w": (1, 1024), "rwkv_lnx_b": (1, 1024),
    "rwkv_w_o": (1, 1024, 1024),
    "mla_w_down": (1, 1024, 672), "mla_cq_gain": (1, 384), "mla_ckv_gain": (1, 256), "mla_w_uq": (1, 384, 1536),
    "mla_w_ukv": (1, 256, 2048), "mla_q_gain": (1, 96), "mla_k_gain": (1, 96), "mla_w_o": (1, 1024, 1024),
}


def make_consts():
    c = {}
    c["c_ident"] = np.eye(128, dtype=np.float32)
    theta = np.float32(500000.0)

    def tables(rot):
        inv = (theta ** (-np.arange(0, rot, 2, dtype=np.float32) / np.float32(rot))).astype(np.float32)
        ang = (np.arange(S, dtype=np.float32)[:, None] * inv[None, :]).astype(np.float32)
        return np.cos(ang).astype(np.float32), np.sin(ang).astype(np.float32)

    def rope_consts(dh, start, rot):
        cs, sn = tables(rot)
        half = rot // 2
        C = np.ones((dh, S), np.float32)
        Sn = np.zeros((dh, S), np.float32)
        Pm = np.zeros((dh, dh), np.float32)
        for i in range(half):
            C[start + i] = cs[:, i]
            C[start + half + i] = cs[:, i]
            Sn[start + i] = sn[:, i]
            Sn[start + half + i] = sn[:, i]
            Pm[start + i, start + half + i] = -1.0
            Pm[start + half + i, start + i] = 1.0
        return C, Sn, np.ascontiguousarray(Pm.T)

    c["c_swa_cos"], c["c_swa_sin"], c["c_swa_rot"] = rope_consts(64, 0, 16)
    c["c_mla_cos"], c["c_mla_sin"], r96 = rope_consts(96, 64, 32)
    rp = np.zeros((128, 128), np.float32)
    rp[:96, :96] = r96
    c["c_mla_rot"] = rp
    o96 = np.zeros((128, 128), np.float32)
    o96[:96, :96] = 1.0
    c["c_ones96"] = o96
    b = np.arange(128)[:, None]
    a = np.arange(128)[None, :]
    c["c_mlo"] = (b <= a).astype(np.float32)
    c["c_mhi"] = (a <= b).astype(np.float32)
    p = np.arange(128)
    c["c_bd64"] = (p[:, None] // 64 == p[None, :] // 64).astype(np.float32)
    same = (p[:, None] // 64 == p[None, :] // 64)
    dec = np.float32(-0.6065306597126334)
    c["c_tri_f"] = np.concatenate([(same & (p[:, None] <= p[None, :])), (same & (p[:, None] < p[None, :]))], axis=1).astype(np.float32) * dec
    c["c_tri_b"] = np.concatenate([(same & (p[:, None] >= p[None, :])), (same & (p[:, None] > p[None, :]))], axis=1).astype(np.float32) * dec
    s_ = (p % 64)[:, None]
    t_ = np.arange(64)[None, :]
    c["c_mask_f"] = np.tile(np.concatenate([s_ < t_, s_ <= t_], axis=1), (1, 4)).astype(np.float32)
    c["c_mask_b"] = np.tile(np.concatenate([s_ > t_, s_ >= t_], axis=1), (1, 4)).astype(np.float32)
    c["c_lmask_f"] = np.tile(t_ < s_, (1, 8)).astype(np.float32)
    c["c_lmask_b"] = np.tile(t_ > s_, (1, 8)).astype(np.float32)
    c["c_ist"] = (s_ == t_).astype(np.float32)
    ids = np.zeros((32, 96), np.float32)
    ids[np.arange(32), 64 + np.arange(32)] = 1.0
    c["c_mla_ids"] = ids
    return c


CONST_SHAPES = {k: v.shape for k, v in make_consts().items()}

FULL_STAGES = [("swa", 0), ("ffn", 0), ("rwkv", 0), ("ffn", 1), ("mla", 0), ("ffn", 2), ("swa", 1), ("ffn", 3)]


def run(inputs, stages, ncores=8, debug=False):
    nc = build(stages, debug)
    consts = make_consts()
    x = np.ascontiguousarray(inputs["x"], dtype=np.float32)
    in_maps = []
    for b in range(ncores):
        m = {"x": x[b]}
        for name in PARAM_SHAPES:
            m[name] = np.ascontiguousarray(inputs[name], dtype=np.float32)
        m.update(consts)
        in_maps.append(m)
    res = run_bass_kernel_spmd(nc, in_maps, core_ids=list(range(ncores)))
    return np.stack([r["y"] for r in res.results], axis=0)


def kernel(**inputs):
    return run(inputs, FULL_STAGES, 8).astype(np.float32)
```
